# Optimizing a Trainium2 kernel written in Bass

```python
import math
import jax, jax.numpy as jnp
from jax import lax
import numpy as np

D_MODEL = 1024
BATCH = 8
SEQ = 2048
DEPTH = 1
DEC_BATCH = 128
DEC_SEQ = 4
PAST_LEN = 16384
PAGE_SIZE = 128

N_MEM = 256
XA_HEADS = 4
XA_HEAD_DIM = D_MODEL // XA_HEADS
SSD_WIDTH = D_MODEL // 2
SSD_HEAD_DIM = 64
SSD_HEADS = SSD_WIDTH // SSD_HEAD_DIM
SSD_GROUPS = 2
SSD_STATE = 128
SSD_CONV = 4
SSD_CHUNK = 128
SSD_XBC = SSD_WIDTH + 2 * SSD_GROUPS * SSD_STATE
SSD_PROJ = SSD_WIDTH + SSD_XBC + SSD_HEADS
RWKV_WIDTH = D_MODEL - SSD_WIDTH
RWKV_HEAD_DIM = 64
RWKV_HEADS = RWKV_WIDTH // RWKV_HEAD_DIM
DECAY_LORA = 64
ICLR_LORA = 64
GATE_LORA = 128
RWKV_PROJ = 3 * RWKV_WIDTH + DECAY_LORA + ICLR_LORA + GATE_LORA
IN_PROJ = SSD_PROJ + RWKV_PROJ
D_FF = 2816
FFN_CONV = 3
EPS = 1e-6
GN_EPS = 64e-5

kernel_name = "hymba_ssd_rwkv7_memxattn_convffn_step"


def _split_last(x, sizes):
    idx = np.cumsum(sizes)[:-1].tolist()
    return jnp.split(x, idx, axis=-1)


def rmsnorm(x, g):
    xf = x.astype(jnp.float32)
    y = xf * lax.rsqrt(jnp.mean(xf * xf, axis=-1, keepdims=True) + EPS)
    return (y * g.astype(jnp.float32)).astype(x.dtype)


def causal_dwconv(x, buf, w, b):
    width = w.shape[0]
    l = x.shape[1]
    xp = jnp.concatenate([buf.astype(x.dtype), x], axis=1)
    y = b + sum(xp[:, j:j + l] * w[j] for j in range(width))
    return y, xp[:, xp.shape[1] - (width - 1):]


def ssd_chunked(x, dt, a, bm, cm, h0):
    b, l = x.shape[:2]
    q = SSD_CHUNK if l % SSD_CHUNK == 0 else l
    c = l // q
    g, e, p, n = SSD_GROUPS, SSD_HEADS // SSD_GROUPS, SSD_HEAD_DIM, SSD_STATE
    xdt = (x * dt[..., None]).reshape(b, c, q, g, e, p)
    a_cs = jnp.cumsum((dt * a).reshape(b, c, q, g, e), axis=2)
    bc = bm.reshape(b, c, q, g, n)
    cc = cm.reshape(b, c, q, g, n)
    a_t = jnp.moveaxis(a_cs, 2, -1)
    causal = jnp.tril(jnp.ones((q, q), dtype=bool))
    seg = jnp.exp(jnp.where(causal, a_t[..., :, None] - a_t[..., None, :], -jnp.inf))
    cb = jnp.einsum("bcqgn,bcsgn->bcgqs", cc, bc)
    y_diag = jnp.einsum("bcgeqs,bcsgep->bcqgep", cb[:, :, :, None] * seg, xdt)
    decay_end = jnp.exp(a_cs[:, :, -1:] - a_cs)
    chunk_states = jnp.einsum("bcqgn,bcqgep->bcgepn", bc, xdt * decay_end[..., None])
    chunk_decay = jnp.exp(a_cs[:, :, -1])

    def step(h, inp):
        s_c, d_c = inp
        return h * d_c[..., None, None] + s_c, h

    h_last, h_prev = lax.scan(step, h0.reshape(b, g, e, p, n),
                              (jnp.moveaxis(chunk_states, 1, 0), jnp.moveaxis(chunk_decay, 1, 0)))
    h_prev = jnp.moveaxis(h_prev, 0, 1)
    y_off = jnp.einsum("bcqgn,bcgepn->bcqgep", cc, h_prev) * jnp.exp(a_cs)[..., None]
    y = (y_diag + y_off).reshape(b, l, SSD_HEADS, p)
    return y, h_last.reshape(b, SSD_HEADS, p, n)


def ssd_branch(u, conv_buf, h0, conv_w, conv_b, dt_bias, a_log, d_skip, norm_w):
    b, l, _ = u.shape
    f32 = jnp.float32
    z, xbc, dt_raw = _split_last(u, [SSD_WIDTH, SSD_XBC, SSD_HEADS])
    xbc, new_buf = causal_dwconv(xbc, conv_buf, conv_w, conv_b)
    xbc = jax.nn.silu(xbc)
    xs, bm, cm = _split_last(xbc, [SSD_WIDTH, SSD_GROUPS * SSD_STATE, SSD_GROUPS * SSD_STATE])
    dt = jax.nn.softplus((dt_raw + dt_bias).astype(f32))
    a = -jnp.exp(a_log.astype(f32))
    xh = xs.astype(f32).reshape(b, l, SSD_HEADS, SSD_HEAD_DIM)
    y, h = ssd_chunked(xh, dt, a,
                       bm.astype(f32).reshape(b, l, SSD_GROUPS, SSD_STATE),
                       cm.astype(f32).reshape(b, l, SSD_GROUPS, SSD_STATE),
                       h0.astype(f32))
    y = y + d_skip.astype(f32)[:, None] * xh
    y = y.reshape(b, l, SSD_WIDTH) * jax.nn.silu(z.astype(f32))
    yg = y.reshape(b, l, SSD_GROUPS, SSD_WIDTH // SSD_GROUPS)
    yg = yg * lax.rsqrt(jnp.mean(yg * yg, axis=-1, keepdims=True) + EPS)
    y = yg.reshape(b, l, SSD_WIDTH) * norm_w.astype(f32)
    return y.astype(u.dtype), new_buf, h.astype(h0.dtype)


def wkv7_scan(r, w, k, v, kk, a, s0):
    def step(s, inp):
        r_t, w_t, k_t, v_t, kk_t, a_t = inp
        sa = jnp.einsum("bhij,bhj->bhi", s, -kk_t)
        s = (s * w_t[:, :, None, :] + sa[..., None] * (kk_t * a_t)[:, :, None, :]
             + v_t[..., None] * k_t[:, :, None, :])
        return s, jnp.einsum("bhij,bhj->bhi", s, r_t)

    xs = tuple(jnp.moveaxis(t, 1, 0) for t in (r, w, k, v, kk, a))
    s_last, ys = lax.scan(step, s0, xs)
    return jnp.moveaxis(ys, 0, 1), s_last


def rwkv_branch(u, shift_buf, s0, mu, w0, w2, a0, a2, g2, k_k, k_a, r_k, ln_w, ln_b):
    b, l, _ = u.shape
    f32 = jnp.float32
    hs = (b, l, RWKV_HEADS, RWKV_HEAD_DIM)
    hd = (RWKV_HEADS, RWKV_HEAD_DIM)
    u_prev = jnp.concatenate([shift_buf[:, None, :].astype(u.dtype), u[:, :-1]], axis=1)
    um = u + (u_prev - u) * mu
    r, k, v, lw, la, lg = _split_last(um, [RWKV_WIDTH] * 3 + [DECAY_LORA, ICLR_LORA, GATE_LORA])
    w_log = -jax.nn.softplus(-(w0 + jnp.tanh(lw) @ w2).astype(f32)) - 0.5
    decay = jnp.exp(-jnp.exp(w_log)).reshape(hs)
    a = jax.nn.sigmoid((a0 + la @ a2).astype(f32)).reshape(hs)
    g = jax.nn.sigmoid(lg) @ g2
    r = r.astype(f32).reshape(hs)
    k = k.astype(f32).reshape(hs)
    v = v.astype(f32).reshape(hs)
    kk = k * k_k.astype(f32).reshape(hd)
    kk = kk / jnp.maximum(jnp.sqrt(jnp.sum(kk * kk, axis=-1, keepdims=True)), 1e-12)
    k = k * (1.0 + (a - 1.0) * k_a.astype(f32).reshape(hd))
    y, s_new = wkv7_scan(r, decay, k, v, kk, a, s0.astype(f32))
    mean = jnp.mean(y, axis=-1, keepdims=True)
    var = jnp.mean(jnp.square(y - mean), axis=-1, keepdims=True)
    y = (y - mean) * lax.rsqrt(var + GN_EPS) * ln_w.astype(f32).reshape(hd) + ln_b.astype(f32).reshape(hd)
    y = y + jnp.sum(r * k * r_k.astype(f32), axis=-1, keepdims=True) * v
    y = y.reshape(b, l, RWKV_WIDTH).astype(u.dtype) * g
    return y, u[:, -1], s_new.astype(s0.dtype)


def mem_kv(mem, g, w_k, w_v):
    b = mem.shape[0]
    m = rmsnorm(mem, g)
    k = (m @ w_k).reshape(b, mem.shape[1], XA_HEADS, XA_HEAD_DIM)
    v = (m @ w_v).reshape(b, mem.shape[1], XA_HEADS, XA_HEAD_DIM)
    return k, v


def cross_attn(h, mk, mv, w_q, w_o):
    b, l, _ = h.shape
    q = (h @ w_q).reshape(b, l, XA_HEADS, XA_HEAD_DIM)
    s = jnp.einsum("blhd,bmhd->bhlm", q, mk.astype(q.dtype)).astype(jnp.float32) * (XA_HEAD_DIM ** -0.5)
    pr = jax.nn.softmax(s, axis=-1).astype(h.dtype)
    o = jnp.einsum("bhlm,bmhd->blhd", pr, mv.astype(h.dtype)).reshape(b, l, D_MODEL)
    return o @ w_o


def conv_ffn(h, buf, w_up, conv_w, conv_b, w_down):
    up = h @ w_up
    up, new_buf = causal_dwconv(up, buf, conv_w, conv_b)
    gate, val = jnp.split(up, 2, axis=-1)
    return (jax.nn.silu(gate) * val) @ w_down, new_buf


def _layer(x, mem_k, mem_v, ssm_conv, ssm, shift, wkv, ffn_conv, p):
    h = rmsnorm(x, p["norm_mix_w"])
    u = h @ p["w_in"]
    y_ssd, ssm_conv_n, ssm_n = ssd_branch(u[..., :SSD_PROJ], ssm_conv, ssm, p["ssd_conv_w"], p["ssd_conv_b"],
                                          p["ssd_dt_bias"], p["ssd_a_log"], p["ssd_d"], p["ssd_norm_w"])
    y_rw, shift_n, wkv_n = rwkv_branch(u[..., SSD_PROJ:], shift, wkv, p["rwkv_mu"], p["rwkv_w0"], p["rwkv_w2"],
                                       p["rwkv_a0"], p["rwkv_a2"], p["rwkv_g2"], p["rwkv_k_k"], p["rwkv_k_a"],
                                       p["rwkv_r_k"], p["rwkv_ln_w"], p["rwkv_ln_b"])
    x = x + jnp.concatenate([y_ssd, y_rw], axis=-1) @ p["w_out"]
    x = x + cross_attn(rmsnorm(x, p["norm_xa_w"]), mem_k, mem_v, p["xa_w_q"], p["xa_w_o"])
    f, ffn_conv_n = conv_ffn(rmsnorm(x, p["norm_ffn_w"]), ffn_conv, p["ffn_w_up"], p["ffn_conv_w"],
                             p["ffn_conv_b"], p["ffn_w_down"])
    return x + f, ssm_conv_n, ssm_n, shift_n, wkv_n, ffn_conv_n


def setup_inputs(seed: int = 0) -> dict:
    key = jax.random.key(seed)
    ks = iter(jax.random.split(key, 64))
    f32 = jnp.float32

    def nrm(shape, scale):
        return jax.random.normal(next(ks), shape, f32) * scale

    def gain(shape):
        return 1.0 + nrm(shape, 0.02)

    L = DEPTH
    dt0 = jnp.exp(jax.random.uniform(next(ks), (L, SSD_HEADS), f32, math.log(1e-3), math.log(1e-1)))
    return {
        "x_prompt": nrm((BATCH, SEQ, D_MODEL), 1.0),
        "x_sample": nrm((DEC_BATCH, DEC_SEQ, D_MODEL), 1.0),
        "mem_prompt": nrm((BATCH, N_MEM, D_MODEL), 1.0),
        "state_ssm_conv": nrm((L, DEC_BATCH, SSD_CONV - 1, SSD_XBC), 1.0),
        "state_ssm": nrm((L, DEC_BATCH, SSD_HEADS, SSD_HEAD_DIM, SSD_STATE), 0.1),
        "state_shift": nrm((L, DEC_BATCH, RWKV_PROJ), 1.0),
        "state_wkv": nrm((L, DEC_BATCH, RWKV_HEADS, RWKV_HEAD_DIM, RWKV_HEAD_DIM), 0.1),
        "state_ffn_conv": nrm((L, DEC_BATCH, FFN_CONV - 1, 2 * D_FF), 1.0),
        "cache_mem_k": nrm((L, DEC_BATCH, N_MEM, XA_HEADS, XA_HEAD_DIM), 1.0),
        "cache_mem_v": nrm((L, DEC_BATCH, N_MEM, XA_HEADS, XA_HEAD_DIM), 1.0),
        "norm_mix_w": gain((L, D_MODEL)),
        "w_in": nrm((L, D_MODEL, IN_PROJ), D_MODEL ** -0.5),
        "ssd_conv_w": nrm((L, SSD_CONV, SSD_XBC), SSD_CONV ** -0.5),
        "ssd_conv_b": nrm((L, SSD_XBC), 0.01),
        "ssd_dt_bias": dt0 + jnp.log(-jnp.expm1(-dt0)),
        "ssd_a_log": jnp.log(jax.random.uniform(next(ks), (L, SSD_HEADS), f32, 1.0, 16.0)),
        "ssd_d": gain((L, SSD_HEADS)),
        "ssd_norm_w": gain((L, SSD_WIDTH)),
        "rwkv_mu": jax.random.uniform(next(ks), (L, RWKV_PROJ), f32, 0.0, 1.0),
        "rwkv_w0": jax.random.uniform(next(ks), (L, RWKV_WIDTH), f32, -6.0, 1.0),
        "rwkv_w2": nrm((L, DECAY_LORA, RWKV_WIDTH), 0.1 * DECAY_LORA ** -0.5),
        "rwkv_a0": nrm((L, RWKV_WIDTH), 0.1),
        "rwkv_a2": nrm((L, ICLR_LORA, RWKV_WIDTH), 0.5 * ICLR_LORA ** -0.5),
        "rwkv_g2": nrm((L, GATE_LORA, RWKV_WIDTH), GATE_LORA ** -0.5),
        "rwkv_k_k": 0.85 + nrm((L, RWKV_WIDTH), 0.02),
        "rwkv_k_a": gain((L, RWKV_WIDTH)),
        "rwkv_r_k": nrm((L, RWKV_HEADS, RWKV_HEAD_DIM), 0.1),
        "rwkv_ln_w": gain((L, RWKV_WIDTH)),
        "rwkv_ln_b": nrm((L, RWKV_WIDTH), 0.01),
        "w_out": nrm((L, D_MODEL, D_MODEL), D_MODEL ** -0.5),
        "norm_xa_w": gain((L, D_MODEL)),
        "mem_norm_w": gain((L, D_MODEL)),
        "xa_w_q": nrm((L, D_MODEL, D_MODEL), D_MODEL ** -0.5),
        "xa_w_k": nrm((L, D_MODEL, D_MODEL), D_MODEL ** -0.5),
        "xa_w_v": nrm((L, D_MODEL, D_MODEL), D_MODEL ** -0.5),
        "xa_w_o": nrm((L, D_MODEL, D_MODEL), D_MODEL ** -0.5),
        "norm_ffn_w": gain((L, D_MODEL)),
        "ffn_w_up": nrm((L, D_MODEL, 2 * D_FF), D_MODEL ** -0.5),
        "ffn_conv_w": nrm((L, FFN_CONV, 2 * D_FF), FFN_CONV ** -0.5),
        "ffn_conv_b": nrm((L, 2 * D_FF), 0.01),
        "ffn_w_down": nrm((L, D_FF, D_MODEL), D_FF ** -0.5),
        "final_norm_w": gain((D_MODEL,)),
    }


def reference(x_prompt, x_sample, mem_prompt, state_ssm_conv, state_ssm, state_shift, state_wkv,
              state_ffn_conv, cache_mem_k, cache_mem_v, norm_mix_w, w_in, ssd_conv_w, ssd_conv_b,
              ssd_dt_bias, ssd_a_log, ssd_d, ssd_norm_w, rwkv_mu, rwkv_w0, rwkv_w2, rwkv_a0, rwkv_a2,
              rwkv_g2, rwkv_k_k, rwkv_k_a, rwkv_r_k, rwkv_ln_w, rwkv_ln_b, w_out, norm_xa_w, mem_norm_w,
              xa_w_q, xa_w_k, xa_w_v, xa_w_o, norm_ffn_w, ffn_w_up, ffn_conv_w, ffn_conv_b, ffn_w_down,
              final_norm_w):
    bp = x_prompt.shape[0]
    dtp = x_prompt.dtype
    xp, xs = x_prompt, x_sample
    ssm_conv_p, ssm_conv_s, ssm_p, ssm_s = [], [], [], []
    shift_p, shift_s, wkv_p, wkv_s = [], [], [], []
    ffn_conv_p, ffn_conv_s, mem_k_p, mem_v_p = [], [], [], []
    for i in range(DEPTH):
        p = dict(norm_mix_w=norm_mix_w[i], w_in=w_in[i], ssd_conv_w=ssd_conv_w[i], ssd_conv_b=ssd_conv_b[i],
                 ssd_dt_bias=ssd_dt_bias[i], ssd_a_log=ssd_a_log[i], ssd_d=ssd_d[i], ssd_norm_w=ssd_norm_w[i],
                 rwkv_mu=rwkv_mu[i], rwkv_w0=rwkv_w0[i], rwkv_w2=rwkv_w2[i], rwkv_a0=rwkv_a0[i],
                 rwkv_a2=rwkv_a2[i], rwkv_g2=rwkv_g2[i], rwkv_k_k=rwkv_k_k[i], rwkv_k_a=rwkv_k_a[i],
                 rwkv_r_k=rwkv_r_k[i], rwkv_ln_w=rwkv_ln_w[i], rwkv_ln_b=rwkv_ln_b[i], w_out=w_out[i],
                 norm_xa_w=norm_xa_w[i], xa_w_q=xa_w_q[i], xa_w_o=xa_w_o[i], norm_ffn_w=norm_ffn_w[i],
                 ffn_w_up=ffn_w_up[i], ffn_conv_w=ffn_conv_w[i], ffn_conv_b=ffn_conv_b[i],
                 ffn_w_down=ffn_w_down[i])
        mk, mv = mem_kv(mem_prompt, mem_norm_w[i], xa_w_k[i], xa_w_v[i])
        xp, c0, s0, h0, w0_, f0 = _layer(
            xp, mk, mv,
            jnp.zeros((bp, SSD_CONV - 1, SSD_XBC), dtp),
            jnp.zeros((bp, SSD_HEADS, SSD_HEAD_DIM, SSD_STATE), dtp),
            jnp.zeros((bp, RWKV_PROJ), dtp),
            jnp.zeros((bp, RWKV_HEADS, RWKV_HEAD_DIM, RWKV_HEAD_DIM), dtp),
            jnp.zeros((bp, FFN_CONV - 1, 2 * D_FF), dtp), p)
        xs, c1, s1, h1, w1_, f1 = _layer(
            xs, cache_mem_k[i], cache_mem_v[i], state_ssm_conv[i], state_ssm[i], state_shift[i],
            state_wkv[i], state_ffn_conv[i], p)
        ssm_conv_p.append(c0); ssm_conv_s.append(c1)
        ssm_p.append(s0); ssm_s.append(s1)
        shift_p.append(h0); shift_s.append(h1)
        wkv_p.append(w0_); wkv_s.append(w1_)
        ffn_conv_p.append(f0); ffn_conv_s.append(f1)
        mem_k_p.append(mk); mem_v_p.append(mv)
    y_prompt = rmsnorm(xp, final_norm_w)
    y_sample = rmsnorm(xs, final_norm_w)
    return (y_prompt, y_sample,
            jnp.stack(ssm_conv_p), jnp.stack(ssm_conv_s),
            jnp.stack(ssm_p), jnp.stack(ssm_s),
            jnp.stack(shift_p), jnp.stack(shift_s),
            jnp.stack(wkv_p), jnp.stack(wkv_s),
            jnp.stack(ffn_conv_p), jnp.stack(ffn_conv_s),
            jnp.stack(mem_k_p), jnp.stack(mem_v_p))
```

```python
import os
import numpy as np
from contextlib import ExitStack
import concourse.bass as bass
import concourse.mybir as mybir
from concourse.bass_utils import run_bass_kernel_spmd

F32 = mybir.dt.float32
BF16 = mybir.dt.bfloat16
AF = mybir.ActivationFunctionType
ALU = mybir.AluOpType
AX = mybir.AxisListType

NCORES = 8
EPS = 1e-6
GN_EPS = 64e-5
C0 = float(np.exp(-0.5))
NDMA = 40

ROWS = [("norm_mix", 1024), ("norm_xa", 1024), ("mem_norm", 1024), ("norm_ffn", 1024), ("final_norm", 1024),
        ("ssd_norm", 512), ("dt_bias", 8), ("a_log", 8), ("ssd_d", 8), ("ln_w", 512), ("ln_b", 512), ("r_k_row", 512)]
COLS = [("ssd_cw", 32), ("ssd_cb", 8), ("mu", 14), ("w0", 4), ("a0", 4), ("k_k", 4), ("k_a", 4), ("r_k", 4),
        ("ffn_cw", 132), ("ffn_cb", 44)]
CONSTS = [("ident", 128), ("tri", 128), ("lm", 128), ("ones", 128)]
CONSTS2 = [("wmask", 640), ("i64", 64), ("bones", 128), ("scanm", 512)]


def _offs(spec):
    o = {}
    p = 0
    for n, s in spec:
        o[n] = (p, s)
        p += s
    return o, p


RO, NROW = _offs(ROWS)
CO, NCOL = _offs(COLS)
KO, NCONST = _offs(CONSTS)
KO2, NCONST2 = _offs(CONSTS2)


class Buf:
    __slots__ = ("name", "w", "r", "psum")

    def __init__(self, name="", psum=False):
        self.name = name
        self.w = None
        self.r = {}
        self.psum = psum


class TT:
    def __init__(self, t, name):
        self.t = t
        self.b = Buf(name)

    def __getitem__(self, idx):
        return self.t[idx]


def _b(x):
    return x if isinstance(x, Buf) else x.b


class View:
    def __init__(self, ap, buf):
        self.ap = ap
        self.b = buf

    def __getitem__(self, idx):
        return self.ap[idx]


class K:
    def __init__(self, nc):
        self.nc = nc
        self.engs = ["pe", "act", "dve", "pool", "sp"]
        self.sem = {e: nc.alloc_semaphore(name=f"s_{e}") for e in self.engs}
        self.cnt = {e: 0 for e in self.engs}
        self.waited = {e: {} for e in self.engs}
        self.prog = {e: [] for e in self.engs}
        self.dsem = [nc.alloc_semaphore(name=f"d{i}") for i in range(NDMA)]
        self.dval = [0] * NDMA
        self.drr = 0
        self.uid = 0
        self.seq = {e: 0 for e in self.LAZY}
        self.marks = {e: [] for e in self.LAZY}
        self.entry = {e: {} for e in self.LAZY}

    def _semh(self, key):
        return self.sem[key] if isinstance(key, str) else self.dsem[key]

    def _deps(self, e, reads, writes):
        need = {}

        def add(ev):
            if ev is None:
                return
            kk, v = ev
            if need.get(kk, 0) < v:
                need[kk] = v

        for b in reads:
            b = _b(b)
            add(b.w)
            if b.psum:
                for kk, v in b.r.items():
                    if kk != e:
                        add((kk, v))
        for b in writes:
            b = _b(b)
            add(b.w)
            for kk, v in b.r.items():
                add((kk, v))
        waits = []
        wd = self.waited[e]
        for kk, v in need.items():
            if kk == "pe" and e == "pe":
                continue
            if kk in self.LAZY:
                v = self.resolve(kk, v)
            if wd.get(kk, 0) < v:
                wd[kk] = v
                waits.append((self._semh(kk), v))
        return waits

    def _mark(self, ev, reads, writes):
        kk, v = ev
        for b in reads:
            b = _b(b)
            if b.r.get(kk, 0) < v:
                b.r[kk] = v
        for b in writes:
            b = _b(b)
            b.w = ev
            b.r = {}

    LAZY = ("pe",)

    def resolve(self, e, seq):
        import bisect
        marks = self.marks[e]
        i = bisect.bisect_left(marks, seq)
        if i < len(marks):
            return i + 1
        idx = self.entry[e][seq]
        w, fn, sem, inc = self.prog[e][idx]
        assert inc == 0 and fn is not None
        self.prog[e][idx] = (w, fn, sem, 1)
        marks.append(seq)
        self.cnt[e] = len(marks)
        return len(marks)

    def op(self, e, fn, reads=(), writes=()):
        waits = self._deps(e, reads, writes)
        if e in self.LAZY:
            self.seq[e] += 1
            ev = (e, self.seq[e])
            self.entry[e][self.seq[e]] = len(self.prog[e])
            self.prog[e].append((waits, fn, self.sem[e], 0))
        else:
            self.cnt[e] += 1
            ev = (e, self.cnt[e])
            self.prog[e].append((waits, fn, self.sem[e], 1))
        self._mark(ev, reads, writes)
        return ev

    def dma(self, q, out, in_, reads=(), writes=(), **kw):
        waits = self._deps(q, reads, writes)
        i = self.drr
        self.drr = (self.drr + 1) % NDMA
        if self.dval[i] > 0:
            wd = self.waited[q]
            if wd.get(i, 0) < self.dval[i]:
                wd[i] = self.dval[i]
                waits.append((self.dsem[i], self.dval[i]))
        self.dval[i] += 16
        ev = (i, self.dval[i])
        self.prog[q].append((waits, lambda eng: eng.dma_start(out=out, in_=in_, **kw), self.dsem[i], 16))
        self._mark(ev, reads, writes)
        return ev

    def barrier(self):
        for e in self.LAZY:
            if self.seq[e] > 0:
                self.resolve(e, self.seq[e])
        for e in self.engs:
            waits = []
            wd = self.waited[e]
            for e2 in self.engs:
                if self.cnt[e2] > 0 and wd.get(e2, 0) < self.cnt[e2] and e2 != e:
                    wd[e2] = self.cnt[e2]
                    waits.append((self.sem[e2], self.cnt[e2]))
            for i in range(NDMA):
                if self.dval[i] > 0 and wd.get(i, 0) < self.dval[i]:
                    wd[i] = self.dval[i]
                    waits.append((self.dsem[i], self.dval[i]))
            if waits:
                self.prog[e].append((waits, None, None, 0))

    def finish(self):
        self.barrier()
        print("instr counts", {e: (len(self.prog[e]), sum(len(w) for w, _, _, _ in self.prog[e])) for e in self.engs})
        nc = self.nc
        prog = self.prog

        def replay(name, eng):
            for waits, fn, sem, inc in prog[name]:
                for s, v in waits:
                    eng.wait_ge(s, v)
                if fn is not None:
                    ins = fn(eng)
                    if inc:
                        ins.then_inc(sem, inc)

        with nc.Block() as block:
            @block.tensor
            def _(eng):
                replay("pe", eng)

            @block.scalar
            def _(eng):
                replay("act", eng)

            @block.vector
            def _(eng):
                replay("dve", eng)

            @block.gpsimd
            def _(eng):
                replay("pool", eng)

            @block.sync
            def _(eng):
                replay("sp", eng)

    def sb(self, name, shape, dtype, es=None):
        self.uid += 1
        nm = f"{name}_{self.uid}"
        if es is None:
            t = self.nc.alloc_sbuf_tensor(nm, list(shape), dtype)
        else:
            t = es.enter_context(self.nc.sbuf_tensor(nm, list(shape), dtype))
        return TT(t, nm)

    def mm(self, out, lhsT, rhs, start, stop, reads, writes):
        return self.op("pe", lambda e: e.matmul(out, lhsT, rhs, start=start, stop=stop), reads, writes)

    def tr(self, out, in_, ident, reads, writes):
        return self.op("pe", lambda e: e.transpose(out, in_, ident), reads, writes)

    def act(self, out, in_, func, reads, writes, scale=1.0, bias=0.0, accum_out=None):
        if accum_out is None:
            return self.op("act", lambda e: e.activation(out, in_, func, bias=bias, scale=scale), reads, writes)
        return self.op("act", lambda e: e.activation(out, in_, func, bias=bias, scale=scale, accum_out=accum_out),
                       reads, writes)

    def tt(self, out, in0, in1, op, reads, writes, eng="dve"):
        return self.op(eng, lambda e: e.tensor_tensor(out, in0, in1, op), reads, writes)

    def ts(self, out, in0, s1, s2, op0, op1, reads, writes, eng="dve"):
        if op1 is None:
            return self.op(eng, lambda e: e.tensor_scalar(out, in0, s1, None, op0), reads, writes)
        return self.op(eng, lambda e: e.tensor_scalar(out, in0, s1, s2, op0, op1), reads, writes)

    def stt(self, out, in0, scalar, in1, op0, op1, reads, writes):
        return self.op("dve", lambda e: e.scalar_tensor_tensor(out, in0, scalar, in1, op0, op1), reads, writes)

    def cp(self, out, in_, reads, writes, eng="dve"):
        return self.op(eng, lambda e: e.tensor_copy(out, in_), reads, writes)

    def recip(self, out, in_, reads, writes):
        return self.op("dve", lambda e: e.reciprocal(out, in_), reads, writes)

    def red(self, out, in_, op, reads, writes, negate=False):
        return self.op("dve", lambda e: e.tensor_reduce(out, in_, AX.X, op, negate=negate), reads, writes)

    def carve(self, parent, pieces):
        out = []
        off = 0
        for nbytes, dt_ in pieces:
            ap = parent.t[:, off // 4:(off + nbytes) // 4]
            if dt_ == BF16:
                ap = ap.bitcast(BF16)
            b = Buf(parent.b.name + "_v")
            b.w = parent.b.w
            b.r = dict(parent.b.r)
            out.append(View(ap, b))
            off += nbytes
        assert off <= 2048
        return out

    def merge(self, parent, views):
        pr = parent.b.r
        for v in views:
            evs = list(v.b.r.items())
            if v.b.w is not None:
                evs.append(v.b.w)
            for kk, val in evs:
                if pr.get(kk, 0) < val:
                    pr[kk] = val

    def scan(self, out, d0, d1, init, op0, op1, reads, writes):
        return self.op("dve", lambda e: e.tensor_tensor_scan(out, d0, d1, init, op0, op1), reads, writes)

    def memset(self, ap, val, writes, eng="dve"):
        return self.op(eng, lambda e: e.memset(ap, val), (), writes)


def bc(ap, axis, n):
    a = ap.unsqueeze(axis)
    shp = list(a.shape)
    shp[axis] = n
    return a.broadcast_to(shp)


def build(stop="all", dbg=False):
    nc = bass.Bass("TRN2", target_bir_lowering=False)
    k = K(nc)
    DBGU = int(os.environ.get("DBGU", "9"))
    skipBC = stop.startswith("x")
    stop = stop.lstrip("x")
    NTB = 0 if skipBC else 4
    mult, add, sub = ALU.mult, ALU.add, ALU.subtract

    def din(name, shape):
        return nc.dram_tensor(name, list(shape), F32, kind="ExternalInput").ap()

    def dout(name, shape):
        return nc.dram_tensor(name, list(shape), F32, kind="ExternalOutput").ap()

    xp_d = din("xp", [2048, 1024])
    xs_d = din("xs", [64, 1024])
    mem_d = din("mem", [256, 1024])
    st_conv_d = din("st_conv", [128, 8, 3, 16])
    st_ssm_d = din("st_ssm", [128, 8192])
    st_shift_d = din("st_shift", [128, 14, 16])
    st_wkv_d = din("st_wkv", [128, 4096])
    st_ffn_d = din("st_ffn", [128, 44, 2, 16])
    ck_d = din("ck", [16, 256, 1024])
    cv_d = din("cv", [16, 256, 1024])
    w_in_d = din("w_in", [1024, 3336])
    w_out_d = din("w_out", [1024, 1024])
    wq_d = din("wq", [1024, 1024])
    wk_d = din("wk", [1024, 1024])
    wv_d = din("wv", [1024, 1024])
    wo_d = din("wo", [1024, 1024])
    wup_d = din("wup", [1024, 5632])
    wdn_d = din("wdn", [2816, 1024])
    w2a2_d = din("w2a2", [128, 512])
    g2_d = din("g2", [128, 512])
    rows_d = din("rows", [1, NROW])
    cols_d = din("cols", [128, NCOL])
    consts_d = din("consts", [128, NCONST])
    consts2_d = din("consts2", [128, NCONST2])

    y_p_d = dout("y_p", [2048, 1024])
    y_s_d = dout("y_s", [64, 1024])
    o_conv_p = dout("o_conv_p", [128, 8, 3])
    o_conv_s = dout("o_conv_s", [128, 8, 3, 16])
    o_ssm_p = dout("o_ssm_p", [512, 128])
    o_ssm_s = dout("o_ssm_s", [128, 8192])
    o_shift_p = dout("o_shift_p", [128, 14])
    o_shift_s = dout("o_shift_s", [128, 14, 16])
    o_wkv_p = dout("o_wkv_p", [128, 4, 64])
    o_wkv_s = dout("o_wkv_s", [128, 4096])
    o_ffn_p = dout("o_ffn_p", [128, 44, 2])
    o_ffn_s = dout("o_ffn_s", [128, 44, 2, 16])
    o_mk = dout("o_mk", [256, 1024])
    o_mv = dout("o_mv", [256, 1024])
    dbg_d = {}
    if dbg:
        dbg_d["yssd"] = dout("dbg_yssd", [2112, 512])
        dbg_d["yrwT"] = dout("dbg_yrwT", [128, 4, 2112])
        dbg_d["x1"] = dout("dbg_x1", [2112, 1024])
        dbg_d["x2"] = dout("dbg_x2", [2112, 1024])
        dbg_d["yrws"] = dout("dbg_yrws", [64, 512])

    x_tok = k.sb("x_tok", [128, 17, 1024], F32)
    xb = [Buf(f"x{b}") for b in range(17)]
    hT = k.sb("hT", [128, 8, 2112], BF16)
    hTb = [Buf(f"hT{b}") for b in range(17)]
    consts = k.sb("consts", [128, NCONST], F32)
    cols = k.sb("cols", [128, NCOL], F32)
    identb = k.sb("identb", [128, 128], BF16)
    gbc = k.sb("gbc", [128, 1024], F32)
    ssx = k.sb("ssx", [128, 17, 4], F32)
    ssb = [Buf(f"ss{b}") for b in range(17)]

    def cst(name, rows=slice(0, 128)):
        o, s = KO[name]
        return consts[rows, o:o + s]

    def col(name, j0=0, j1=None):
        o, s = CO[name]
        if j1 is None:
            j1 = j0 + 1
        return cols[:, o + j0:o + j1]

    def row_bc(name, tt_, nrows=128):
        o, s = RO[name]
        src = rows_d[0:1, o:o + s].partition_broadcast(nrows)
        k.dma("sp", tt_.t[0:nrows, 0:s].unsqueeze(1), src, writes=[tt_])

    psA = [nc.alloc_psum_tensor(f"ps{i}", [128, 2, 512], F32) for i in range(4)]
    pb = [Buf(f"bank{i}", psum=True) for i in range(8)]

    def bank(i):
        return psA[i // 2][:, i % 2, :]

    def bank_bf(i):
        return psA[i // 2][:, i % 2, :].bitcast(BF16)

    k.dma("sp", consts[:, :], consts_d, writes=[consts])
    k.dma("sp", cols[:, :], cols_d, writes=[cols])
    k.cp(identb[:, :], cst("ident"), [consts], [identb])
    ident = cst("ident")

    def blk_rows(blk):
        return 128 if blk < 16 else 64

    def load_weight(name, src_view, ncols, pieces, es_, nk=8):
        t_ = k.sb(name, [128, nk, ncols], BF16, es_)
        bl = []
        for (c0_, c1_) in pieces:
            b_ = Buf(f"{name}_{c0_}")
            k.dma("pool", t_.t[:, :, c0_:c1_], src_view[:, :, c0_:c1_], writes=[b_])
            bl.append((c0_, c1_, b_))

        def rb(c0_, c1_):
            r_ = [b_ for (a0, a1, b_) in bl if a0 < c1_ and c0_ < a1]
            assert r_
            return r_
        return t_, rb

    def norm_T(blk, es_tmp, tmp):
        rows = blk_rows(blk)
        junk, hb = tmp
        xin = x_tok[0:rows, blk, :]
        ss = ssx[0:rows, blk, :]
        k.act(junk[0:rows, :], xin, AF.Square, [xb[blk]], [junk, ssb[blk]], accum_out=ss[:, 0:1])
        k.act(ss[:, 1:2], ss[:, 0:1], AF.Sqrt, [ssb[blk], epsc], [ssb[blk]], scale=1.0 / 1024, bias=epsc[0:rows, 0:1])
        k.recip(ss[:, 2:3], ss[:, 1:2], [ssb[blk]], [ssb[blk]])
        k.stt(hb[0:rows, :], xin, ss[:, 2:3], gbc[0:rows, :], mult, mult, [xb[blk], ssb[blk], gbc], [hb])
        pt = bank_bf(6)[:, :].rearrange("p (c t) -> p c t", c=8)
        for c in range(8):
            k.tr(pt[:, c, 0:rows], hb[0:rows, c * 128:(c + 1) * 128], identb[0:rows, 0:rows], [hb, identb], [pb[6]])
        k.act(hT[:, :, blk * 128:blk * 128 + rows], pt[:, :, 0:rows], AF.Copy, [pb[6]], [hTb[blk]])

    epsc = k.sb("epsc", [128, 2], F32)
    k.memset(epsc[:, 0:1], EPS, [epsc])
    k.memset(epsc[:, 1:2], GN_EPS, [epsc])

    with ExitStack() as es:
        junk = k.sb("junk", [128, 1024], BF16, es)
        hb2 = [k.sb("hb", [128, 1024], BF16, es) for _ in range(2)]
        row_bc("norm_mix", gbc)
        for blk in range(17):
            rows = blk_rows(blk)
            src = xp_d[blk * 128:(blk + 1) * 128, :] if blk < 16 else xs_d
            k.dma("sp", x_tok[0:rows, blk, :], src, writes=[xb[blk]])
        for blk in range(17):
            norm_T(blk, es, (junk, hb2[blk % 2]))
        k.barrier()

    with ExitStack() as es:
        w_in_v = w_in_d.rearrange("(c p) n -> p c n", p=128)
        Wis, WisB = load_weight("Wis", w_in_v[:, :, 0:1544], 1544,
                                [(512 + 128 * c_, 640 + 128 * c_) for c_ in range(8)] + [(0, 512), (1536, 1544)], es)
        Wout = k.sb("Wout", [128, 4, 1024], BF16, es)
        w_out_v = w_out_d.rearrange("(c p) n -> p c n", p=128)
        k.dma("pool", Wout[:, :, :], w_out_v[:, 0:4, :], writes=[Wout])
        sm = k.sb("smallrow", [128, 32], F32, es)
        nwbc = k.sb("nwbc", [128, 512], F32, es)
        for nm, o in (("dt_bias", 0), ("a_log", 8), ("ssd_d", 16)):
            ro, rs = RO[nm]
            k.dma("sp", sm.t[:, o:o + 8].unsqueeze(1), rows_d[0:1, ro:ro + 8].partition_broadcast(128), writes=[sm])
        row_bc("ssd_norm", nwbc)
        k.act(sm[:, 24:32], sm[:, 8:16], AF.Exp, [sm], [sm])
        k.ts(sm[:, 24:32], sm[:, 24:32], -1.0, None, mult, None, [sm], [sm])
        xpre = [k.sb(f"xpre{c}", [128, 515], F32, es) for c in range(8)]
        xact = [k.sb(f"xact{c}", [128, 512], BF16, es) for c in range(8)]
        acc2 = [k.sb("acc", [128, 512], F32, es) for _ in range(2)]
        zs = k.sb("zs", [128, 512], F32, es)
        dtt = k.sb("dtt", [128, 32], F32, es)
        e3 = k.sb("e3", [128, 24], F32, es)
        xdt = k.sb("xdt", [128, 8, 64], BF16, es)
        xdte = k.sb("xdte", [128, 8, 64], BF16, es)
        xsD = k.sb("xsD", [128, 512], F32, es)
        Btok = k.sb("Btok", [128, 256], BF16, es)
        Rm = k.sb("Rm", [128, 8, 128], F32, es)
        seg = Rm
        cbm = k.sb("cbm", [128, 2, 128], F32, es)
        MT = k.sb("MT", [128, 8, 128], BF16, es)
        hst = k.sb("hst", [128, 512], F32, es)
        hbf = k.sb("hbf", [128, 512], BF16, es)
        y1 = k.sb("y1", [128, 512], F32, es)
        yj = k.sb("yj", [128, 512], BF16, es)
        ss2 = k.sb("ss2", [128, 8], F32, es)
        ybf = k.sb("ybf", [128, 512], BF16, es)
        ysT = k.sb("ysT", [128, 4, 512], BF16, es)
        convo = k.sb("convo", [128, 8, 3], F32, es)
        for c in range(8):
            k.memset(xpre[c][:, 0:3], 0.0, [xpre[c]])
        tri, lm, ones = cst("tri"), cst("lm"), cst("ones")

        for t in range(NTB):
            n = 512
            c0 = t * 512
            hbufs = hTb[4 * t:4 * t + 4]
            for c in range(8):
                bi_ = c % 2
                for kk in range(8):
                    k.mm(bank(bi_)[:, 0:n], Wis[:, kk, 512 + 128 * c:512 + 128 * (c + 1)], hT[:, kk, c0:c0 + n],
                         kk == 0, kk == 7, WisB(512 + 128 * c, 640 + 128 * c) + hbufs, [pb[bi_]])
                k.act(xpre[c][:, 3:3 + n], bank(bi_)[:, 0:n], AF.Copy, [pb[bi_]], [xpre[c]])
                acc = acc2[c % 2]
                k.act(acc[:, 0:n], xpre[c][:, 0:n], AF.Identity, [xpre[c], cols], [acc],
                      scale=col("ssd_cw", 4 * c), bias=col("ssd_cb", c))
                for j in range(1, 4):
                    k.stt(acc[:, 0:n], xpre[c][:, j:j + n], col("ssd_cw", 4 * c + j), acc[:, 0:n], mult, add,
                          [xpre[c], cols, acc], [acc])
                k.act(xact[c][:, 0:n], acc[:, 0:n], AF.Silu, [acc], [xact[c]])
                if t == 3:
                    k.cp(convo[:, c, :], xpre[c][:, n:n + 3], [xpre[c]], [convo], eng="pool")
                else:
                    k.cp(xpre[c][:, 0:3], xpre[c][:, n:n + 3], [xpre[c]], [xpre[c]], eng="pool")
            for bi in range(4):
                blk = 4 * t + bi
                cs = slice(bi * 128, bi * 128 + 128)
                hc = slice(blk * 128, blk * 128 + 128)
                for kk in range(8):
                    k.mm(bank(2)[:, :], hT[:, kk, hc], Wis[:, kk, 0:512], kk == 0, kk == 7, WisB(0, 512) + [hTb[blk]], [pb[2]])
                    k.mm(bank(3)[:, 0:8], hT[:, kk, hc], Wis[:, kk, 1536:1544], kk == 0, kk == 7,
                         WisB(1536, 1544) + [hTb[blk]], [pb[3]])
                k.act(zs[:, :], bank(2)[:, :], AF.Silu, [pb[2]], [zs])
                k.tt(dtt[:, 0:8], bank(3)[:, 0:8], sm[:, 0:8], add, [pb[3], sm], [dtt])
                k.ts(dtt[:, 24:32], dtt[:, 0:8], 30.0, None, ALU.min, None, [dtt], [dtt])
                k.act(dtt[:, 24:32], dtt[:, 24:32], AF.Exp, [dtt], [dtt])
                k.act(dtt[:, 8:16], dtt[:, 24:32], AF.Ln, [dtt], [dtt], bias=1.0)
                k.ts(dtt[:, 24:32], dtt[:, 0:8], -30.0, 0.0, add, ALU.max, [dtt], [dtt])
                k.tt(dtt[:, 8:16], dtt[:, 8:16], dtt[:, 24:32], add, [dtt], [dtt])
                k.tt(dtt[:, 16:24], dtt[:, 8:16], sm[:, 24:32], mult, [dtt, sm], [dtt])
                dt_ = dtt[:, 8:16]
                dta = dtt[:, 16:24]
                ptb = bank_bf(6)[:, 0:768].rearrange("p (c t) -> p c t", c=6)
                for c in range(6):
                    k.tr(ptb[:, c, :], xact[c][:, cs], identb[:, :], [xact[c], identb], [pb[6]])
                xs_ps = ptb[:, 0:4, :].rearrange("p c (h d) -> p (c h) d", h=2)
                k.tt(xdt[:, :, :], xs_ps, bc(dt_, 2, 64), mult, [pb[6], dtt], [xdt])
                k.tt(xsD[:, :].rearrange("p (h d) -> p h d", h=8), xs_ps, bc(sm[:, 16:24], 2, 64), mult,
                     [pb[6], sm], [xsD])
                k.act(Btok[:, :], ptb[:, 4:6, :].rearrange("p c t -> p (c t)"), AF.Copy, [pb[6]], [Btok])
                k.tt(Rm[:, :, :], bc(tri, 1, 8), bc(dta, 2, 128), mult, [consts, dtt], [Rm])
                for hh in range(2):
                    k.mm(bank(4 + hh)[:, :], lm, Rm[:, 4 * hh:4 * hh + 4, :].rearrange("p h q -> p (h q)"), True, True,
                         [consts, Rm], [pb[4 + hh]])
                k.mm(bank(3)[:, 8:16], tri, dta, True, True, [consts, dtt], [pb[3]])
                k.mm(bank(3)[:, 16:24], lm, dta, True, True, [consts, dtt], [pb[3]])
                k.mm(bank(3)[:, 24:32], ones, dta, True, True, [consts, dtt], [pb[3]])
                k.act(e3[:, :], bank(3)[:, 8:32], AF.Exp, [pb[3]], [e3])
                k.act(seg[:, :, :].rearrange("p h q -> p (h q)"), psA[2][:, :, :].rearrange("p a b -> p (a b)"),
                      AF.Exp, [pb[4], pb[5]], [seg])
                for g in range(2):
                    k.mm(bank(3)[:, 64 + 128 * g:64 + 128 * (g + 1)], xact[4 + g][:, cs], xact[6 + g][:, cs], True,
                         True, [xact[4 + g], xact[6 + g]], [pb[3]])
                k.tt(cbm[:, :, :], bank(3)[:, 64:320].rearrange("p (g q) -> p g q", g=2), bc(tri, 1, 2), mult,
                     [pb[3], consts], [cbm])
                k.tt(MT[:, :, :].rearrange("p (g e) q -> p g e q", g=2),
                     seg[:, :, :].rearrange("p (g e) q -> p g e q", g=2), bc(cbm[:, :, :], 2, 4), mult,
                     [seg, cbm], [MT])
                k.tt(xdte[:, :, :], xdt[:, :, :], bc(e3[:, 8:16], 2, 64), mult, [xdt, e3], [xdte])
                for h in range(8):
                    k.mm(bank(7)[:, 64 * h:64 * (h + 1)], MT[:, h, :], xdt[:, h, :], True, True, [MT, xdt], [pb[7]])
                if blk > 0:
                    for g in range(2):
                        k.mm(bank(5)[:, 256 * g:256 * (g + 1)], xact[6 + g][:, cs], hbf[:, 256 * g:256 * (g + 1)],
                             True, True, [xact[6 + g], hbf], [pb[5]])
                    k.tt(y1[:, :].rearrange("p (h d) -> p h d", h=8),
                         bank(5)[:, :].rearrange("p (h d) -> p h d", h=8), bc(e3[:, 0:8], 2, 64), mult,
                         [pb[5], e3], [y1])
                    k.tt(y1[:, :], bank(7)[:, :], y1[:, :], add, [pb[7], y1], [y1])
                    k.tt(y1[:, :], y1[:, :], xsD[:, :], add, [y1, xsD], [y1])
                else:
                    k.tt(y1[:, :], bank(7)[:, :], xsD[:, :], add, [pb[7], xsD], [y1])
                for g in range(2):
                    k.mm(bank(4)[:, 256 * g:256 * (g + 1)], Btok[:, 128 * g:128 * (g + 1)],
                         xdte[:, 4 * g:4 * g + 4, :].rearrange("p h d -> p (h d)"), True, True, [Btok, xdte], [pb[4]])
                if blk > 0:
                    k.tt(hst[:, :].rearrange("p (h d) -> p h d", h=8), hst[:, :].rearrange("p (h d) -> p h d", h=8),
                         bc(e3[:, 16:24], 2, 64), mult, [hst, e3], [hst])
                    k.tt(hst[:, :], bank(4)[:, :], hst[:, :], add, [pb[4], hst], [hst])
                else:
                    k.cp(hst[:, :], bank(4)[:, :], [pb[4]], [hst])
                k.act(hbf[:, :], hst[:, :], AF.Copy, [hst], [hbf])
                k.tt(y1[:, :], y1[:, :], zs[:, :], mult, [y1, zs], [y1])
                for g in range(2):
                    k.act(yj[:, 256 * g:256 * (g + 1)], y1[:, 256 * g:256 * (g + 1)], AF.Square, [y1], [yj, ss2],
                          accum_out=ss2[:, g:g + 1])
                k.act(ss2[:, 2:4], ss2[:, 0:2], AF.Sqrt, [ss2, epsc], [ss2], scale=1.0 / 256, bias=epsc[:, 0:1])
                k.recip(ss2[:, 4:6], ss2[:, 2:4], [ss2], [ss2])
                k.tt(y1[:, :].rearrange("p (g d) -> p g d", g=2), y1[:, :].rearrange("p (g d) -> p g d", g=2),
                     bc(ss2[:, 4:6], 2, 256), mult, [y1, ss2], [y1])
                k.tt(y1[:, :], y1[:, :], nwbc[:, :], mult, [y1, nwbc], [y1])
                if dbg and blk in (0, 5, 15):
                    k.dma("sp", dbg_d["yssd"][blk * 128:(blk + 1) * 128, :], y1[:, :], reads=[y1])
                k.cp(ybf[:, :], y1[:, :], [y1], [ybf])
                pty = bank_bf(6)[:, 0:512].rearrange("p (c t) -> p c t", c=4)
                for c in range(4):
                    k.tr(pty[:, c, :], ybf[:, 128 * c:128 * (c + 1)], identb[:, :], [ybf, identb], [pb[6]])
                k.act(ysT[:, :, cs], pty[:, :, :], AF.Copy, [pb[6]], [ysT])
            for bi in range(4):
                blk = 4 * t + bi
                cs = slice(bi * 128, bi * 128 + 128)
                for kk in range(4):
                    for hf in range(2):
                        k.mm(bank(hf)[:, :], ysT[:, kk, cs], Wout[:, kk, 512 * hf:512 * (hf + 1)], kk == 0, kk == 3,
                             [ysT, Wout], [pb[hf]])
                for hf in range(2):
                    k.tt(x_tok[:, blk, 512 * hf:512 * (hf + 1)], bank(hf)[:, :], x_tok[:, blk, 512 * hf:512 * (hf + 1)],
                         add, [pb[hf], xb[blk]], [xb[blk]])
        k.dma("sp", o_conv_p, convo[:, :, :], reads=[convo])
        for j in range(4):
            k.tr(bank(2)[:, 128 * j:128 * (j + 1)], hst[:, 128 * j:128 * (j + 1)], ident, [hst, consts], [pb[2]])
        k.cp(y1[:, :], bank(2)[:, :], [pb[2]], [y1])
        k.dma("sp", o_ssm_p.rearrange("(j p) n -> p j n", p=128), y1[:, :].rearrange("p (j n) -> p j n", j=4),
              reads=[y1])
        k.barrier()

    with ExitStack() as es:
        Wout = k.sb("Wout", [128, 4, 1024], BF16, es)
        k.dma("pool", Wout[:, :, :], w_out_v[:, 0:4, :], writes=[Wout])
        sm = k.sb("smallrow", [128, 32], F32, es)
        nwbc = k.sb("nwbc", [128, 512], F32, es)
        for nm, o in (("dt_bias", 0), ("a_log", 8), ("ssd_d", 16)):
            ro, rs_ = RO[nm]
            k.dma("sp", sm.t[:, o:o + 8].unsqueeze(1), rows_d[0:1, ro:ro + 8].partition_broadcast(128), writes=[sm])
        row_bc("ssd_norm", nwbc)
        k.act(sm[:, 24:32], sm[:, 8:16], AF.Exp, [sm], [sm])
        k.ts(sm[:, 24:32], sm[:, 24:32], -1.0, None, mult, None, [sm], [sm])
        zs = k.sb("zss", [64, 512], F32, es)
        dtt = k.sb("dtts", [64, 40], F32, es)
        xsD = k.sb("xsDs", [64, 512], F32, es)
        xq = k.sb("xq", [128, 4, 64], F32, es)
        Bq = k.sb("Bq", [128, 4, 128], F32, es)
        Cq = k.sb("Cq", [128, 4, 128], F32, es)
        dAq = k.sb("dAq", [128, 4, 1], F32, es)
        yq = k.sb("yq", [128, 4, 64], F32, es)
        y1 = k.sb("y1s", [64, 512], F32, es)
        yj = k.sb("yjs", [64, 512], BF16, es)
        ss2 = k.sb("ss2s", [64, 8], F32, es)
        ybf = k.sb("ybfs", [64, 512], BF16, es)
        ysT = k.sb("ysTs", [128, 4, 64], BF16, es)
        esa = ExitStack()
        Wis = k.sb("Wis", [128, 8, 1544], BF16, esa)
        for h2 in range(2):
            k.dma("pool", Wis[:, 4 * h2:4 * h2 + 4, :], w_in_v[:, 4 * h2:4 * h2 + 4, 0:1544], writes=[Wis])
        xpre = k.sb("xpres", [128, 8, 112], F32, esa)
        xactf = k.sb("xactf", [128, 8, 64], F32, esa)
        acc = k.sb("accs", [128, 64], F32, esa)
        tokx = k.sb("tokx", [64, 1024], F32, esa)
        xdtf = k.sb("xdtf", [64, 512], F32, esa)
        Bh = k.sb("Bh", [64, 8, 128], F32, esa)
        Ch = k.sb("Ch", [64, 8, 128], F32, esa)
        scr_x_t = nc.dram_tensor("scr_x", [64, 512], F32)
        scr_B_t = nc.dram_tensor("scr_B", [64, 1024], F32)
        scr_C_t = nc.dram_tensor("scr_C", [64, 1024], F32)
        scr_a_t = nc.dram_tensor("scr_a", [64, 8], F32)
        scr_y_t = nc.dram_tensor("scr_y", [64, 512], F32)
        sbx, sbB, sbC, sba, sby = [Buf(n_) for n_ in "scx scB scC sca scy".split()]
        k.dma("sp", xpre[:, :, 0:48], st_conv_d.rearrange("p c j s -> p c (j s)"), writes=[xpre])
        n = 64
        c0 = 2048
        hbufs = [hTb[16]]
        for c in range(8):
            bi_ = c % 2
            for kk in range(8):
                k.mm(bank(bi_)[:, 0:n], Wis[:, kk, 512 + 128 * c:512 + 128 * (c + 1)], hT[:, kk, c0:c0 + n],
                     kk == 0, kk == 7, [Wis] + hbufs, [pb[bi_]])
            k.act(xpre[:, c, 48:112], bank(bi_)[:, 0:n], AF.Copy, [pb[bi_]], [xpre])
            k.act(acc[:, :], xpre[:, c, 0:64], AF.Identity, [xpre, cols], [acc], scale=col("ssd_cw", 4 * c),
                  bias=col("ssd_cb", c))
            for j in range(1, 4):
                k.stt(acc[:, :], xpre[:, c, 16 * j:16 * j + 64], col("ssd_cw", 4 * c + j), acc[:, :], mult, add,
                      [xpre, cols, acc], [acc])
            k.act(xactf[:, c, :], acc[:, :], AF.Silu, [acc], [xactf])
        k.dma("sp", o_conv_s.rearrange("p c j s -> p c (j s)"), xpre[:, :, 64:112], reads=[xpre])
        for kk in range(8):
            k.mm(bank(2)[0:64, :], hT[:, kk, c0:c0 + n], Wis[:, kk, 0:512], kk == 0, kk == 7, [Wis] + hbufs, [pb[2]])
        k.act(zs[:, :], bank(2)[0:64, :], AF.Silu, [pb[2]], [zs])
        for kk in range(8):
            k.mm(bank(3)[0:64, 0:8], hT[:, kk, c0:c0 + n], Wis[:, kk, 1536:1544], kk == 0, kk == 7, [Wis] + hbufs,
                 [pb[3]])
        k.tt(dtt[:, 0:8], bank(3)[0:64, 0:8], sm[0:64, 0:8], add, [pb[3], sm], [dtt])
        k.ts(dtt[:, 24:32], dtt[:, 0:8], 30.0, None, ALU.min, None, [dtt], [dtt])
        k.act(dtt[:, 24:32], dtt[:, 24:32], AF.Exp, [dtt], [dtt])
        k.act(dtt[:, 8:16], dtt[:, 24:32], AF.Ln, [dtt], [dtt], bias=1.0)
        k.ts(dtt[:, 24:32], dtt[:, 0:8], -30.0, 0.0, add, ALU.max, [dtt], [dtt])
        k.tt(dtt[:, 8:16], dtt[:, 8:16], dtt[:, 24:32], add, [dtt], [dtt])
        k.tt(dtt[:, 16:24], dtt[:, 8:16], sm[0:64, 24:32], mult, [dtt, sm], [dtt])
        k.act(dtt[:, 32:40], dtt[:, 16:24], AF.Exp, [dtt], [dtt])
        for c in range(8):
            bk = 4 + c // 4
            k.tr(bank(bk)[0:64, 128 * (c % 4):128 * (c % 4 + 1)], xactf[:, c, :], ident, [xactf, consts], [pb[bk]])
        k.act(tokx[:, 0:512], bank(4)[0:64, :], AF.Copy, [pb[4]], [tokx])
        k.act(tokx[:, 512:1024], bank(5)[0:64, :], AF.Copy, [pb[5]], [tokx])
        v8 = lambda ap: ap.rearrange("p (h d) -> p h d", h=8)
        k.tt(v8(xdtf[:, :]), v8(tokx[:, 0:512]), bc(dtt[:, 8:16], 2, 64), mult, [tokx, dtt], [xdtf])
        k.tt(v8(xsD[:, :]), v8(tokx[:, 0:512]), bc(sm[0:64, 16:24], 2, 64), mult, [tokx, sm], [xsD])
        k.cp(Bh[:, :, :].rearrange("p (g e) n -> p g e n", g=2),
             bc(tokx[:, 512:768].rearrange("p (g n) -> p g n", g=2), 2, 4), [tokx], [Bh])
        k.cp(Ch[:, :, :].rearrange("p (g e) n -> p g e n", g=2),
             bc(tokx[:, 768:1024].rearrange("p (g n) -> p g n", g=2), 2, 4), [tokx], [Ch])
        k.dma("sp", scr_x_t.ap(), xdtf[:, :], reads=[xdtf], writes=[sbx])
        k.dma("sp", scr_B_t.ap(), Bh[:, :, :].rearrange("p h n -> p (h n)"), reads=[Bh], writes=[sbB])
        k.dma("sp", scr_C_t.ap(), Ch[:, :, :].rearrange("p h n -> p (h n)"), reads=[Ch], writes=[sbC])
        k.dma("sp", scr_a_t.ap(), dtt[:, 32:40], reads=[dtt], writes=[sba])
        k.dma("sp", xq[:, :, :], bass.AP(scr_x_t, 0, [[64, 128], [8192, 4], [1, 64]]), reads=[sbx], writes=[xq])
        k.dma("sp", Bq[:, :, :], bass.AP(scr_B_t, 0, [[128, 128], [16384, 4], [1, 128]]), reads=[sbB], writes=[Bq])
        k.dma("sp", Cq[:, :, :], bass.AP(scr_C_t, 0, [[128, 128], [16384, 4], [1, 128]]), reads=[sbC], writes=[Cq])
        k.dma("sp", dAq[:, :, :], bass.AP(scr_a_t, 0, [[1, 128], [128, 4], [1, 1]]), reads=[sba], writes=[dAq],
              allow_slow_non_contiguous=True)
        k.barrier()
        esa.close()
        hS = k.sb("hS", [128, 8192], F32, es)
        tmpS = k.sb("tmpS", [128, 8192], F32, es)
        k.dma("sp", hS[:, :], st_ssm_d, writes=[hS])
        h3 = lambda ap: ap.rearrange("p (a b) -> p a b", b=128)
        for l in range(4):
            k.tt(h3(tmpS[:, :]), bc(xq[:, l, :], 2, 128), bc(Bq[:, l, :], 1, 64), mult, [xq, Bq], [tmpS], eng="pool")
            k.stt(hS[:, :], hS[:, :], dAq[:, l, :], tmpS[:, :], mult, add, [hS, dAq, tmpS], [hS])
            k.tt(h3(tmpS[:, :]), h3(hS[:, :]), bc(Cq[:, l, :], 1, 64), mult, [hS, Cq], [tmpS])
            k.red(yq[:, l, :], h3(tmpS[:, :]), add, [tmpS], [yq])
        k.dma("sp", o_ssm_s, hS[:, :], reads=[hS])
        k.dma("sp", bass.AP(scr_y_t, 0, [[64, 128], [8192, 4], [1, 64]]), yq[:, :, :], reads=[yq], writes=[sby])
        k.dma("sp", y1[:, :], scr_y_t.ap(), reads=[sby], writes=[y1])
        k.tt(y1[:, :], y1[:, :], xsD[:, :], add, [y1, xsD], [y1])
        k.tt(y1[:, :], y1[:, :], zs[:, :], mult, [y1, zs], [y1])
        for g in range(2):
            k.act(yj[:, 256 * g:256 * (g + 1)], y1[:, 256 * g:256 * (g + 1)], AF.Square, [y1], [yj, ss2],
                  accum_out=ss2[:, g:g + 1])
        k.act(ss2[:, 2:4], ss2[:, 0:2], AF.Sqrt, [ss2, epsc], [ss2], scale=1.0 / 256, bias=epsc[0:64, 0:1])
        k.recip(ss2[:, 4:6], ss2[:, 2:4], [ss2], [ss2])
        k.tt(y1[:, :].rearrange("p (g d) -> p g d", g=2), y1[:, :].rearrange("p (g d) -> p g d", g=2),
             bc(ss2[:, 4:6], 2, 256), mult, [y1, ss2], [y1])
        k.tt(y1[:, :], y1[:, :], nwbc[0:64, :], mult, [y1, nwbc], [y1])
        if dbg:
            k.dma("sp", dbg_d["yssd"][2048:2112, :], y1[:, :], reads=[y1])
        k.cp(ybf[:, :], y1[:, :], [y1], [ybf])
        pty = bank_bf(6)[:, 0:256].rearrange("p (c t) -> p c t", c=4)
        for c in range(4):
            k.tr(pty[:, c, :], ybf[:, 128 * c:128 * (c + 1)], identb[0:64, 0:64], [ybf, identb], [pb[6]])
        k.act(ysT[:, :, :], pty[:, :, :], AF.Copy, [pb[6]], [ysT])
        for hf in range(2):
            bk = hf
            for kk in range(4):
                k.mm(bank(bk)[0:64, :], ysT[:, kk, :], Wout[:, kk, 512 * hf:512 * (hf + 1)], kk == 0, kk == 3,
                     [ysT, Wout], [pb[bk]])
            k.tt(x_tok[0:64, 16, 512 * hf:512 * (hf + 1)], bank(bk)[0:64, :], x_tok[0:64, 16, 512 * hf:512 * (hf + 1)],
                 add, [pb[bk], xb[16]], [xb[16]])
        k.barrier()

    with ExitStack() as es:
        Wir, WirB = load_weight("Wir", w_in_v[:, :, 1544:3336], 1792,
                                [(128 * c_, 128 * c_ + 128) for c_ in [12, 13] + [c_ for p_ in range(4) for c_ in (p_, 4 + p_, 8 + p_)]], es)
        Wout2 = k.sb("Wout2", [128, 4, 1024], BF16, es)
        k.dma("pool", Wout2[:, :, :], w_out_v[:, 4:8, :], writes=[Wout2])
        W2A2 = k.sb("W2A2", [128, 512], BF16, es)
        G2 = k.sb("G2", [128, 512], BF16, es)
        k.dma("pool", W2A2[:, :], w2a2_d, writes=[W2A2])
        k.dma("pool", G2[:, :], g2_d, writes=[G2])
        c2 = k.sb("consts2", [128, NCONST2], F32, es)
        k.dma("sp", c2[:, :], consts2_d, writes=[c2])

        def cst2(name):
            o, s = KO2[name]
            return c2[:, o:o + s]

        lnw = k.sb("lnw", [128, 4, 64], F32, es)
        lnb = k.sb("lnb", [128, 4, 64], F32, es)
        for nm, tt_ in (("ln_w", lnw), ("ln_b", lnb)):
            o, s = RO[nm]
            v = rows_d[0:1, o:o + 512].rearrange("a (pr hp d) -> a pr hp d", pr=4, hp=2)
            for hp in range(2):
                k.dma("sp", tt_.t[64 * hp:64 * hp + 64, :, :].unsqueeze(1), v[:, :, hp, :].partition_broadcast(64),
                      writes=[tt_])
        omm = k.sb("omm", [128, 14], F32, es)
        k.ts(omm[:, :], col("mu", 0, 14), -1.0, 1.0, mult, add, [cols], [omm])
        onesb = k.sb("onesb", [128, 2], BF16, es)
        k.memset(onesb[:, :], 1.0, [onesb])
        usb = [k.sb("usb", [128, 516], F32, es) for _ in range(2)]
        carry = k.sb("carry", [128, 14, 16], F32, es)
        k.memset(carry[:, :, :], 0.0, [carry])
        tlla = k.sb("tlla", [128, 512], BF16, es)
        sg = k.sb("sg", [128, 512], BF16, es)
        f = lambda nm: k.sb(nm, [128, 512], F32, es)
        um_r, um_k, um_v, sigw, a_t, kkn, t1, t2, csm, ecw, encw, exw = [f(n_) for n_ in
            "um_r um_k um_v sigw a_t kkn t1 t2 csm ecw encw exw".split()]
        gT = k.sb("gT", [128, 512], BF16, es)
        um12 = um_v
        tmpm = kkn
        AR = k.sb("AR", [128, 8, 2, 128], BF16, es)
        Bt = k.sb("Bt", [128, 8, 128], BF16, es)
        Kt = k.sb("Kt", [128, 8, 128], BF16, es)
        vT = k.sb("vT", [128, 8, 128], BF16, es)
        for z_ in (AR, Bt, Kt, vT):
            k.memset(z_.t[:].rearrange("p a b -> p (a b)") if len(z_.t.shape) == 3 else
                     z_.t[:].rearrange("p a b c -> p (a b c)"), 0.0, [z_])
        Ub2 = [k.sb("Ub", [128, 128], BF16, es) for _ in range(2)]
        Hf = k.sb("Hf", [128, 4, 128], F32, es)
        Hb = k.sb("Hb", [128, 4, 128], BF16, es)
        Hfb = [Buf(f"Hf{p}") for p in range(4)]
        Hbb = [Buf(f"Hb{p}") for p in range(4)]
        ht = k.sb("ht", [128, 128], F32, es)
        hout = k.sb("hout", [128, 4, 64], F32, es)
        vtok = k.sb("vtok", [128, 8, 64], BF16, es)
        class _V:
            def __init__(self, base):
                self.b = base.b
                self.v = base.t[:, :].rearrange("p (c d) -> p c d", d=64)

            def __getitem__(self, idx):
                return self.v[idx]
        ysb = _V(exw)
        ysq = _V(t2)
        st8 = k.sb("st8", [128, 6, 8], F32, es)
        ynb = k.sb("ynb", [128, 8, 64], BF16, es)
        prod = View(ynb.t[:].rearrange("p c d -> p (c d)"), ynb.b)
        yrT = k.sb("yrT", [128, 4, 512], BF16, es)
        shifto = k.sb("shifto", [128, 14], F32, es)
        k.memset(Hf[:, :, :], 0.0, Hfb)
        k.memset(Hb[:, :, :], 0.0, Hbb)
        wmask, i64c, bones, scanm = cst2("wmask"), cst2("i64"), cst2("bones"), cst2("scanm")
        HP = [slice(0, 64), slice(64, 128)]

        def proj_chunk(c, c0, n, S, hbufs, dst, ucount):
            bi_ = 4 + ucount % 2
            u_ = usb[ucount % 2]
            for kk in range(8):
                k.mm(bank(bi_)[:, 0:n], Wir[:, kk, 128 * c:128 * (c + 1)], hT[:, kk, c0:c0 + n], kk == 0, kk == 7,
                     WirB(128 * c, 128 * c + 128) + hbufs, [pb[bi_]])
            k.act(u_[:, S:S + n], bank(bi_)[:, 0:n], AF.Copy, [pb[bi_]], [u_])
            k.cp(u_[:, 0:S], carry[:, c, 0:S], [carry], [u_], eng="pool")
            k.act(tmpm[:, 0:n], u_[:, 0:n], AF.Identity, [u_, cols], [tmpm], scale=col("mu", c))
            k.stt(dst[:, 0:n], u_[:, S:S + n], omm[:, c:c + 1], tmpm[:, 0:n], mult, add, [u_, omm, tmpm], [dst])
            k.cp(carry[:, c, 0:S], u_[:, n:n + S], [u_], [carry], eng="pool")

        ucount = 0
        for t in range(NTB):
            n = 512
            S = 1
            c0 = t * 512
            nch = n // 64
            hbufs = hTb[4 * t:4 * t + 4]
            proj_chunk(12, c0, n, S, hbufs, um12, ucount); ucount += 1
            k.act(tlla[0:64, 0:n], um12[0:64, 0:n], AF.Tanh, [um12], [tlla])
            k.act(tlla[64:128, 0:n], um12[64:128, 0:n], AF.Copy, [um12], [tlla])
            proj_chunk(13, c0, n, S, hbufs, um12, ucount); ucount += 1
            k.act(sg[:, 0:n], um12[:, 0:n], AF.Sigmoid, [um12], [sg])
            for p in range(4):
                pc = slice(128 * p, 128 * (p + 1))
                k.mm(bank(6)[:, 0:n], W2A2[0:64, pc], tlla[0:64, 0:n], True, True, [W2A2, tlla], [pb[6]])
                k.mm(bank(7)[:, 0:n], W2A2[64:128, pc], tlla[64:128, 0:n], True, True, [W2A2, tlla], [pb[7]])
                k.act(sigw[:, 0:n], bank(6)[:, 0:n], AF.Sigmoid, [pb[6], cols], [sigw], bias=col("w0", p))
                k.act(a_t[:, 0:n], bank(7)[:, 0:n], AF.Sigmoid, [pb[7], cols], [a_t], bias=col("a0", p))
                k.mm(bank(6)[:, 0:n], G2[:, pc], sg[:, 0:n], True, True, [G2, sg], [pb[6]])
                k.act(gT[:, 0:n], bank(6)[:, 0:n], AF.Copy, [pb[6]], [gT])
                proj_chunk(p, c0, n, S, hbufs, um_r, ucount); ucount += 1
                proj_chunk(4 + p, c0, n, S, hbufs, um_k, ucount); ucount += 1
                proj_chunk(8 + p, c0, n, S, hbufs, um_v, ucount); ucount += 1
                k.ts(t1[:, 0:n], um_k[:, 0:n], col("k_k", p), None, mult, None, [um_k, cols], [t1])
                k.tt(t2[:, 0:n], t1[:, 0:n], t1[:, 0:n], mult, [t1], [t2])
                k.mm(bank(7)[:, 0:n], bones, t2[:, 0:n], True, True, [c2, t2], [pb[7]])
                k.ts(t2[:, 0:n], bank(7)[:, 0:n], 1e-19, None, ALU.max, None, [pb[7]], [t2])
                k.act(t2[:, 0:n], t2[:, 0:n], AF.Ln, [t2], [t2])
                k.act(t2[:, 0:n], t2[:, 0:n], AF.Exp, [t2], [t2], scale=-0.5)
                k.tt(kkn[:, 0:n], t1[:, 0:n], t2[:, 0:n], mult, [t1, t2], [kkn])
                k.ts(t1[:, 0:n], a_t[:, 0:n], -1.0, col("k_a", p), add, mult, [a_t, cols], [t1])
                k.stt(t1[:, 0:n], t1[:, 0:n], 1.0, um_k[:, 0:n], add, mult, [t1, um_k], [t1])
                k.scan(csm[:, 0:n], scanm[:, 0:n], sigw[:, 0:n], 0.0, mult, add, [c2, sigw], [csm])
                k.tt(t2[:, 0:n], csm[:, 0:n], sigw[:, 0:n], sub, [csm, sigw], [t2])
                k.act(ecw[:, 0:n], csm[:, 0:n], AF.Exp, [csm], [ecw], scale=-C0)
                k.act(encw[:, 0:n], csm[:, 0:n], AF.Exp, [csm], [encw], scale=C0)
                k.act(exw[:, 0:n], t2[:, 0:n], AF.Exp, [t2], [exw], scale=-C0)
                v3 = lambda ap: ap.rearrange("p (c t) -> p c t", t=64)
                k.tt(t2[:, 0:n], a_t[:, 0:n], kkn[:, 0:n], mult, [a_t, kkn], [t2])
                for hp in range(2):
                    rs = HP[hp]
                    dc_ = slice(64 * hp, 64 * hp + 64)
                    k.stt(AR[rs, 0:nch, 0, dc_], v3(kkn[rs, 0:n]), -1.0, v3(exw[rs, 0:n]), mult, mult, [kkn, exw], [AR])
                    k.tt(AR[rs, 0:nch, 1, dc_], v3(um_r[rs, 0:n]), v3(ecw[rs, 0:n]), mult, [um_r, ecw], [AR])
                    k.tt(Bt[rs, 0:nch, dc_], v3(t2[rs, 0:n]), v3(encw[rs, 0:n]), mult, [t2, encw], [Bt])
                    k.tt(Kt[rs, 0:nch, dc_], v3(t1[rs, 0:n]), v3(encw[rs, 0:n]), mult, [t1, encw], [Kt])
                    k.act(vT[rs, 0:nch, dc_], v3(um_v[rs, 0:n]), AF.Copy, [um_v], [vT])
                k.stt(prod[:, 0:n], um_r[:, 0:n], col("r_k", p), t1[:, 0:n], mult, mult, [um_r, cols, t1], [prod])
                for ch in range(nch):
                    tk = slice(64 * ch, 64 * ch + 64)
                    for hp in range(2):
                        rs = HP[hp]
                        k.mm(bank(6)[rs, 256 + ch:257 + ch], prod[rs, tk], onesb[rs, 0:1], True, True, [prod, onesb],
                             [pb[6]])
                k.cp(st8[:, 5, 0:nch], bank(6)[:, 256:256 + nch], [pb[6]], [st8])
                cpar = [(um_r, um_k), (um_v, sigw), (a_t, kkn), (t1, csm)]
                ubs = []
                for g in range(4):
                    va = k.carve(cpar[g][0], [(1280, BF16), (768, BF16)])
                    vb = k.carve(cpar[g][1], [(768, BF16), (256, BF16), (1024, BF16)])
                    ubs.append([va[0], va[1], vb[0], vb[1], vb[2]])
                sm_a = k.carve(encw, [(256, BF16), (256, BF16), (512, F32)] * 2)
                sm_b = k.carve(t2, [(256, BF16), (256, BF16), (512, F32)] * 2)
                smalls = sm_a + sm_b
                for grp in range(nch // 4):
                    chs = [4 * grp + g for g in range(4)]
                    for g, ch in enumerate(chs):
                        mats = ubs[g][0]
                        arc = AR[:, ch, :, :].rearrange("p a t -> p (a t)")
                        k.mm(bank(g)[:, 0:256], Bt[:, ch, :], arc, True, True, [Bt, AR], [pb[g]])
                        k.mm(bank(g)[:, 256:512], Kt[:, ch, :], arc, True, True, [Kt, AR], [pb[g]])
                        k.tt(mats[:, 0:512], bank(g)[:, 0:512], wmask[:, 0:512], mult, [pb[g], c2], [mats])
                    for g, ch in enumerate(chs):
                        mats, l0 = ubs[g][0], ubs[g][1]
                        k.mm(bank(g)[:, 0:128], AR[:, ch, 0, :], Bt[:, ch, :], True, True, [Bt, AR], [pb[g]])
                        k.tt(mats[:, 512:640], bank(g)[:, 0:128], wmask[:, 512:640], mult, [pb[g], c2], [mats])
                        k.tt(l0[:, 0:128], mats[:, 0:128], ident, add, [mats, consts], [l0])
                    for g in range(4):
                        mats, l0 = ubs[g][0], ubs[g][1]
                        k.mm(bank(g)[:, 0:128], mats[:, 512:640], mats[:, 0:128], True, True, [mats], [pb[g]])
                        k.mm(bank(g)[:, 128:256], mats[:, 0:128], mats[:, 512:640], True, True, [mats], [pb[g]])
                        k.act(l0[:, 128:384], bank(g)[:, 0:256], AF.Copy, [pb[g]], [l0])
                    ci, ni = 1, 2
                    for j in range(1, 5):
                        for g in range(4):
                            cur, nxt = ubs[g][ci], ubs[g][ni]
                            k.mm(bank(g)[:, 0:256], cur[:, 256:384], cur[:, 0:256], True, True, [cur], [pb[g]])
                            k.mm(bank(g)[:, 256:384], cur[:, 128:256], cur[:, 256:384], True, True, [cur], [pb[g]])
                            k.tt(nxt[:, 0:128], cur[:, 0:128], bank(g)[:, 0:128], add, [cur, pb[g]], [nxt])
                            k.act(nxt[:, 128:384], bank(g)[:, 128:384], AF.Copy, [pb[g]], [nxt])
                        ci, ni = ni, ci
                    for g in range(4):
                        cur, TTm_ = ubs[g][ci], ubs[g][3]
                        k.mm(bank(g)[:, 0:128], cur[:, 256:384], cur[:, 0:128], True, True, [cur], [pb[g]])
                        k.tt(TTm_[:, :], cur[:, 0:128], bank(g)[:, 0:128], add, [cur, pb[g]], [TTm_])
                    for g, ch in enumerate(chs):
                        tok_ = ubs[g][4]
                        ptk = bank_bf(g)
                        srcs = [(AR[:, ch, 0, :], AR), (Bt[:, ch, :], Bt), (Kt[:, ch, :], Kt), (vT[:, ch, :], vT)]
                        for q, (sap, sb_) in enumerate(srcs):
                            k.tr(ptk[:, 128 * q:128 * (q + 1)], sap, identb[:, :], [sb_, identb], [pb[g]])
                        k.act(tok_[:, :], bank_bf(g)[:, 0:512], AF.Copy, [pb[g]], [tok_])
                        for hp in range(2):
                            rs = HP[hp]
                            k.cp(vtok[rs, ch, :], tok_[rs, 384 + 64 * hp:448 + 64 * hp], [tok_], [vtok], eng="pool")
                    for g in range(4):
                        mats, tok_, X1b_ = ubs[g][0], ubs[g][4], smalls[3 * g]
                        k.mm(bank(g)[:, 256:384], mats[:, 256:384], tok_[:, 384:512], True, True, [mats, tok_], [pb[g]])
                        k.act(X1b_[:, :], bank(g)[:, 256:384], AF.Copy, [pb[g]], [X1b_])
                    for g in range(4):
                        tok_, TTm_, X1b_ = ubs[g][4], ubs[g][3], smalls[3 * g]
                        ApT_, Vp_ = smalls[3 * g + 1], smalls[3 * g + 2]
                        k.mm(bank(g)[:, 0:128], tok_[:, 0:128], TTm_[:, :], True, True, [tok_, TTm_], [pb[g]])
                        k.mm(bank(g)[:, 128:256], TTm_[:, :], X1b_[:, :], True, True, [TTm_, X1b_], [pb[g]])
                        k.act(ApT_[:, :], bank(g)[:, 0:128], AF.Copy, [pb[g]], [ApT_])
                        k.cp(Vp_[:, :], bank(g)[:, 128:256], [pb[g]], [Vp_])
                    if DBGU < 4:
                        continue
                    for g, ch in enumerate(chs):
                        mats, tok_ = ubs[g][0], ubs[g][4]
                        ApT_, Vp_ = smalls[3 * g + 1], smalls[3 * g + 2]
                        Ub_ = Ub2[g % 2]
                        k.mm(bank(6)[:, 0:128], ApT_[:, :], Hb[:, p, :], True, True, [ApT_, Hbb[p]], [pb[6]])
                        k.tt(Ub_[:, :], bank(6)[:, 0:128], Vp_[:, :], add, [pb[6], Vp_], [Ub_])
                        ho = bank(6)[:, 128:256]
                        k.mm(ho, tok_[:, 256:384], tok_[:, 384:512], True, False, [tok_], [pb[6]])
                        k.mm(ho, tok_[:, 128:256], Ub_[:, :], False, True, [tok_, Ub_], [pb[6]])
                        yo = bank(7)[:, 128 * g:128 * (g + 1)]
                        k.mm(yo, AR[:, ch, 1, :], Hb[:, p, :], True, False, [AR, Hbb[p]], [pb[7]])
                        k.mm(yo, mats[:, 128:256], Ub_[:, :], False, False, [mats, Ub_], [pb[7]])
                        k.mm(yo, mats[:, 384:512], tok_[:, 384:512], False, True, [mats, tok_], [pb[7]])
                        gcol = ecw[:, 64 * ch + 63:64 * ch + 64]
                        k.tt(ht[:, :], bank(6)[:, 128:256], Hf[:, p, :], add, [pb[6], Hfb[p]], [ht])
                        k.act(Hb[:, p, :], ht[:, :], AF.Identity, [ht, ecw], [Hbb[p]], scale=gcol)
                        k.ts(Hf[:, p, :], ht[:, :], gcol, None, mult, None, [ht, ecw], [Hfb[p]])
                    if DBGU < 5:
                        continue
                    k.act(usb[0][:, 0:512], bank(7)[:, :], AF.Copy, [pb[7]], [usb[0]])
                    for hp in range(2):
                        rs = HP[hp]
                        k.cp(ysb[rs, 4 * grp:4 * grp + 4, :],
                             usb[0][rs, 0:512].rearrange("p (c d) -> p c d", d=128)[:, :, 64 * hp:64 * hp + 64],
                             [usb[0]], [ysb], eng="pool")
                for g in range(4):
                    k.merge(cpar[g][0], ubs[g][0:2])
                    k.merge(cpar[g][1], ubs[g][2:5])
                k.merge(encw, sm_a)
                k.merge(t2, sm_b)
                k.red(st8[:, 0, 0:nch], ysb[:, 0:nch, :], add, [ysb], [st8])
                k.tt(ysq[:, 0:nch, :], ysb[:, 0:nch, :], ysb[:, 0:nch, :], mult, [ysb], [ysq])
                k.red(st8[:, 1, 0:nch], ysq[:, 0:nch, :], add, [ysq], [st8])
                k.ts(st8[:, 2, 0:nch], st8[:, 0, 0:nch], 1.0 / 64, None, mult, None, [st8], [st8])
                k.tt(st8[:, 4, 0:nch], st8[:, 2, 0:nch], st8[:, 2, 0:nch], mult, [st8], [st8])
                k.stt(st8[:, 3, 0:nch], st8[:, 1, 0:nch], 1.0 / 64, st8[:, 4, 0:nch], mult, sub, [st8], [st8])
                k.act(st8[:, 3, 0:nch], st8[:, 3, 0:nch], AF.Sqrt, [st8, epsc], [st8], bias=epsc[:, 1:2])
                k.recip(st8[:, 3, 0:nch], st8[:, 3, 0:nch], [st8], [st8])
                k.tt(ysb[:, 0:nch, :], ysb[:, 0:nch, :], bc(st8[:, 2, 0:nch], 2, 64), sub, [ysb, st8], [ysb])
                k.tt(ysb[:, 0:nch, :], ysb[:, 0:nch, :], bc(st8[:, 3, 0:nch], 2, 64), mult, [ysb, st8], [ysb])
                k.tt(ysb[:, 0:nch, :], ysb[:, 0:nch, :], bc(lnw[:, p, :], 1, nch), mult, [ysb, lnw], [ysb])
                k.tt(ysb[:, 0:nch, :], ysb[:, 0:nch, :], bc(lnb[:, p, :], 1, nch), add, [ysb, lnb], [ysb])
                k.tt(ysq[:, 0:nch, :], vtok[:, 0:nch, :], bc(st8[:, 5, 0:nch], 2, 64), mult, [vtok, st8], [ysq])
                k.tt(ynb[:, 0:nch, :], ysb[:, 0:nch, :], ysq[:, 0:nch, :], add, [ysb, ysq], [ynb])
                pty = bank_bf(5)
                for ch in range(nch):
                    for hp in range(2):
                        rs = HP[hp]
                        k.tr(pty[rs, 64 * ch:64 * (ch + 1)], ynb[rs, ch, :], identb[rs, rs], [ynb, identb], [pb[5]])
                k.tt(yrT[:, p, 0:n], pty[:, 0:n], gT[:, 0:n], mult, [pb[5], gT], [yrT])
            if dbg:
                for p in range(4):
                    k.cp(um_r[:, 0:n], yrT[:, p, 0:n], [yrT], [um_r])
                    k.dma("sp", dbg_d["yrwT"][:, p, c0:c0 + n], um_r[:, 0:n], reads=[um_r])
            for bi in range(n // 128):
                blk = 4 * t + bi
                cs = slice(bi * 128, bi * 128 + 128)
                for kk in range(4):
                    for hf in range(2):
                        k.mm(bank(hf)[:, :], yrT[:, kk, cs], Wout2[:, kk, 512 * hf:512 * (hf + 1)], kk == 0, kk == 3,
                             [yrT, Wout2], [pb[hf]])
                for hf in range(2):
                    k.tt(x_tok[:, blk, 512 * hf:512 * (hf + 1)], bank(hf)[:, :], x_tok[:, blk, 512 * hf:512 * (hf + 1)],
                         add, [pb[hf], xb[blk]], [xb[blk]])
        k.cp(shifto[:, :], carry[:, :, 0], [carry], [shifto])
        k.dma("sp", o_shift_p, shifto[:, :], reads=[shifto])
        for hp in range(2):
            rs = HP[hp]
            k.cp(hout[rs, :, :], Hf[rs, :, 64 * hp:64 * hp + 64], Hfb, [hout])
        k.dma("sp", o_wkv_p, hout[:, :, :], reads=[hout])
        k.barrier()

    with ExitStack() as es:
        Wout2 = k.sb("Wout2", [128, 4, 1024], BF16, es)
        k.dma("pool", Wout2[:, :, :], w_out_v[:, 4:8, :], writes=[Wout2])
        G2 = k.sb("G2", [128, 512], BF16, es)
        k.dma("pool", G2[:, :], g2_d, writes=[G2])
        lnwr = k.sb("lnwr", [64, 512], F32, es)
        lnbr = k.sb("lnbr", [64, 512], F32, es)
        rkr = k.sb("rkr", [64, 512], F32, es)
        row_bc("ln_w", lnwr, 64)
        row_bc("ln_b", lnbr, 64)
        row_bc("r_k_row", rkr, 64)
        sg = k.sb("sgs", [128, 64], BF16, es)
        tokq = k.sb("tokq", [64, 6, 512], F32, es)
        rq = k.sb("rq", [128, 6, 4, 64], F32, es)
        yq = k.sb("yqw", [128, 4, 64], F32, es)
        sa = k.sb("sa", [128, 64], F32, es)
        yw = k.sb("yw", [64, 512], F32, es)
        ysq = k.sb("ysqs", [64, 512], F32, es)
        st8 = k.sb("st8s", [64, 6, 8], F32, es)
        gtok = k.sb("gtok", [64, 512], F32, es)
        ynb = k.sb("ynbs", [64, 512], BF16, es)
        yrT = k.sb("yrTs", [128, 4, 64], BF16, es)
        scr_q_t = nc.dram_tensor("scr_q", [6, 64, 512], F32)
        scr_w_t = nc.dram_tensor("scr_yw", [64, 512], F32)
        sbq, sbw = Buf("scq"), Buf("scw")
        esa = ExitStack()
        Wir = k.sb("Wir", [128, 8, 1792], BF16, esa)
        for h2 in range(2):
            k.dma("pool", Wir[:, 4 * h2:4 * h2 + 4, :], w_in_v[:, 4 * h2:4 * h2 + 4, 1544:3336], writes=[Wir])
        W2A2 = k.sb("W2A2", [128, 512], BF16, esa)
        k.dma("pool", W2A2[:, :], w2a2_d, writes=[W2A2])
        bon = k.sb("bon", [128, 128], F32, esa)
        o_, s_ = KO2["bones"]
        k.dma("sp", bon[:, :], consts2_d[:, o_:o_ + s_], writes=[bon])
        omm = k.sb("omm", [128, 14], F32, esa)
        k.ts(omm[:, :], col("mu", 0, 14), -1.0, 1.0, mult, add, [cols], [omm])
        usb = [k.sb("usbs", [128, 80], F32, esa) for _ in range(2)]
        tmpm = k.sb("tmpms", [128, 64], F32, esa)
        carry = k.sb("carrys", [128, 14, 16], F32, esa)
        k.dma("sp", carry[:, :, :], st_shift_d, writes=[carry])
        tlla = k.sb("tllas", [128, 64], BF16, esa)
        f = lambda nm: k.sb(nm, [128, 64], F32, esa)
        um_r, um_k, um_v, sigw, a_t, kkn, t1, t2, wdec = [f(n_) for n_ in
                                                          "um_r um_k um_v sigw a_t kkn t1 t2 wdec".split()]
        n = 64
        S = 16
        c0 = 2048
        hbufs = [hTb[16]]

        def proj_chunk_s(c, dst, ucount):
            bi_ = ucount % 2
            u_ = usb[ucount % 2]
            for kk in range(8):
                k.mm(bank(bi_)[:, 0:n], Wir[:, kk, 128 * c:128 * (c + 1)], hT[:, kk, c0:c0 + n], kk == 0, kk == 7,
                     [Wir] + hbufs, [pb[bi_]])
            k.act(u_[:, S:S + n], bank(bi_)[:, 0:n], AF.Copy, [pb[bi_]], [u_])
            k.cp(u_[:, 0:S], carry[:, c, 0:S], [carry], [u_], eng="pool")
            k.act(tmpm[:, 0:n], u_[:, 0:n], AF.Identity, [u_, cols], [tmpm], scale=col("mu", c))
            k.stt(dst[:, 0:n], u_[:, S:S + n], omm[:, c:c + 1], tmpm[:, 0:n], mult, add, [u_, omm, tmpm], [dst])
            k.cp(carry[:, c, 0:S], u_[:, n:n + S], [u_], [carry], eng="pool")

        uc = 0
        proj_chunk_s(12, um_v, uc); uc += 1
        k.act(tlla[0:64, :], um_v[0:64, :], AF.Tanh, [um_v], [tlla])
        k.act(tlla[64:128, :], um_v[64:128, :], AF.Copy, [um_v], [tlla])
        proj_chunk_s(13, um_v, uc); uc += 1
        k.act(sg[:, :], um_v[:, :], AF.Sigmoid, [um_v], [sg])
        for p in range(4):
            pc = slice(128 * p, 128 * (p + 1))
            k.mm(bank(2)[:, 0:n], W2A2[0:64, pc], tlla[0:64, :], True, True, [W2A2, tlla], [pb[2]])
            k.mm(bank(3)[:, 0:n], W2A2[64:128, pc], tlla[64:128, :], True, True, [W2A2, tlla], [pb[3]])
            k.act(sigw[:, :], bank(2)[:, 0:n], AF.Sigmoid, [pb[2], cols], [sigw], bias=col("w0", p))
            k.act(a_t[:, :], bank(3)[:, 0:n], AF.Sigmoid, [pb[3], cols], [a_t], bias=col("a0", p))
            proj_chunk_s(p, um_r, uc); uc += 1
            proj_chunk_s(4 + p, um_k, uc); uc += 1
            proj_chunk_s(8 + p, um_v, uc); uc += 1
            k.ts(t1[:, :], um_k[:, :], col("k_k", p), None, mult, None, [um_k, cols], [t1])
            k.tt(t2[:, :], t1[:, :], t1[:, :], mult, [t1], [t2])
            k.mm(bank(2)[:, 0:n], bon[:, :], t2[:, :], True, True, [bon, t2], [pb[2]])
            k.act(t2[:, :], bank(2)[:, 0:n], AF.Sqrt, [pb[2]], [t2])
            k.ts(t2[:, :], t2[:, :], 1e-12, None, ALU.max, None, [t2], [t2])
            k.recip(t2[:, :], t2[:, :], [t2], [t2])
            k.tt(kkn[:, :], t1[:, :], t2[:, :], mult, [t1, t2], [kkn])
            k.ts(t1[:, :], a_t[:, :], -1.0, col("k_a", p), add, mult, [a_t, cols], [t1])
            k.stt(t1[:, :], t1[:, :], 1.0, um_k[:, :], add, mult, [t1, um_k], [t1])
            k.tt(t2[:, :], a_t[:, :], kkn[:, :], mult, [a_t, kkn], [t2])
            k.act(wdec[:, :], sigw[:, :], AF.Exp, [sigw], [wdec], scale=-C0)
            srcs = [um_r, wdec, t1, um_v, kkn, t2]
            for q, sb_ in enumerate(srcs):
                bk = 4 + q // 3
                k.tr(bank(bk)[0:64, 128 * (q % 3):128 * (q % 3 + 1)], sb_[:, :], ident, [sb_, consts], [pb[bk]])
            for hq in range(2):
                k.act(tokq[:, 3 * hq:3 * hq + 3, pc], bank(4 + hq)[0:64, 0:384].rearrange("p (q d) -> p q d", q=3),
                      AF.Copy, [pb[4 + hq]], [tokq])
        k.mm(bank(2)[0:64, :], sg[:, :], G2[:, :], True, True, [sg, G2], [pb[2]])
        k.act(gtok[:, :], bank(2)[0:64, :], AF.Copy, [pb[2]], [gtok])
        k.dma("sp", o_shift_s, carry[:, :, :], reads=[carry])
        k.dma("sp", scr_q_t.ap().rearrange("q t d -> t q d"), tokq[:, :, :], reads=[tokq], writes=[sbq])
        k.dma("sp", rq[:, :, :, :], bass.AP(scr_q_t, 0, [[64, 128], [32768, 6], [8192, 4], [1, 64]]), reads=[sbq],
              writes=[rq])
        k.barrier()
        esa.close()
        Sw = k.sb("Sw", [128, 4096], F32, es)
        tmpW = k.sb("tmpW", [128, 4096], F32, es)
        tmpV = k.sb("tmpV", [128, 4096], F32, es)
        k.dma("sp", Sw[:, :], st_wkv_d, writes=[Sw])
        m3 = lambda ap: ap.rearrange("p (i j) -> p i j", j=64)
        for l in range(4):
            k.tt(m3(tmpV[:, :]), bc(rq[:, 3, l, :], 2, 64), bc(rq[:, 2, l, :], 1, 64), mult, [rq], [tmpV], eng="pool")
            k.tt(m3(tmpW[:, :]), m3(Sw[:, :]), bc(rq[:, 4, l, :], 1, 64), mult, [Sw, rq], [tmpW])
            k.red(sa[:, :], m3(tmpW[:, :]), add, [tmpW], [sa], negate=True)
            k.tt(m3(Sw[:, :]), m3(Sw[:, :]), bc(rq[:, 1, l, :], 1, 64), mult, [Sw, rq], [Sw])
            k.tt(m3(tmpW[:, :]), bc(sa[:, :], 2, 64), bc(rq[:, 5, l, :], 1, 64), mult, [sa, rq], [tmpW])
            k.tt(Sw[:, :], Sw[:, :], tmpW[:, :], add, [Sw, tmpW], [Sw])
            k.tt(Sw[:, :], Sw[:, :], tmpV[:, :], add, [Sw, tmpV], [Sw])
            k.tt(m3(tmpW[:, :]), m3(Sw[:, :]), bc(rq[:, 0, l, :], 1, 64), mult, [Sw, rq], [tmpW])
            k.red(yq[:, l, :], m3(tmpW[:, :]), add, [tmpW], [yq])
        k.dma("sp", o_wkv_s, Sw[:, :], reads=[Sw])
        k.dma("sp", bass.AP(scr_w_t, 0, [[64, 128], [8192, 4], [1, 64]]), yq[:, :, :], reads=[yq], writes=[sbw])
        k.dma("sp", yw[:, :], scr_w_t.ap(), reads=[sbw], writes=[yw])
        y8 = lambda ap: ap.rearrange("p (h d) -> p h d", h=8)
        k.red(st8[:, 0, :], y8(yw[:, :]), add, [yw], [st8])
        k.tt(ysq[:, :], yw[:, :], yw[:, :], mult, [yw], [ysq])
        k.red(st8[:, 1, :], y8(ysq[:, :]), add, [ysq], [st8])
        k.ts(st8[:, 2, :], st8[:, 0, :], 1.0 / 64, None, mult, None, [st8], [st8])
        k.tt(st8[:, 4, :], st8[:, 2, :], st8[:, 2, :], mult, [st8], [st8])
        k.stt(st8[:, 3, :], st8[:, 1, :], 1.0 / 64, st8[:, 4, :], mult, sub, [st8], [st8])
        k.act(st8[:, 3, :], st8[:, 3, :], AF.Sqrt, [st8, epsc], [st8], bias=epsc[0:64, 1:2])
        k.recip(st8[:, 3, :], st8[:, 3, :], [st8], [st8])
        k.tt(y8(yw[:, :]), y8(yw[:, :]), bc(st8[:, 2, :], 2, 64), sub, [yw, st8], [yw])
        k.tt(y8(yw[:, :]), y8(yw[:, :]), bc(st8[:, 3, :], 2, 64), mult, [yw, st8], [yw])
        k.tt(yw[:, :], yw[:, :], lnwr[:, :], mult, [yw, lnwr], [yw])
        k.tt(yw[:, :], yw[:, :], lnbr[:, :], add, [yw, lnbr], [yw])
        k.tt(ysq[:, :], tokq[:, 0, :], tokq[:, 2, :], mult, [tokq], [ysq])
        k.tt(ysq[:, :], ysq[:, :], rkr[:, :], mult, [ysq, rkr], [ysq])
        k.red(st8[:, 5, :], y8(ysq[:, :]), add, [ysq], [st8])
        k.tt(y8(ysq[:, :]), y8(tokq[:, 3, :]), bc(st8[:, 5, :], 2, 64), mult, [tokq, st8], [ysq])
        k.tt(yw[:, :], yw[:, :], ysq[:, :], add, [yw, ysq], [yw])
        k.tt(yw[:, :], yw[:, :], gtok[:, :], mult, [yw, gtok], [yw])
        if dbg:
            k.dma("sp", dbg_d["yrws"], yw[:, :], reads=[yw])
        k.cp(ynb[:, :], yw[:, :], [yw], [ynb])
        pty = bank_bf(6)[:, 0:256].rearrange("p (c t) -> p c t", c=4)
        for c in range(4):
            k.tr(pty[:, c, :], ynb[:, 128 * c:128 * (c + 1)], identb[0:64, 0:64], [ynb, identb], [pb[6]])
        k.act(yrT[:, :, :], pty[:, :, :], AF.Copy, [pb[6]], [yrT])
        for hf in range(2):
            bk = hf
            for kk in range(4):
                k.mm(bank(bk)[0:64, :], yrT[:, kk, :], Wout2[:, kk, 512 * hf:512 * (hf + 1)], kk == 0, kk == 3,
                     [yrT, Wout2], [pb[bk]])
            k.tt(x_tok[0:64, 16, 512 * hf:512 * (hf + 1)], bank(bk)[0:64, :], x_tok[0:64, 16, 512 * hf:512 * (hf + 1)],
                 add, [pb[bk], xb[16]], [xb[16]])
        k.barrier()

    if dbg:
        for blk in range(17):
            rows = blk_rows(blk)
            k.dma("sp", dbg_d["x1"][blk * 128:blk * 128 + rows, :], x_tok[0:rows, blk, :], reads=[xb[blk]])

    if stop == "C":
        k.finish()
        return nc
    with ExitStack() as es:
        KT = k.sb("KT", [128, 8, 256], BF16, es)
        Vtok = k.sb("Vtok", [128, 2, 1024], BF16, es)
        junk = k.sb("junk", [128, 1024], BF16, es)
        hb2 = [k.sb("hb", [128, 1024], BF16, es) for _ in range(2)]
        onesbf = k.sb("onesbf", [128, 128], BF16, es)
        k.memset(onesbf[:, :], 1.0, [onesbf])
        with ExitStack() as es0:
            Wk, WkB = load_weight("Wk", wk_d.rearrange("(c p) n -> p c n", p=128), 1024,
                                  [(256 * c_, 256 * c_ + 256) for c_ in range(4)], es0)
            Wv, WvB = load_weight("Wv", wv_d.rearrange("(c p) n -> p c n", p=128), 1024, [(0, 512), (512, 1024)], es0)
            memtok = k.sb("memtok", [128, 2, 1024], F32, es0)
            mT = k.sb("mT", [128, 8, 256], BF16, es0)
            mss = k.sb("mss", [128, 2, 4], F32, es0)
            kvst = [k.sb("kvst", [128, 1024], F32, es0) for _ in range(2)]
            row_bc("mem_norm", gbc)
            k.dma("sp", memtok[:, :, :], mem_d.rearrange("(b p) d -> p b d", p=128), writes=[memtok])
            for mb in range(2):
                hb = hb2[mb]
                k.act(junk[:, :], memtok[:, mb, :], AF.Square, [memtok], [junk, mss], accum_out=mss[:, mb, 0:1])
                k.act(mss[:, mb, 1:2], mss[:, mb, 0:1], AF.Sqrt, [mss, epsc], [mss], scale=1.0 / 1024, bias=epsc[:, 0:1])
                k.recip(mss[:, mb, 2:3], mss[:, mb, 1:2], [mss], [mss])
                k.stt(hb[:, :], memtok[:, mb, :], mss[:, mb, 2:3], gbc[:, :], mult, mult, [memtok, mss, gbc], [hb])
                pt = bank_bf(6)[:, :].rearrange("p (c t) -> p c t", c=8)
                for c in range(8):
                    k.tr(pt[:, c, :], hb[:, c * 128:(c + 1) * 128], identb[:, :], [hb, identb], [pb[6]])
                k.act(mT[:, :, mb * 128:(mb + 1) * 128], pt[:, :, :], AF.Copy, [pb[6]], [mT])
            for c in range(8):
                bi_ = c % 2
                for kk in range(8):
                    k.mm(bank(bi_)[:, 0:256], Wk[:, kk, 128 * c:128 * (c + 1)], mT[:, kk, :], kk == 0, kk == 7,
                         WkB(128 * c, 128 * c + 128) + [mT], [pb[bi_]])
                k.act(KT[:, c, :], bank(bi_)[:, 0:256], AF.Copy, [pb[bi_]], [KT])
            ci = 0
            for W_, WB_, od, isv in ((Wk, WkB, o_mk, False), (Wv, WvB, o_mv, True)):
                for mb in range(2):
                    st_ = kvst[ci % 2]
                    ci += 1
                    for hf in range(2):
                        bk = 2 + hf
                        for kk in range(8):
                            k.mm(bank(bk)[:, :], mT[:, kk, mb * 128:(mb + 1) * 128], W_[:, kk, 512 * hf:512 * (hf + 1)],
                                 kk == 0, kk == 7, WB_(512 * hf, 512 * hf + 512) + [mT], [pb[bk]])
                        k.act(st_[:, 512 * hf:512 * (hf + 1)], bank(bk)[:, :], AF.Copy, [pb[bk]], [st_])
                        if isv:
                            k.cp(Vtok[:, mb, 512 * hf:512 * (hf + 1)], st_[:, 512 * hf:512 * (hf + 1)], [st_], [Vtok])
                    k.dma("sp", od[mb * 128:(mb + 1) * 128, :], st_[:, :], reads=[st_])
            k.barrier()
        Wq, WqB = load_weight("Wq", wq_d.rearrange("(c p) n -> p c n", p=128), 1024,
                              [(256 * c_, 256 * c_ + 256) for c_ in range(4)], es)
        Wo, WoB = load_weight("Wo", wo_d.rearrange("(c p) n -> p c n", p=128), 1024, [(0, 512), (512, 1024)], es)
        row_bc("norm_xa", gbc)
        for blk in range(17):
            norm_T(blk, es, (junk, hb2[blk % 2]))
        qT = k.sb("qT", [128, 8, 512], BF16, es)
        oT = k.sb("oT", [128, 8, 512], BF16, es)
        pT = [k.sb("pT", [128, 512], BF16, es) for _ in range(2)]
        rden = k.sb("rden", [128, 512], F32, es)
        NKV = 2
        KTs = [k.sb("KTs", [128, 8, 256], BF16, es) for _ in range(NKV)]
        Kst = [k.sb("Kst", [128, 2, 1024], BF16, es) for _ in range(NKV)]
        Vst = [k.sb("Vst", [128, 2, 1024], BF16, es) for _ in range(NKV)]

        def attend(KT_, V_, qcols, ocols, n, rd):
            for hd in range(4):
                for mb in range(2):
                    bk = 2 + mb
                    for j in range(2):
                        k.mm(bank(bk)[:, 0:n], KT_[:, 2 * hd + j, mb * 128:(mb + 1) * 128], qT[:, 2 * hd + j, qcols],
                             j == 0, j == 1, rd + [qT], [pb[bk]])
                    k.act(pT[mb][:, 0:n], bank(bk)[:, 0:n], AF.Exp, [pb[bk]], [pT[mb]])
                for mb in range(2):
                    k.mm(bank(4)[:, 0:n], onesbf[:, :], pT[mb][:, 0:n], mb == 0, mb == 1, [onesbf, pT[mb]], [pb[4]])
                k.recip(rden[:, 0:n], bank(4)[:, 0:n], [pb[4]], [rden])
                for j in range(2):
                    dc = 2 * hd + j
                    for mb in range(2):
                        k.mm(bank(5)[:, 0:n], V_[:, mb, 128 * dc:128 * (dc + 1)], pT[mb][:, 0:n], mb == 0, mb == 1,
                             rd + [pT[mb]], [pb[5]])
                    k.tt(oT[:, dc, ocols], bank(5)[:, 0:n], rden[:, 0:n], mult, [pb[5], rden], [oT])

        for t in range(5):
            if stop == "D1p" and t == 4:
                break
            if stop == "D0":
                break
            n = 512 if t < 4 else 64
            c0 = t * 512
            nb = 4 if t < 4 else 1
            hbufs = hTb[4 * t:4 * t + nb]
            for c in range(8):
                bi_ = c % 2
                for kk in range(8):
                    k.mm(bank(bi_)[:, 0:n], Wq[:, kk, 128 * c:128 * (c + 1)], hT[:, kk, c0:c0 + n], kk == 0, kk == 7,
                         WqB(128 * c, 128 * c + 128) + hbufs, [pb[bi_]])
                k.act(qT[:, c, 0:n], bank(bi_)[:, 0:n], AF.Copy, [pb[bi_]], [qT], scale=0.0625)
            if t < 4:
                attend(KT, Vtok, slice(0, n), slice(0, n), n, [KT, Vtok])
            else:
                for s_ in range(16):
                    Ks_, Vs_, KTs_ = Kst[s_ % NKV], Vst[s_ % NKV], KTs[s_ % NKV]
                    k.dma("pool", Ks_[:, :, :], ck_d[s_].rearrange("(b p) d -> p b d", p=128), writes=[Ks_])
                    k.dma("pool", Vs_[:, :, :], cv_d[s_].rearrange("(b p) d -> p b d", p=128), writes=[Vs_])
                    for mb in range(2):
                        ptk = bank_bf(6 + mb)[:, :].rearrange("p (c t) -> p c t", c=8)
                        for c in range(8):
                            k.tr(ptk[:, c, :], Ks_[:, mb, 128 * c:128 * (c + 1)], identb[:, :], [Ks_, identb],
                                 [pb[6 + mb]])
                        k.act(KTs_[:, :, mb * 128:(mb + 1) * 128], ptk[:, :, :], AF.Copy, [pb[6 + mb]], [KTs_])
                    sel = slice(s_, 64, 16)
                    for hd in range(4):
                        for mb in range(2):
                            for j in range(2):
                                k.mm(bank(2 + mb)[:, 4 * hd:4 * hd + 4], KTs_[:, 2 * hd + j, mb * 128:(mb + 1) * 128],
                                     qT[:, 2 * hd + j, sel], j == 0, j == 1, [KTs_, qT], [pb[2 + mb]])
                    for mb in range(2):
                        k.act(pT[mb][:, 0:16], bank(2 + mb)[:, 0:16], AF.Exp, [pb[2 + mb]], [pT[mb]])
                    for mb in range(2):
                        k.mm(bank(4)[:, 0:16], onesbf[:, :], pT[mb][:, 0:16], mb == 0, mb == 1, [onesbf, pT[mb]], [pb[4]])
                    k.recip(rden[:, 0:16], bank(4)[:, 0:16], [pb[4]], [rden])
                    for dc in range(8):
                        hd = dc // 2
                        for mb in range(2):
                            k.mm(bank(5)[:, 4 * dc:4 * dc + 4], Vs_[:, mb, 128 * dc:128 * (dc + 1)],
                                 pT[mb][:, 4 * hd:4 * hd + 4], mb == 0, mb == 1, [Vs_, pT[mb]], [pb[5]])
                    k.tt(oT[:, :, sel].rearrange("p (h j) t -> p h j t", j=2),
                         bank(5)[:, 0:32].rearrange("p (h j t) -> p h j t", h=4, j=2),
                         bc(rden[:, 0:16].rearrange("p (h t) -> p h t", h=4), 2, 2), mult, [pb[5], rden], [oT])
            for bi in range(nb):
                blk = 4 * t + bi
                rows = blk_rows(blk)
                cs = slice(bi * 128, bi * 128 + rows)
                for kk in range(8):
                    for hf in range(2):
                        k.mm(bank(hf)[0:rows, :], oT[:, kk, cs], Wo[:, kk, 512 * hf:512 * (hf + 1)], kk == 0, kk == 7,
                             [oT] + WoB(512 * hf, 512 * hf + 512), [pb[hf]])
                for hf in range(2):
                    k.tt(x_tok[0:rows, blk, 512 * hf:512 * (hf + 1)], bank(hf)[0:rows, :],
                         x_tok[0:rows, blk, 512 * hf:512 * (hf + 1)], add, [pb[hf], xb[blk]], [xb[blk]])
        k.barrier()

    if dbg:
        for blk in range(17):
            rows = blk_rows(blk)
            k.dma("sp", dbg_d["x2"][blk * 128:blk * 128 + rows, :], x_tok[0:rows, blk, :], reads=[xb[blk]])

    if stop in ("D", "D0", "D1p"):
        k.finish()
        return nc
    with ExitStack() as es:
        junk = k.sb("junk", [128, 1024], BF16, es)
        hb2 = [k.sb("hb", [128, 1024], BF16, es) for _ in range(2)]
        row_bc("norm_ffn", gbc)
        for blk in range(17):
            norm_T(blk, es, (junk, hb2[blk % 2]))
        k.barrier()
    with ExitStack() as es:
        G = 3
        Wup = [k.sb("Wup", [128, 8, 2 * G * 128], BF16, es) for _ in range(2)]
        Wdn = [k.sb("Wdn", [128, G, 1024], BF16, es) for _ in range(2)]
        wup_v = wup_d.rearrange("(c p) n -> p c n", p=128)
        wdn_v = wdn_d.rearrange("(f p) n -> p f n", p=128)
        pre = [k.sb("pre", [128, 2050], F32, es) for _ in range(2)]
        accf = [k.sb("accf", [128, 2048], F32, es) for _ in range(2)]
        sgt = accf[0]
        actb = k.sb("actb", [128, G, 2112], BF16, es)
        pres = [k.sb("pres", [128, 96], F32, es) for _ in range(2)]
        carF = k.sb("carF", [128, 44, 2], F32, es)
        carS = k.sb("carS", [128, 44, 2, 16], F32, es)
        k.dma("sp", carS[:, :, :, :], st_ffn_d, writes=[carS])
        for pr in pre:
            k.memset(pr[:, 0:2], 0.0, [pr])
        groups = [(f0, min(G, 22 - f0)) for f0 in range(0, 22, G)]
        pcount = 0
        dcount = 0
        hb_all = hTb[0:16]
        for gi, (f0, gn) in enumerate(groups):
            Wu, Wd = Wup[gi % 2], Wdn[gi % 2]
            k.dma("pool", Wu[:, :, 0:gn * 128], wup_v[:, :, f0 * 128:(f0 + gn) * 128], writes=[Wu])
            k.dma("pool", Wu[:, :, G * 128:G * 128 + gn * 128], wup_v[:, :, 2816 + f0 * 128:2816 + (f0 + gn) * 128],
                  writes=[Wu])
            k.dma("pool", Wd[:, 0:gn, :], wdn_v[:, f0:f0 + gn, :], writes=[Wd])
            for i in range(gn):
                for half in range(2):
                    fc = (f0 + i) + 22 * half
                    wc = half * G * 128 + i * 128
                    pr = pre[pcount % 2]
                    pcount += 1
                    for kk in range(8):
                        for tt_ in range(4):
                            k.mm(bank(tt_)[:, :], Wu[:, kk, wc:wc + 128], hT[:, kk, tt_ * 512:(tt_ + 1) * 512], kk == 0,
                                 kk == 7, [Wu] + hb_all, [pb[tt_]])
                    for h2 in range(2):
                        k.act(pr[:, 2 + 1024 * h2:2 + 1024 * (h2 + 1)], psA[h2][:, :, :].rearrange("p a b -> p (a b)"),
                              AF.Copy, [pb[2 * h2], pb[2 * h2 + 1]], [pr])
                    ac = accf[half]
                    k.act(ac[:, :], pr[:, 0:2048], AF.Identity, [pr, cols], [ac], scale=col("ffn_cw", 3 * fc),
                          bias=col("ffn_cb", fc))
                    for j in range(1, 3):
                        k.stt(ac[:, :], pr[:, j:j + 2048], col("ffn_cw", 3 * fc + j), ac[:, :], mult, add,
                              [pr, cols, ac], [ac])
                    k.cp(carF[:, fc, :], pr[:, 2048:2050], [pr], [carF], eng="pool")
                k.act(sgt[:, :], accf[0][:, :], AF.Silu, [accf[0]], [accf[0]])
                k.tt(actb[:, i, 0:2048], sgt[:, :], accf[1][:, :], mult, [accf[0], accf[1]], [actb])
                n, S, c0 = 64, 16, 2048
                for half in range(2):
                    fc = (f0 + i) + 22 * half
                    wc = half * G * 128 + i * 128
                    prs = pres[half]
                    bk = 6 + half
                    for kk in range(8):
                        k.mm(bank(bk)[:, 0:n], Wu[:, kk, wc:wc + 128], hT[:, kk, c0:c0 + n], kk == 0, kk == 7,
                             [Wu, hTb[16]], [pb[bk]])
                    k.act(prs[:, 32:96], bank(bk)[:, 0:n], AF.Copy, [pb[bk]], [prs])
                    k.cp(prs[:, 0:32], carS[:, fc, :, :].rearrange("p j s -> p (j s)"), [carS], [prs], eng="pool")
                    ac = accf[half]
                    k.act(ac[:, 0:n], prs[:, 0:n], AF.Identity, [prs, cols], [ac], scale=col("ffn_cw", 3 * fc),
                          bias=col("ffn_cb", fc))
                    for j in range(1, 3):
                        k.stt(ac[:, 0:n], prs[:, j * S:j * S + n], col("ffn_cw", 3 * fc + j), ac[:, 0:n], mult, add,
                              [prs, cols, ac], [ac])
                    k.cp(carS[:, fc, :, :].rearrange("p j s -> p (j s)"), prs[:, n:n + 32], [prs], [carS], eng="pool")
                k.act(sgt[:, 0:n], accf[0][:, 0:n], AF.Silu, [accf[0]], [accf[0]])
                k.tt(actb[:, i, 2048:2112], sgt[:, 0:n], accf[1][:, 0:n], mult, [accf[0], accf[1]], [actb])
            for blk in range(17):
                rows = blk_rows(blk)
                cs = slice(blk * 128, blk * 128 + rows)
                bb = 4 + 2 * (dcount % 2)
                dcount += 1
                for i in range(gn):
                    for hf in range(2):
                        k.mm(bank(bb + hf)[0:rows, :], actb[:, i, cs], Wd[:, i, 512 * hf:512 * (hf + 1)], i == 0,
                             i == gn - 1, [actb, Wd], [pb[bb + hf]])
                for hf in range(2):
                    k.tt(x_tok[0:rows, blk, 512 * hf:512 * (hf + 1)], bank(bb + hf)[0:rows, :],
                         x_tok[0:rows, blk, 512 * hf:512 * (hf + 1)], add, [pb[bb + hf], xb[blk]], [xb[blk]])
        k.dma("sp", o_ffn_p, carF[:, :, :], reads=[carF])
        k.dma("sp", o_ffn_s, carS[:, :, :, :], reads=[carS])
        k.barrier()

    if stop == "E":
        k.finish()
        return nc
    with ExitStack() as es:
        junk = k.sb("junk", [128, 1024], BF16, es)
        yo = [k.sb("yo", [128, 1024], F32, es) for _ in range(2)]
        row_bc("final_norm", gbc)
        for blk in range(17):
            rows = blk_rows(blk)
            xin = x_tok[0:rows, blk, :]
            ss = ssx[0:rows, blk, :]
            y_ = yo[blk % 2]
            k.act(junk[0:rows, :], xin, AF.Square, [xb[blk]], [junk, ssb[blk]], accum_out=ss[:, 0:1])
            k.act(ss[:, 1:2], ss[:, 0:1], AF.Sqrt, [ssb[blk], epsc], [ssb[blk]], scale=1.0 / 1024, bias=epsc[0:rows, 0:1])
            k.recip(ss[:, 2:3], ss[:, 1:2], [ssb[blk]], [ssb[blk]])
            k.stt(y_[0:rows, :], xin, ss[:, 2:3], gbc[0:rows, :], mult, mult, [xb[blk], ssb[blk], gbc], [y_])
            dst = y_p_d[blk * 128:(blk + 1) * 128, :] if blk < 16 else y_s_d
            k.dma("sp", dst, y_[0:rows, :], reads=[y_])
        k.barrier()
    k.finish()
    return nc


def _consts():
    c = np.zeros((128, NCONST), np.float32)
    c2 = np.zeros((128, NCONST2), np.float32)
    i = np.arange(128)
    o, s = KO["ident"]
    c[:, o:o + s] = np.eye(128)
    o, s = KO["tri"]
    c[:, o:o + s] = (i[:, None] <= i[None, :])
    o, s = KO["lm"]
    c[:, o:o + s] = (i[:, None] > i[None, :])
    o, s = KO["ones"]
    c[:, o:o + s] = 1.0
    s64 = np.arange(64)
    su = (s64[:, None] < s64[None, :]).astype(np.float32)
    iu = (s64[:, None] <= s64[None, :]).astype(np.float32)
    sl = (s64[:, None] > s64[None, :]).astype(np.float32)
    def bd(m_):
        z_ = np.zeros((128, 128), np.float32)
        z_[:64, :64] = m_
        z_[64:, 64:] = m_
        return z_
    o, s = KO2["wmask"]
    c2[:, o:o + s] = np.concatenate([bd(su), bd(iu), bd(su), bd(iu), bd(sl)], axis=1)
    o, s = KO2["i64"]
    c2[:, o:o + s] = np.concatenate([np.eye(64), np.eye(64)], axis=0)
    o, s = KO2["bones"]
    bo = np.zeros((128, 128), np.float32)
    bo[:64, :64] = 1
    bo[64:, 64:] = 1
    c2[:, o:o + s] = bo
    o, s = KO2["scanm"]
    sm = np.ones(512, np.float32)
    sm[::64] = 0
    c2[:, o:o + s] = sm[None, :]
    return c, c2


def _fm(v, nch):
    return np.ascontiguousarray(np.asarray(v, np.float32).reshape(nch, 128).T)


def _prep_shared(inp):
    sh = {}
    f = lambda a: np.ascontiguousarray(np.asarray(a, np.float32))
    sh["w_in"] = f(inp["w_in"][0])
    sh["w_out"] = f(inp["w_out"][0])
    sh["wq"] = f(inp["xa_w_q"][0])
    sh["wk"] = f(inp["xa_w_k"][0])
    sh["wv"] = f(inp["xa_w_v"][0])
    sh["wo"] = f(inp["xa_w_o"][0])
    sh["wup"] = f(inp["ffn_w_up"][0])
    sh["wdn"] = f(inp["ffn_w_down"][0])
    sh["w2a2"] = f(np.concatenate([inp["rwkv_w2"][0], inp["rwkv_a2"][0]], axis=0))
    sh["g2"] = f(inp["rwkv_g2"][0])
    rows = np.zeros((1, NROW), np.float32)
    src = {"norm_mix": inp["norm_mix_w"][0], "norm_xa": inp["norm_xa_w"][0], "mem_norm": inp["mem_norm_w"][0],
           "norm_ffn": inp["norm_ffn_w"][0], "final_norm": inp["final_norm_w"], "ssd_norm": inp["ssd_norm_w"][0],
           "dt_bias": inp["ssd_dt_bias"][0], "a_log": inp["ssd_a_log"][0], "ssd_d": inp["ssd_d"][0],
           "ln_w": inp["rwkv_ln_w"][0], "ln_b": inp["rwkv_ln_b"][0], "r_k_row": inp["rwkv_r_k"][0]}
    for nme, (o, s) in RO.items():
        rows[0, o:o + s] = np.asarray(src[nme], np.float32).reshape(-1)
    sh["rows"] = rows
    cols = np.zeros((128, NCOL), np.float32)

    def put(nme, arr):
        o, s = CO[nme]
        cols[:, o:o + s] = arr.reshape(128, s)

    cw = np.asarray(inp["ssd_conv_w"][0], np.float32)
    put("ssd_cw", np.ascontiguousarray(cw.T.reshape(8, 128, 4).transpose(1, 0, 2)))
    put("ssd_cb", _fm(inp["ssd_conv_b"][0], 8))
    put("mu", _fm(inp["rwkv_mu"][0], 14))
    put("w0", _fm(inp["rwkv_w0"][0], 4))
    put("a0", _fm(inp["rwkv_a0"][0], 4))
    put("k_k", _fm(inp["rwkv_k_k"][0], 4))
    put("k_a", _fm(inp["rwkv_k_a"][0], 4))
    put("r_k", _fm(np.asarray(inp["rwkv_r_k"][0]).reshape(-1), 4))
    fw = np.asarray(inp["ffn_conv_w"][0], np.float32)
    put("ffn_cw", np.ascontiguousarray(fw.T.reshape(44, 128, 3).transpose(1, 0, 2)))
    put("ffn_cb", _fm(inp["ffn_conv_b"][0], 44))
    sh["cols"] = cols
    sh["consts"], sh["consts2"] = _consts()
    return sh


def _prep_core(inp, c):
    f = lambda a: np.ascontiguousarray(np.asarray(a, np.float32))
    sl = slice(16 * c, 16 * c + 16)
    m = {}
    m["xp"] = f(inp["x_prompt"][c])
    m["xs"] = f(np.asarray(inp["x_sample"][sl]).transpose(1, 0, 2).reshape(64, 1024))
    m["mem"] = f(inp["mem_prompt"][c])
    sc = np.asarray(inp["state_ssm_conv"][0, sl])
    m["st_conv"] = f(sc.transpose(2, 1, 0).reshape(8, 128, 3, 16).transpose(1, 0, 2, 3))
    m["st_ssm"] = f(np.asarray(inp["state_ssm"][0, sl]).reshape(128, 8192))
    ss = np.asarray(inp["state_shift"][0, sl])
    m["st_shift"] = f(ss.T.reshape(14, 128, 16).transpose(1, 0, 2))
    m["st_wkv"] = f(np.asarray(inp["state_wkv"][0, sl]).reshape(128, 4096))
    sf = np.asarray(inp["state_ffn_conv"][0, sl])
    m["st_ffn"] = f(sf.transpose(2, 1, 0).reshape(44, 128, 2, 16).transpose(1, 0, 2, 3))
    m["ck"] = f(np.asarray(inp["cache_mem_k"][0, sl]).reshape(16, 256, 1024))
    m["cv"] = f(np.asarray(inp["cache_mem_v"][0, sl]).reshape(16, 256, 1024))
    return m


_NC_CACHE = {}


def _get_nc(stop="all", dbg=False):
    key = (stop, dbg)
    if key not in _NC_CACHE:
        _NC_CACHE[key] = build(stop, dbg)
    return _NC_CACHE[key]


def run_raw(inp, stop="all", dbg=False):
    nc = _get_nc(stop, dbg)
    sh = _prep_shared(inp)
    in_maps = []
    for c in range(NCORES):
        m = dict(sh)
        m.update(_prep_core(inp, c))
        in_maps.append(m)
    res = run_bass_kernel_spmd(nc, in_maps, core_ids=list(range(NCORES)))
    return res.results


def kernel(**inp):
    rs = run_raw(inp)
    f32 = np.float32
    y_p = np.stack([r["y_p"] for r in rs]).astype(f32)
    y_s = np.concatenate([r["y_s"].reshape(4, 16, 1024).transpose(1, 0, 2) for r in rs]).astype(f32)
    conv_p = np.stack([r["o_conv_p"].transpose(2, 1, 0).reshape(3, 1024) for r in rs])[None]
    conv_s = np.concatenate([r["o_conv_s"].transpose(3, 2, 1, 0).reshape(16, 3, 1024) for r in rs])[None]
    ssm_p = np.stack([r["o_ssm_p"].reshape(8, 64, 128) for r in rs])[None]
    ssm_s = np.concatenate([r["o_ssm_s"].reshape(16, 8, 64, 128) for r in rs])[None]
    shift_p = np.stack([r["o_shift_p"].T.reshape(1792) for r in rs])[None]
    shift_s = np.concatenate([r["o_shift_s"].transpose(2, 1, 0).reshape(16, 1792) for r in rs])[None]
    wkv_p = np.stack([r["o_wkv_p"].reshape(2, 64, 4, 64).transpose(2, 0, 3, 1).reshape(8, 64, 64) for r in rs])[None]
    wkv_s = np.concatenate([r["o_wkv_s"].reshape(16, 8, 64, 64) for r in rs])[None]
    ffn_p = np.stack([r["o_ffn_p"].transpose(2, 1, 0).reshape(2, 5632) for r in rs])[None]
    ffn_s = np.concatenate([r["o_ffn_s"].transpose(3, 2, 1, 0).reshape(16, 2, 5632) for r in rs])[None]
    mk = np.stack([r["o_mk"].reshape(256, 4, 256) for r in rs])[None]
    mv = np.stack([r["o_mv"].reshape(256, 4, 256) for r in rs])[None]
    outs = (y_p, y_s, conv_p, conv_s, ssm_p, ssm_s, shift_p, shift_s, wkv_p, wkv_s, ffn_p, ffn_s, mk, mv)
    return tuple(np.ascontiguousarray(o, dtype=f32) for o in outs)
```

```python
import os
import numpy as np
from contextlib import ExitStack
import concourse.bass as bass
import concourse.mybir as mybir
from concourse.bass_utils import run_bass_kernel_spmd

F32 = mybir.dt.float32
BF16 = mybir.dt.bfloat16
AF = mybir.ActivationFunctionType
ALU = mybir.AluOpType
AX = mybir.AxisListType

NCORES = 8
EPS = 1e-6
GN_EPS = 64e-5
C0 = float(np.exp(-0.5))
NDMA = 40

ROWS = [("norm_mix", 1024), ("norm_xa", 1024), ("mem_norm", 1024), ("norm_ffn", 1024), ("final_norm", 1024),
        ("ssd_norm", 512), ("dt_bias", 8), ("a_log", 8), ("ssd_d", 8), ("ln_w", 512), ("ln_b", 512), ("r_k_row", 512)]
COLS = [("ssd_cw", 32), ("ssd_cb", 8), ("mu", 14), ("w0", 4), ("a0", 4), ("k_k", 4), ("k_a", 4), ("r_k", 4),
        ("ffn_cw", 132), ("ffn_cb", 44)]
CONSTS = [("ident", 128), ("tri", 128), ("lm", 128), ("ones", 128)]
CONSTS2 = [("wmask", 640), ("i64", 64), ("bones", 128), ("scanm", 512)]


def _offs(spec):
    o = {}
    p = 0
    for n, s in spec:
        o[n] = (p, s)
        p += s
    return o, p


RO, NROW = _offs(ROWS)
CO, NCOL = _offs(COLS)
KO, NCONST = _offs(CONSTS)
KO2, NCONST2 = _offs(CONSTS2)


class Buf:
    __slots__ = ("name", "w", "r", "psum")

    def __init__(self, name="", psum=False):
        self.name = name
        self.w = None
        self.r = {}
        self.psum = psum


class TT:
    def __init__(self, t, name):
        self.t = t
        self.b = Buf(name)

    def __getitem__(self, idx):
        return self.t[idx]


def _b(x):
    return x if isinstance(x, Buf) else x.b


class View:
    def __init__(self, ap, buf):
        self.ap = ap
        self.b = buf

    def __getitem__(self, idx):
        return self.ap[idx]


class K:
    def __init__(self, nc):
        self.nc = nc
        self.engs = ["pe", "act", "dve", "pool", "sp"]
        self.sem = {e: nc.alloc_semaphore(name=f"s_{e}") for e in self.engs}
        self.cnt = {e: 0 for e in self.engs}
        self.waited = {e: {} for e in self.engs}
        self.prog = {e: [] for e in self.engs}
        self.dsem = [nc.alloc_semaphore(name=f"d{i}") for i in range(NDMA)]
        self.dval = [0] * NDMA
        self.drr = 0
        self.uid = 0
        self.seq = {e: 0 for e in self.LAZY}
        self.marks = {e: [] for e in self.LAZY}
        self.entry = {e: {} for e in self.LAZY}

    def _semh(self, key):
        return self.sem[key] if isinstance(key, str) else self.dsem[key]

    def _deps(self, e, reads, writes):
        need = {}

        def add(ev):
            if ev is None:
                return
            kk, v = ev
            if need.get(kk, 0) < v:
                need[kk] = v

        for b in reads:
            b = _b(b)
            add(b.w)
            if b.psum:
                for kk, v in b.r.items():
                    if kk != e:
                        add((kk, v))
        for b in writes:
            b = _b(b)
            add(b.w)
            for kk, v in b.r.items():
                add((kk, v))
        waits = []
        wd = self.waited[e]
        for kk, v in need.items():
            if kk == "pe" and e == "pe":
                continue
            if kk in self.LAZY:
                v = self.resolve(kk, v)
            if wd.get(kk, 0) < v:
                wd[kk] = v
                waits.append((self._semh(kk), v))
        return waits

    def _mark(self, ev, reads, writes):
        kk, v = ev
        for b in reads:
            b = _b(b)
            if b.r.get(kk, 0) < v:
                b.r[kk] = v
        for b in writes:
            b = _b(b)
            b.w = ev
            b.r = {}

    LAZY = ("pe",)

    def resolve(self, e, seq):
        import bisect
        marks = self.marks[e]
        i = bisect.bisect_left(marks, seq)
        if i < len(marks):
            return i + 1
        idx = self.entry[e][seq]
        w, fn, sem, inc = self.prog[e][idx]
        assert inc == 0 and fn is not None
        self.prog[e][idx] = (w, fn, sem, 1)
        marks.append(seq)
        self.cnt[e] = len(marks)
        return len(marks)

    def op(self, e, fn, reads=(), writes=()):
        waits = self._deps(e, reads, writes)
        if e in self.LAZY:
            self.seq[e] += 1
            ev = (e, self.seq[e])
            self.entry[e][self.seq[e]] = len(self.prog[e])
            self.prog[e].append((waits, fn, self.sem[e], 0))
        else:
            self.cnt[e] += 1
            ev = (e, self.cnt[e])
            self.prog[e].append((waits, fn, self.sem[e], 1))
        self._mark(ev, reads, writes)
        return ev

    def dma(self, q, out, in_, reads=(), writes=(), **kw):
        waits = self._deps(q, reads, writes)
        i = self.drr
        self.drr = (self.drr + 1) % NDMA
        if self.dval[i] > 0:
            wd = self.waited[q]
            if wd.get(i, 0) < self.dval[i]:
                wd[i] = self.dval[i]
                waits.append((self.dsem[i], self.dval[i]))
        self.dval[i] += 16
        ev = (i, self.dval[i])
        self.prog[q].append((waits, lambda eng: eng.dma_start(out=out, in_=in_, **kw), self.dsem[i], 16))
        self._mark(ev, reads, writes)
        return ev

    def barrier(self):
        for e in self.LAZY:
            if self.seq[e] > 0:
                self.resolve(e, self.seq[e])
        for e in self.engs:
            waits = []
            wd = self.waited[e]
            for e2 in self.engs:
                if self.cnt[e2] > 0 and wd.get(e2, 0) < self.cnt[e2] and e2 != e:
                    wd[e2] = self.cnt[e2]
                    waits.append((self.sem[e2], self.cnt[e2]))
            for i in range(NDMA):
                if self.dval[i] > 0 and wd.get(i, 0) < self.dval[i]:
                    wd[i] = self.dval[i]
                    waits.append((self.dsem[i], self.dval[i]))
            if waits:
                self.prog[e].append((waits, None, None, 0))

    def finish(self):
        self.barrier()
        print("instr counts", {e: (len(self.prog[e]), sum(len(w) for w, _, _, _ in self.prog[e])) for e in self.engs})
        nc = self.nc
        prog = self.prog

        def replay(name, eng):
            for waits, fn, sem, inc in prog[name]:
                for s, v in waits:
                    eng.wait_ge(s, v)
                if fn is not None:
                    ins = fn(eng)
                    if inc:
                        ins.then_inc(sem, inc)

        with nc.Block() as block:
            @block.tensor
            def _(eng):
                replay("pe", eng)

            @block.scalar
            def _(eng):
                replay("act", eng)

            @block.vector
            def _(eng):
                replay("dve", eng)

            @block.gpsimd
            def _(eng):
                replay("pool", eng)

            @block.sync
            def _(eng):
                replay("sp", eng)

    def sb(self, name, shape, dtype, es=None):
        self.uid += 1
        nm = f"{name}_{self.uid}"
        if es is None:
            t = self.nc.alloc_sbuf_tensor(nm, list(shape), dtype)
        else:
            t = es.enter_context(self.nc.sbuf_tensor(nm, list(shape), dtype))
        return TT(t, nm)

    def mm(self, out, lhsT, rhs, start, stop, reads, writes):
        return self.op("pe", lambda e: e.matmul(out, lhsT, rhs, start=start, stop=stop), reads, writes)

    def tr(self, out, in_, ident, reads, writes):
        return self.op("pe", lambda e: e.transpose(out, in_, ident), reads, writes)

    def act(self, out, in_, func, reads, writes, scale=1.0, bias=0.0, accum_out=None):
        if accum_out is None:
            return self.op("act", lambda e: e.activation(out, in_, func, bias=bias, scale=scale), reads, writes)
        return self.op("act", lambda e: e.activation(out, in_, func, bias=bias, scale=scale, accum_out=accum_out),
                       reads, writes)

    def tt(self, out, in0, in1, op, reads, writes, eng="dve"):
        return self.op(eng, lambda e: e.tensor_tensor(out, in0, in1, op), reads, writes)

    def ts(self, out, in0, s1, s2, op0, op1, reads, writes, eng="dve"):
        if op1 is None:
            return self.op(eng, lambda e: e.tensor_scalar(out, in0, s1, None, op0), reads, writes)
        return self.op(eng, lambda e: e.tensor_scalar(out, in0, s1, s2, op0, op1), reads, writes)

    def stt(self, out, in0, scalar, in1, op0, op1, reads, writes):
        return self.op("dve", lambda e: e.scalar_tensor_tensor(out, in0, scalar, in1, op0, op1), reads, writes)

    def cp(self, out, in_, reads, writes, eng="dve"):
        return self.op(eng, lambda e: e.tensor_copy(out, in_), reads, writes)

    def recip(self, out, in_, reads, writes):
        return self.op("dve", lambda e: e.reciprocal(out, in_), reads, writes)

    def red(self, out, in_, op, reads, writes, negate=False):
        return self.op("dve", lambda e: e.tensor_reduce(out, in_, AX.X, op, negate=negate), reads, writes)

    def carve(self, parent, pieces):
        out = []
        off = 0
        for nbytes, dt_ in pieces:
            ap = parent.t[:, off // 4:(off + nbytes) // 4]
            if dt_ == BF16:
                ap = ap.bitcast(BF16)
            b = Buf(parent.b.name + "_v")
            b.w = parent.b.w
            b.r = dict(parent.b.r)
            out.append(View(ap, b))
            off += nbytes
        assert off <= 2048
        return out

    def merge(self, parent, views):
        pr = parent.b.r
        for v in views:
            evs = list(v.b.r.items())
            if v.b.w is not None:
                evs.append(v.b.w)
            for kk, val in evs:
                if pr.get(kk, 0) < val:
                    pr[kk] = val

    def scan(self, out, d0, d1, init, op0, op1, reads, writes):
        return self.op("dve", lambda e: e.tensor_tensor_scan(out, d0, d1, init, op0, op1), reads, writes)

    def memset(self, ap, val, writes, eng="dve"):
        return self.op(eng, lambda e: e.memset(ap, val), (), writes)


def bc(ap, axis, n):
    a = ap.unsqueeze(axis)
    shp = list(a.shape)
    shp[axis] = n
    return a.broadcast_to(shp)


def build(stop="all", dbg=False):
    nc = bass.Bass("TRN2", target_bir_lowering=False)
    k = K(nc)
    DBGU = int(os.environ.get("DBGU", "9"))
    skipBC = stop.startswith("x")
    stop = stop.lstrip("x")
    NTB = 0 if skipBC else 4
    mult, add, sub = ALU.mult, ALU.add, ALU.subtract

    def din(name, shape):
        return nc.dram_tensor(name, list(shape), F32, kind="ExternalInput").ap()

    def dout(name, shape):
        return nc.dram_tensor(name, list(shape), F32, kind="ExternalOutput").ap()

    xp_d = din("xp", [2048, 1024])
    xs_d = din("xs", [64, 1024])
    mem_d = din("mem", [256, 1024])
    st_conv_d = din("st_conv", [128, 8, 3, 16])
    st_ssm_d = din("st_ssm", [128, 8192])
    st_shift_d = din("st_shift", [128, 14, 16])
    st_wkv_d = din("st_wkv", [128, 4096])
    st_ffn_d = din("st_ffn", [128, 44, 2, 16])
    ck_d = din("ck", [16, 256, 1024])
    cv_d = din("cv", [16, 256, 1024])
    w_in_d = din("w_in", [1024, 3336])
    w_out_d = din("w_out", [1024, 1024])
    wq_d = din("wq", [1024, 1024])
    wk_d = din("wk", [1024, 1024])
    wv_d = din("wv", [1024, 1024])
    wo_d = din("wo", [1024, 1024])
    wup_d = din("wup", [1024, 5632])
    wdn_d = din("wdn", [2816, 1024])
    w2a2_d = din("w2a2", [128, 512])
    g2_d = din("g2", [128, 512])
    rows_d = din("rows", [1, NROW])
    cols_d = din("cols", [128, NCOL])
    consts_d = din("consts", [128, NCONST])
    consts2_d = din("consts2", [128, NCONST2])

    y_p_d = dout("y_p", [2048, 1024])
    y_s_d = dout("y_s", [64, 1024])
    o_conv_p = dout("o_conv_p", [128, 8, 3])
    o_conv_s = dout("o_conv_s", [128, 8, 3, 16])
    o_ssm_p = dout("o_ssm_p", [512, 128])
    o_ssm_s = dout("o_ssm_s", [128, 8192])
    o_shift_p = dout("o_shift_p", [128, 14])
    o_shift_s = dout("o_shift_s", [128, 14, 16])
    o_wkv_p = dout("o_wkv_p", [128, 4, 64])
    o_wkv_s = dout("o_wkv_s", [128, 4096])
    o_ffn_p = dout("o_ffn_p", [128, 44, 2])
    o_ffn_s = dout("o_ffn_s", [128, 44, 2, 16])
    o_mk = dout("o_mk", [256, 1024])
    o_mv = dout("o_mv", [256, 1024])
    dbg_d = {}
    if dbg:
        dbg_d["yssd"] = dout("dbg_yssd", [2112, 512])
        dbg_d["yrwT"] = dout("dbg_yrwT", [128, 4, 2112])
        dbg_d["x1"] = dout("dbg_x1", [2112, 1024])
        dbg_d["x2"] = dout("dbg_x2", [2112, 1024])
        dbg_d["yrws"] = dout("dbg_yrws", [64, 512])

    x_tok = k.sb("x_tok", [128, 17, 1024], F32)
    xb = [Buf(f"x{b}") for b in range(17)]
    hT = k.sb("hT", [128, 8, 2112], BF16)
    hTb = [Buf(f"hT{b}") for b in range(17)]
    consts = k.sb("consts", [128, NCONST], F32)
    cols = k.sb("cols", [128, NCOL], F32)
    identb = k.sb("identb", [128, 128], BF16)
    gbc = k.sb("gbc", [128, 1024], F32)
    ssx = k.sb("ssx", [128, 17, 4], F32)
    ssb = [Buf(f"ss{b}") for b in range(17)]

    def cst(name, rows=slice(0, 128)):
        o, s = KO[name]
        return consts[rows, o:o + s]

    def col(name, j0=0, j1=None):
        o, s = CO[name]
        if j1 is None:
            j1 = j0 + 1
        return cols[:, o + j0:o + j1]

    def row_bc(name, tt_, nrows=128):
        o, s = RO[name]
        src = rows_d[0:1, o:o + s].partition_broadcast(nrows)
        k.dma("sp", tt_.t[0:nrows, 0:s].unsqueeze(1), src, writes=[tt_])

    psA = [nc.alloc_psum_tensor(f"ps{i}", [128, 2, 512], F32) for i in range(4)]
    pb = [Buf(f"bank{i}", psum=True) for i in range(8)]

    def bank(i):
        return psA[i // 2][:, i % 2, :]

    def bank_bf(i):
        return psA[i // 2][:, i % 2, :].bitcast(BF16)

    k.dma("sp", consts[:, :], consts_d, writes=[consts])
    k.dma("sp", cols[:, :], cols_d, writes=[cols])
    k.cp(identb[:, :], cst("ident"), [consts], [identb])
    ident = cst("ident")

    def blk_rows(blk):
        return 128 if blk < 16 else 64

    def load_weight(name, src_view, ncols, pieces, es_, nk=8):
        t_ = k.sb(name, [128, nk, ncols], BF16, es_)
        bl = []
        for (c0_, c1_) in pieces:
            b_ = Buf(f"{name}_{c0_}")
            k.dma("pool", t_.t[:, :, c0_:c1_], src_view[:, :, c0_:c1_], writes=[b_])
            bl.append((c0_, c1_, b_))

        def rb(c0_, c1_):
            r_ = [b_ for (a0, a1, b_) in bl if a0 < c1_ and c0_ < a1]
            assert r_
            return r_
        return t_, rb

    def norm_T(blk, es_tmp, tmp):
        rows = blk_rows(blk)
        junk, hb = tmp
        xin = x_tok[0:rows, blk, :]
        ss = ssx[0:rows, blk, :]
        k.act(junk[0:rows, :], xin, AF.Square, [xb[blk]], [junk, ssb[blk]], accum_out=ss[:, 0:1])
        k.act(ss[:, 1:2], ss[:, 0:1], AF.Sqrt, [ssb[blk], epsc], [ssb[blk]], scale=1.0 / 1024, bias=epsc[0:rows, 0:1])
        k.recip(ss[:, 2:3], ss[:, 1:2], [ssb[blk]], [ssb[blk]])
        k.stt(hb[0:rows, :], xin, ss[:, 2:3], gbc[0:rows, :], mult, mult, [xb[blk], ssb[blk], gbc], [hb])
        pt = bank_bf(6)[:, :].rearrange("p (c t) -> p c t", c=8)
        for c in range(8):
            k.tr(pt[:, c, 0:rows], hb[0:rows, c * 128:(c + 1) * 128], identb[0:rows, 0:rows], [hb, identb], [pb[6]])
        k.act(hT[:, :, blk * 128:blk * 128 + rows], pt[:, :, 0:rows], AF.Copy, [pb[6]], [hTb[blk]])

    epsc = k.sb("epsc", [128, 2], F32)
    k.memset(epsc[:, 0:1], EPS, [epsc])
    k.memset(epsc[:, 1:2], GN_EPS, [epsc])

    with ExitStack() as es:
        junk = k.sb("junk", [128, 1024], BF16, es)
        hb2 = [k.sb("hb", [128, 1024], BF16, es) for _ in range(2)]
        row_bc("norm_mix", gbc)
        for blk in range(17):
            rows = blk_rows(blk)
            src = xp_d[blk * 128:(blk + 1) * 128, :] if blk < 16 else xs_d
            k.dma("sp", x_tok[0:rows, blk, :], src, writes=[xb[blk]])
        for blk in range(17):
            norm_T(blk, es, (junk, hb2[blk % 2]))
        k.barrier()

    with ExitStack() as es:
        w_in_v = w_in_d.rearrange("(c p) n -> p c n", p=128)
        Wis, WisB = load_weight("Wis", w_in_v[:, :, 0:1544], 1544,
                                [(512 + 128 * c_, 640 + 128 * c_) for c_ in range(8)] + [(0, 512), (1536, 1544)], es)
        Wout = k.sb("Wout", [128, 4, 1024], BF16, es)
        w_out_v = w_out_d.rearrange("(c p) n -> p c n", p=128)
        k.dma("pool", Wout[:, :, :], w_out_v[:, 0:4, :], writes=[Wout])
        sm = k.sb("smallrow", [128, 32], F32, es)
        nwbc = k.sb("nwbc", [128, 512], F32, es)
        for nm, o in (("dt_bias", 0), ("a_log", 8), ("ssd_d", 16)):
            ro, rs = RO[nm]
            k.dma("sp", sm.t[:, o:o + 8].unsqueeze(1), rows_d[0:1, ro:ro + 8].partition_broadcast(128), writes=[sm])
        row_bc("ssd_norm", nwbc)
        k.act(sm[:, 24:32], sm[:, 8:16], AF.Exp, [sm], [sm])
        k.ts(sm[:, 24:32], sm[:, 24:32], -1.0, None, mult, None, [sm], [sm])
        xpre = [k.sb(f"xpre{c}", [128, 515], F32, es) for c in range(8)]
        xact = [k.sb(f"xact{c}", [128, 512], BF16, es) for c in range(8)]
        acc2 = [k.sb("acc", [128, 512], F32, es) for _ in range(2)]
        zs = k.sb("zs", [128, 512], F32, es)
        dtt = k.sb("dtt", [128, 32], F32, es)
        e3 = k.sb("e3", [128, 24], F32, es)
        xdt = k.sb("xdt", [128, 8, 64], BF16, es)
        xdte = k.sb("xdte", [128, 8, 64], BF16, es)
        xsD = k.sb("xsD", [128, 512], F32, es)
        Btok = k.sb("Btok", [128, 256], BF16, es)
        Rm = k.sb("Rm", [128, 8, 128], F32, es)
        seg = Rm
        cbm = k.sb("cbm", [128, 2, 128], F32, es)
        MT = k.sb("MT", [128, 8, 128], BF16, es)
        hst = k.sb("hst", [128, 512], F32, es)
        hbf = k.sb("hbf", [128, 512], BF16, es)
        y1 = k.sb("y1", [128, 512], F32, es)
        yj = k.sb("yj", [128, 512], BF16, es)
        ss2 = k.sb("ss2", [128, 8], F32, es)
        ybf = k.sb("ybf", [128, 512], BF16, es)
        ysT = k.sb("ysT", [128, 4, 512], BF16, es)
        convo = k.sb("convo", [128, 8, 3], F32, es)
        for c in range(8):
            k.memset(xpre[c][:, 0:3], 0.0, [xpre[c]])
        tri, lm, ones = cst("tri"), cst("lm"), cst("ones")

        for t in range(NTB):
            n = 512
            c0 = t * 512
            hbufs = hTb[4 * t:4 * t + 4]
            for c in range(8):
                bi_ = c % 2
                for kk in range(8):
                    k.mm(bank(bi_)[:, 0:n], Wis[:, kk, 512 + 128 * c:512 + 128 * (c + 1)], hT[:, kk, c0:c0 + n],
                         kk == 0, kk == 7, WisB(512 + 128 * c, 640 + 128 * c) + hbufs, [pb[bi_]])
                k.act(xpre[c][:, 3:3 + n], bank(bi_)[:, 0:n], AF.Copy, [pb[bi_]], [xpre[c]])
                acc = acc2[c % 2]
                k.act(acc[:, 0:n], xpre[c][:, 0:n], AF.Identity, [xpre[c], cols], [acc],
                      scale=col("ssd_cw", 4 * c), bias=col("ssd_cb", c))
                for j in range(1, 4):
                    k.stt(acc[:, 0:n], xpre[c][:, j:j + n], col("ssd_cw", 4 * c + j), acc[:, 0:n], mult, add,
                          [xpre[c], cols, acc], [acc])
                k.act(xact[c][:, 0:n], acc[:, 0:n], AF.Silu, [acc], [xact[c]])
                if t == 3:
                    k.cp(convo[:, c, :], xpre[c][:, n:n + 3], [xpre[c]], [convo], eng="pool")
                else:
                    k.cp(xpre[c][:, 0:3], xpre[c][:, n:n + 3], [xpre[c]], [xpre[c]], eng="pool")
            for bi in range(4):
                blk = 4 * t + bi
                cs = slice(bi * 128, bi * 128 + 128)
                hc = slice(blk * 128, blk * 128 + 128)
                for kk in range(8):
                    k.mm(bank(2)[:, :], hT[:, kk, hc], Wis[:, kk, 0:512], kk == 0, kk == 7, WisB(0, 512) + [hTb[blk]], [pb[2]])
                    k.mm(bank(3)[:, 0:8], hT[:, kk, hc], Wis[:, kk, 1536:1544], kk == 0, kk == 7,
                         WisB(1536, 1544) + [hTb[blk]], [pb[3]])
                k.act(zs[:, :], bank(2)[:, :], AF.Silu, [pb[2]], [zs])
                k.tt(dtt[:, 0:8], bank(3)[:, 0:8], sm[:, 0:8], add, [pb[3], sm], [dtt])
                k.ts(dtt[:, 24:32], dtt[:, 0:8], 30.0, None, ALU.min, None, [dtt], [dtt])
                k.act(dtt[:, 24:32], dtt[:, 24:32], AF.Exp, [dtt], [dtt])
                k.act(dtt[:, 8:16], dtt[:, 24:32], AF.Ln, [dtt], [dtt], bias=1.0)
                k.ts(dtt[:, 24:32], dtt[:, 0:8], -30.0, 0.0, add, ALU.max, [dtt], [dtt])
                k.tt(dtt[:, 8:16], dtt[:, 8:16], dtt[:, 24:32], add, [dtt], [dtt])
                k.tt(dtt[:, 16:24], dtt[:, 8:16], sm[:, 24:32], mult, [dtt, sm], [dtt])
                dt_ = dtt[:, 8:16]
                dta = dtt[:, 16:24]
                ptb = bank_bf(6)[:, 0:768].rearrange("p (c t) -> p c t", c=6)
                for c in range(6):
                    k.tr(ptb[:, c, :], xact[c][:, cs], identb[:, :], [xact[c], identb], [pb[6]])
                xs_ps = ptb[:, 0:4, :].rearrange("p c (h d) -> p (c h) d", h=2)
                k.tt(xdt[:, :, :], xs_ps, bc(dt_, 2, 64), mult, [pb[6], dtt], [xdt])
                k.tt(xsD[:, :].rearrange("p (h d) -> p h d", h=8), xs_ps, bc(sm[:, 16:24], 2, 64), mult,
                     [pb[6], sm], [xsD])
                k.act(Btok[:, :], ptb[:, 4:6, :].rearrange("p c t -> p (c t)"), AF.Copy, [pb[6]], [Btok])
                k.tt(Rm[:, :, :], bc(tri, 1, 8), bc(dta, 2, 128), mult, [consts, dtt], [Rm])
                for hh in range(2):
                    k.mm(bank(4 + hh)[:, :], lm, Rm[:, 4 * hh:4 * hh + 4, :].rearrange("p h q -> p (h q)"), True, True,
                         [consts, Rm], [pb[4 + hh]])
                k.mm(bank(3)[:, 8:16], tri, dta, True, True, [consts, dtt], [pb[3]])
                k.mm(bank(3)[:, 16:24], lm, dta, True, True, [consts, dtt], [pb[3]])
                k.mm(bank(3)[:, 24:32], ones, dta, True, True, [consts, dtt], [pb[3]])
                k.act(e3[:, :], bank(3)[:, 8:32], AF.Exp, [pb[3]], [e3])
                k.act(seg[:, :, :].rearrange("p h q -> p (h q)"), psA[2][:, :, :].rearrange("p a b -> p (a b)"),
                      AF.Exp, [pb[4], pb[5]], [seg])
                for g in range(2):
                    k.mm(bank(3)[:, 64 + 128 * g:64 + 128 * (g + 1)], xact[4 + g][:, cs], xact[6 + g][:, cs], True,
                         True, [xact[4 + g], xact[6 + g]], [pb[3]])
                k.tt(cbm[:, :, :], bank(3)[:, 64:320].rearrange("p (g q) -> p g q", g=2), bc(tri, 1, 2), mult,
                     [pb[3], consts], [cbm])
                k.tt(MT[:, :, :].rearrange("p (g e) q -> p g e q", g=2),
                     seg[:, :, :].rearrange("p (g e) q -> p g e q", g=2), bc(cbm[:, :, :], 2, 4), mult,
                     [seg, cbm], [MT])
                k.tt(xdte[:, :, :], xdt[:, :, :], bc(e3[:, 8:16], 2, 64), mult, [xdt, e3], [xdte])
                for h in range(8):
                    k.mm(bank(7)[:, 64 * h:64 * (h + 1)], MT[:, h, :], xdt[:, h, :], True, True, [MT, xdt], [pb[7]])
                if blk > 0:
                    for g in range(2):
                        k.mm(bank(5)[:, 256 * g:256 * (g + 1)], xact[6 + g][:, cs], hbf[:, 256 * g:256 * (g + 1)],
                             True, True, [xact[6 + g], hbf], [pb[5]])
                    k.tt(y1[:, :].rearrange("p (h d) -> p h d", h=8),
                         bank(5)[:, :].rearrange("p (h d) -> p h d", h=8), bc(e3[:, 0:8], 2, 64), mult,
                         [pb[5], e3], [y1])
                    k.tt(y1[:, :], bank(7)[:, :], y1[:, :], add, [pb[7], y1], [y1])
                    k.tt(y1[:, :], y1[:, :], xsD[:, :], add, [y1, xsD], [y1])
                else:
                    k.tt(y1[:, :], bank(7)[:, :], xsD[:, :], add, [pb[7], xsD], [y1])
                for g in range(2):
                    k.mm(bank(4)[:, 256 * g:256 * (g + 1)], Btok[:, 128 * g:128 * (g + 1)],
                         xdte[:, 4 * g:4 * g + 4, :].rearrange("p h d -> p (h d)"), True, True, [Btok, xdte], [pb[4]])
                if blk > 0:
                    k.tt(hst[:, :].rearrange("p (h d) -> p h d", h=8), hst[:, :].rearrange("p (h d) -> p h d", h=8),
                         bc(e3[:, 16:24], 2, 64), mult, [hst, e3], [hst])
                    k.tt(hst[:, :], bank(4)[:, :], hst[:, :], add, [pb[4], hst], [hst])
                else:
                    k.cp(hst[:, :], bank(4)[:, :], [pb[4]], [hst])
                k.act(hbf[:, :], hst[:, :], AF.Copy, [hst], [hbf])
                k.tt(y1[:, :], y1[:, :], zs[:, :], mult, [y1, zs], [y1])
                for g in range(2):
                    k.act(yj[:, 256 * g:256 * (g + 1)], y1[:, 256 * g:256 * (g + 1)], AF.Square, [y1], [yj, ss2],
                          accum_out=ss2[:, g:g + 1])
                k.act(ss2[:, 2:4], ss2[:, 0:2], AF.Ln, [ss2, epsc], [ss2], scale=1.0 / 256, bias=epsc[:, 0:1])
                k.act(ss2[:, 4:6], ss2[:, 2:4], AF.Exp, [ss2], [ss2], scale=-0.5)
                k.tt(y1[:, :].rearrange("p (g d) -> p g d", g=2), y1[:, :].rearrange("p (g d) -> p g d", g=2),
                     bc(ss2[:, 4:6], 2, 256), mult, [y1, ss2], [y1])
                k.tt(y1[:, :], y1[:, :], nwbc[:, :], mult, [y1, nwbc], [y1])
                if dbg and blk in (0, 5, 15):
                    k.dma("sp", dbg_d["yssd"][blk * 128:(blk + 1) * 128, :], y1[:, :], reads=[y1])
                k.cp(ybf[:, :], y1[:, :], [y1], [ybf])
                pty = bank_bf(6)[:, 0:512].rearrange("p (c t) -> p c t", c=4)
                for c in range(4):
                    k.tr(pty[:, c, :], ybf[:, 128 * c:128 * (c + 1)], identb[:, :], [ybf, identb], [pb[6]])
                k.act(ysT[:, :, cs], pty[:, :, :], AF.Copy, [pb[6]], [ysT])
            for bi in range(4):
                blk = 4 * t + bi
                cs = slice(bi * 128, bi * 128 + 128)
                for kk in range(4):
                    for hf in range(2):
                        k.mm(bank(hf)[:, :], ysT[:, kk, cs], Wout[:, kk, 512 * hf:512 * (hf + 1)], kk == 0, kk == 3,
                             [ysT, Wout], [pb[hf]])
                for hf in range(2):
                    k.tt(x_tok[:, blk, 512 * hf:512 * (hf + 1)], bank(hf)[:, :], x_tok[:, blk, 512 * hf:512 * (hf + 1)],
                         add, [pb[hf], xb[blk]], [xb[blk]])
        k.dma("sp", o_conv_p, convo[:, :, :], reads=[convo])
        for j in range(4):
            k.tr(bank(2)[:, 128 * j:128 * (j + 1)], hst[:, 128 * j:128 * (j + 1)], ident, [hst, consts], [pb[2]])
        k.cp(y1[:, :], bank(2)[:, :], [pb[2]], [y1])
        k.dma("sp", o_ssm_p.rearrange("(j p) n -> p j n", p=128), y1[:, :].rearrange("p (j n) -> p j n", j=4),
              reads=[y1])
        k.barrier()

    with ExitStack() as es:
        Wout = k.sb("Wout", [128, 4, 1024], BF16, es)
        k.dma("pool", Wout[:, :, :], w_out_v[:, 0:4, :], writes=[Wout])
        sm = k.sb("smallrow", [128, 32], F32, es)
        nwbc = k.sb("nwbc", [128, 512], F32, es)
        for nm, o in (("dt_bias", 0), ("a_log", 8), ("ssd_d", 16)):
            ro, rs_ = RO[nm]
            k.dma("sp", sm.t[:, o:o + 8].unsqueeze(1), rows_d[0:1, ro:ro + 8].partition_broadcast(128), writes=[sm])
        row_bc("ssd_norm", nwbc)
        k.act(sm[:, 24:32], sm[:, 8:16], AF.Exp, [sm], [sm])
        k.ts(sm[:, 24:32], sm[:, 24:32], -1.0, None, mult, None, [sm], [sm])
        zs = k.sb("zss", [64, 512], F32, es)
        dtt = k.sb("dtts", [64, 40], F32, es)
        xsD = k.sb("xsDs", [64, 512], F32, es)
        xq = k.sb("xq", [128, 4, 64], F32, es)
        Bq = k.sb("Bq", [128, 4, 128], F32, es)
        Cq = k.sb("Cq", [128, 4, 128], F32, es)
        dAq = k.sb("dAq", [128, 4, 1], F32, es)
        yq = k.sb("yq", [128, 4, 64], F32, es)
        y1 = k.sb("y1s", [64, 512], F32, es)
        yj = k.sb("yjs", [64, 512], BF16, es)
        ss2 = k.sb("ss2s", [64, 8], F32, es)
        ybf = k.sb("ybfs", [64, 512], BF16, es)
        ysT = k.sb("ysTs", [128, 4, 64], BF16, es)
        esa = ExitStack()
        Wis = k.sb("Wis", [128, 8, 1544], BF16, esa)
        for h2 in range(2):
            k.dma("pool", Wis[:, 4 * h2:4 * h2 + 4, :], w_in_v[:, 4 * h2:4 * h2 + 4, 0:1544], writes=[Wis])
        xpre = k.sb("xpres", [128, 8, 112], F32, esa)
        xactf = k.sb("xactf", [128, 8, 64], F32, esa)
        acc = k.sb("accs", [128, 64], F32, esa)
        tokx = k.sb("tokx", [64, 1024], F32, esa)
        xdtf = k.sb("xdtf", [64, 512], F32, esa)
        Bh = k.sb("Bh", [64, 8, 128], F32, esa)
        Ch = k.sb("Ch", [64, 8, 128], F32, esa)
        scr_x_t = nc.dram_tensor("scr_x", [64, 512], F32)
        scr_B_t = nc.dram_tensor("scr_B", [64, 1024], F32)
        scr_C_t = nc.dram_tensor("scr_C", [64, 1024], F32)
        scr_a_t = nc.dram_tensor("scr_a", [64, 8], F32)
        scr_y_t = nc.dram_tensor("scr_y", [64, 512], F32)
        sbx, sbB, sbC, sba, sby = [Buf(n_) for n_ in "scx scB scC sca scy".split()]
        k.dma("sp", xpre[:, :, 0:48], st_conv_d.rearrange("p c j s -> p c (j s)"), writes=[xpre])
        n = 64
        c0 = 2048
        hbufs = [hTb[16]]
        for c in range(8):
            bi_ = c % 2
            for kk in range(8):
                k.mm(bank(bi_)[:, 0:n], Wis[:, kk, 512 + 128 * c:512 + 128 * (c + 1)], hT[:, kk, c0:c0 + n],
                     kk == 0, kk == 7, [Wis] + hbufs, [pb[bi_]])
            k.act(xpre[:, c, 48:112], bank(bi_)[:, 0:n], AF.Copy, [pb[bi_]], [xpre])
            k.act(acc[:, :], xpre[:, c, 0:64], AF.Identity, [xpre, cols], [acc], scale=col("ssd_cw", 4 * c),
                  bias=col("ssd_cb", c))
            for j in range(1, 4):
                k.stt(acc[:, :], xpre[:, c, 16 * j:16 * j + 64], col("ssd_cw", 4 * c + j), acc[:, :], mult, add,
                      [xpre, cols, acc], [acc])
            k.act(xactf[:, c, :], acc[:, :], AF.Silu, [acc], [xactf])
        k.dma("sp", o_conv_s.rearrange("p c j s -> p c (j s)"), xpre[:, :, 64:112], reads=[xpre])
        for kk in range(8):
            k.mm(bank(2)[0:64, :], hT[:, kk, c0:c0 + n], Wis[:, kk, 0:512], kk == 0, kk == 7, [Wis] + hbufs, [pb[2]])
        k.act(zs[:, :], bank(2)[0:64, :], AF.Silu, [pb[2]], [zs])
        for kk in range(8):
            k.mm(bank(3)[0:64, 0:8], hT[:, kk, c0:c0 + n], Wis[:, kk, 1536:1544], kk == 0, kk == 7, [Wis] + hbufs,
                 [pb[3]])
        k.tt(dtt[:, 0:8], bank(3)[0:64, 0:8], sm[0:64, 0:8], add, [pb[3], sm], [dtt])
        k.ts(dtt[:, 24:32], dtt[:, 0:8], 30.0, None, ALU.min, None, [dtt], [dtt])
        k.act(dtt[:, 24:32], dtt[:, 24:32], AF.Exp, [dtt], [dtt])
        k.act(dtt[:, 8:16], dtt[:, 24:32], AF.Ln, [dtt], [dtt], bias=1.0)
        k.ts(dtt[:, 24:32], dtt[:, 0:8], -30.0, 0.0, add, ALU.max, [dtt], [dtt])
        k.tt(dtt[:, 8:16], dtt[:, 8:16], dtt[:, 24:32], add, [dtt], [dtt])
        k.tt(dtt[:, 16:24], dtt[:, 8:16], sm[0:64, 24:32], mult, [dtt, sm], [dtt])
        k.act(dtt[:, 32:40], dtt[:, 16:24], AF.Exp, [dtt], [dtt])
        for c in range(8):
            bk = 4 + c // 4
            k.tr(bank(bk)[0:64, 128 * (c % 4):128 * (c % 4 + 1)], xactf[:, c, :], ident, [xactf, consts], [pb[bk]])
        k.act(tokx[:, 0:512], bank(4)[0:64, :], AF.Copy, [pb[4]], [tokx])
        k.act(tokx[:, 512:1024], bank(5)[0:64, :], AF.Copy, [pb[5]], [tokx])
        v8 = lambda ap: ap.rearrange("p (h d) -> p h d", h=8)
        k.tt(v8(xdtf[:, :]), v8(tokx[:, 0:512]), bc(dtt[:, 8:16], 2, 64), mult, [tokx, dtt], [xdtf])
        k.tt(v8(xsD[:, :]), v8(tokx[:, 0:512]), bc(sm[0:64, 16:24], 2, 64), mult, [tokx, sm], [xsD])
        k.cp(Bh[:, :, :].rearrange("p (g e) n -> p g e n", g=2),
             bc(tokx[:, 512:768].rearrange("p (g n) -> p g n", g=2), 2, 4), [tokx], [Bh])
        k.cp(Ch[:, :, :].rearrange("p (g e) n -> p g e n", g=2),
             bc(tokx[:, 768:1024].rearrange("p (g n) -> p g n", g=2), 2, 4), [tokx], [Ch])
        k.dma("sp", scr_x_t.ap(), xdtf[:, :], reads=[xdtf], writes=[sbx])
        k.dma("sp", scr_B_t.ap(), Bh[:, :, :].rearrange("p h n -> p (h n)"), reads=[Bh], writes=[sbB])
        k.dma("sp", scr_C_t.ap(), Ch[:, :, :].rearrange("p h n -> p (h n)"), reads=[Ch], writes=[sbC])
        k.dma("sp", scr_a_t.ap(), dtt[:, 32:40], reads=[dtt], writes=[sba])
        k.dma("sp", xq[:, :, :], bass.AP(scr_x_t, 0, [[64, 128], [8192, 4], [1, 64]]), reads=[sbx], writes=[xq])
        k.dma("sp", Bq[:, :, :], bass.AP(scr_B_t, 0, [[128, 128], [16384, 4], [1, 128]]), reads=[sbB], writes=[Bq])
        k.dma("sp", Cq[:, :, :], bass.AP(scr_C_t, 0, [[128, 128], [16384, 4], [1, 128]]), reads=[sbC], writes=[Cq])
        k.dma("sp", dAq[:, :, :], bass.AP(scr_a_t, 0, [[1, 128], [128, 4], [1, 1]]), reads=[sba], writes=[dAq],
              allow_slow_non_contiguous=True)
        k.barrier()
        esa.close()
        hS = k.sb("hS", [128, 8192], F32, es)
        tmpS = k.sb("tmpS", [128, 8192], F32, es)
        k.dma("sp", hS[:, :], st_ssm_d, writes=[hS])
        h3 = lambda ap: ap.rearrange("p (a b) -> p a b", b=128)
        for l in range(4):
            k.tt(h3(tmpS[:, :]), bc(xq[:, l, :], 2, 128), bc(Bq[:, l, :], 1, 64), mult, [xq, Bq], [tmpS], eng="pool")
            k.stt(hS[:, :], hS[:, :], dAq[:, l, :], tmpS[:, :], mult, add, [hS, dAq, tmpS], [hS])
            k.tt(h3(tmpS[:, :]), h3(hS[:, :]), bc(Cq[:, l, :], 1, 64), mult, [hS, Cq], [tmpS])
            k.red(yq[:, l, :], h3(tmpS[:, :]), add, [tmpS], [yq])
        k.dma("sp", o_ssm_s, hS[:, :], reads=[hS])
        k.dma("sp", bass.AP(scr_y_t, 0, [[64, 128], [8192, 4], [1, 64]]), yq[:, :, :], reads=[yq], writes=[sby])
        k.dma("sp", y1[:, :], scr_y_t.ap(), reads=[sby], writes=[y1])
        k.tt(y1[:, :], y1[:, :], xsD[:, :], add, [y1, xsD], [y1])
        k.tt(y1[:, :], y1[:, :], zs[:, :], mult, [y1, zs], [y1])
        for g in range(2):
            k.act(yj[:, 256 * g:256 * (g + 1)], y1[:, 256 * g:256 * (g + 1)], AF.Square, [y1], [yj, ss2],
                  accum_out=ss2[:, g:g + 1])
        k.act(ss2[:, 2:4], ss2[:, 0:2], AF.Sqrt, [ss2, epsc], [ss2], scale=1.0 / 256, bias=epsc[0:64, 0:1])
        k.recip(ss2[:, 4:6], ss2[:, 2:4], [ss2], [ss2])
        k.tt(y1[:, :].rearrange("p (g d) -> p g d", g=2), y1[:, :].rearrange("p (g d) -> p g d", g=2),
             bc(ss2[:, 4:6], 2, 256), mult, [y1, ss2], [y1])
        k.tt(y1[:, :], y1[:, :], nwbc[0:64, :], mult, [y1, nwbc], [y1])
        if dbg:
            k.dma("sp", dbg_d["yssd"][2048:2112, :], y1[:, :], reads=[y1])
        k.cp(ybf[:, :], y1[:, :], [y1], [ybf])
        pty = bank_bf(6)[:, 0:256].rearrange("p (c t) -> p c t", c=4)
        for c in range(4):
            k.tr(pty[:, c, :], ybf[:, 128 * c:128 * (c + 1)], identb[0:64, 0:64], [ybf, identb], [pb[6]])
        k.act(ysT[:, :, :], pty[:, :, :], AF.Copy, [pb[6]], [ysT])
        for hf in range(2):
            bk = hf
            for kk in range(4):
                k.mm(bank(bk)[0:64, :], ysT[:, kk, :], Wout[:, kk, 512 * hf:512 * (hf + 1)], kk == 0, kk == 3,
                     [ysT, Wout], [pb[bk]])
            k.tt(x_tok[0:64, 16, 512 * hf:512 * (hf + 1)], bank(bk)[0:64, :], x_tok[0:64, 16, 512 * hf:512 * (hf + 1)],
                 add, [pb[bk], xb[16]], [xb[16]])
        k.barrier()

    with ExitStack() as es:
        Wir, WirB = load_weight("Wir", w_in_v[:, :, 1544:3336], 1792,
                                [(128 * c_, 128 * c_ + 128) for c_ in [12, 13] + [c_ for p_ in range(4) for c_ in (p_, 4 + p_, 8 + p_)]], es)
        Wout2 = k.sb("Wout2", [128, 4, 1024], BF16, es)
        k.dma("pool", Wout2[:, :, :], w_out_v[:, 4:8, :], writes=[Wout2])
        W2A2 = k.sb("W2A2", [128, 512], BF16, es)
        G2 = k.sb("G2", [128, 512], BF16, es)
        k.dma("pool", W2A2[:, :], w2a2_d, writes=[W2A2])
        k.dma("pool", G2[:, :], g2_d, writes=[G2])
        c2 = k.sb("consts2", [128, NCONST2], F32, es)
        k.dma("sp", c2[:, :], consts2_d, writes=[c2])

        def cst2(name):
            o, s = KO2[name]
            return c2[:, o:o + s]

        lnw = k.sb("lnw", [128, 4, 64], F32, es)
        lnb = k.sb("lnb", [128, 4, 64], F32, es)
        for nm, tt_ in (("ln_w", lnw), ("ln_b", lnb)):
            o, s = RO[nm]
            v = rows_d[0:1, o:o + 512].rearrange("a (pr hp d) -> a pr hp d", pr=4, hp=2)
            for hp in range(2):
                k.dma("sp", tt_.t[64 * hp:64 * hp + 64, :, :].unsqueeze(1), v[:, :, hp, :].partition_broadcast(64),
                      writes=[tt_])
        omm = k.sb("omm", [128, 14], F32, es)
        k.ts(omm[:, :], col("mu", 0, 14), -1.0, 1.0, mult, add, [cols], [omm])
        onesb = k.sb("onesb", [128, 2], BF16, es)
        k.memset(onesb[:, :], 1.0, [onesb])
        usb = [k.sb("usb", [128, 516], F32, es) for _ in range(2)]
        carry = k.sb("carry", [128, 14, 16], F32, es)
        k.memset(carry[:, :, :], 0.0, [carry])
        tlla = k.sb("tlla", [128, 512], BF16, es)
        sg = k.sb("sg", [128, 512], BF16, es)
        f = lambda nm: k.sb(nm, [128, 512], F32, es)
        um_r, um_k, um_v, sigw, a_t, kkn, t1, t2, csm, ecw, encw, exw = [f(n_) for n_ in
            "um_r um_k um_v sigw a_t kkn t1 t2 csm ecw encw exw".split()]
        gT = k.sb("gT", [128, 512], BF16, es)
        um12 = um_v
        tmpm = kkn
        AR = k.sb("AR", [128, 8, 2, 128], BF16, es)
        Bt = k.sb("Bt", [128, 8, 128], BF16, es)
        Kt = k.sb("Kt", [128, 8, 128], BF16, es)
        vT = k.sb("vT", [128, 8, 128], BF16, es)
        for z_ in (AR, Bt, Kt, vT):
            k.memset(z_.t[:].rearrange("p a b -> p (a b)") if len(z_.t.shape) == 3 else
                     z_.t[:].rearrange("p a b c -> p (a b c)"), 0.0, [z_])
        Ub2 = [k.sb("Ub", [128, 128], BF16, es) for _ in range(2)]
        Hf = k.sb("Hf", [128, 4, 128], F32, es)
        Hb = k.sb("Hb", [128, 4, 128], BF16, es)
        Hfb = [Buf(f"Hf{p}") for p in range(4)]
        Hbb = [Buf(f"Hb{p}") for p in range(4)]
        ht = k.sb("ht", [128, 128], F32, es)
        hout = k.sb("hout", [128, 4, 64], F32, es)
        vtok = k.sb("vtok", [128, 8, 64], BF16, es)
        class _V:
            def __init__(self, base):
                self.b = base.b
                self.v = base.t[:, :].rearrange("p (c d) -> p c d", d=64)

            def __getitem__(self, idx):
                return self.v[idx]
        ysb = _V(exw)
        ysq = _V(t2)
        st8 = k.sb("st8", [128, 6, 8], F32, es)
        ynb = k.sb("ynb", [128, 8, 64], BF16, es)
        prod = View(ynb.t[:].rearrange("p c d -> p (c d)"), ynb.b)
        yrT = k.sb("yrT", [128, 4, 512], BF16, es)
        shifto = k.sb("shifto", [128, 14], F32, es)
        k.memset(Hf[:, :, :], 0.0, Hfb)
        k.memset(Hb[:, :, :], 0.0, Hbb)
        wmask, i64c, bones, scanm = cst2("wmask"), cst2("i64"), cst2("bones"), cst2("scanm")
        HP = [slice(0, 64), slice(64, 128)]

        def proj_chunk(c, c0, n, S, hbufs, dst, ucount):
            bi_ = 4 + ucount % 2
            u_ = usb[ucount % 2]
            for kk in range(8):
                k.mm(bank(bi_)[:, 0:n], Wir[:, kk, 128 * c:128 * (c + 1)], hT[:, kk, c0:c0 + n], kk == 0, kk == 7,
                     WirB(128 * c, 128 * c + 128) + hbufs, [pb[bi_]])
            k.act(u_[:, S:S + n], bank(bi_)[:, 0:n], AF.Copy, [pb[bi_]], [u_])
            k.cp(u_[:, 0:S], carry[:, c, 0:S], [carry], [u_], eng="pool")
            k.act(tmpm[:, 0:n], u_[:, 0:n], AF.Identity, [u_, cols], [tmpm], scale=col("mu", c))
            k.stt(dst[:, 0:n], u_[:, S:S + n], omm[:, c:c + 1], tmpm[:, 0:n], mult, add, [u_, omm, tmpm], [dst])
            k.cp(carry[:, c, 0:S], u_[:, n:n + S], [u_], [carry], eng="pool")

        ucount = 0
        for t in range(NTB):
            n = 512
            S = 1
            c0 = t * 512
            nch = n // 64
            hbufs = hTb[4 * t:4 * t + 4]
            proj_chunk(12, c0, n, S, hbufs, um12, ucount); ucount += 1
            k.act(tlla[0:64, 0:n], um12[0:64, 0:n], AF.Tanh, [um12], [tlla])
            k.act(tlla[64:128, 0:n], um12[64:128, 0:n], AF.Copy, [um12], [tlla])
            proj_chunk(13, c0, n, S, hbufs, um12, ucount); ucount += 1
            k.act(sg[:, 0:n], um12[:, 0:n], AF.Sigmoid, [um12], [sg])
            for p in range(4):
                pc = slice(128 * p, 128 * (p + 1))
                k.mm(bank(6)[:, 0:n], W2A2[0:64, pc], tlla[0:64, 0:n], True, True, [W2A2, tlla], [pb[6]])
                k.mm(bank(7)[:, 0:n], W2A2[64:128, pc], tlla[64:128, 0:n], True, True, [W2A2, tlla], [pb[7]])
                k.act(sigw[:, 0:n], bank(6)[:, 0:n], AF.Sigmoid, [pb[6], cols], [sigw], bias=col("w0", p))
                k.act(a_t[:, 0:n], bank(7)[:, 0:n], AF.Sigmoid, [pb[7], cols], [a_t], bias=col("a0", p))
                k.mm(bank(6)[:, 0:n], G2[:, pc], sg[:, 0:n], True, True, [G2, sg], [pb[6]])
                k.act(gT[:, 0:n], bank(6)[:, 0:n], AF.Copy, [pb[6]], [gT])
                proj_chunk(p, c0, n, S, hbufs, um_r, ucount); ucount += 1
                proj_chunk(4 + p, c0, n, S, hbufs, um_k, ucount); ucount += 1
                proj_chunk(8 + p, c0, n, S, hbufs, um_v, ucount); ucount += 1
                k.ts(t1[:, 0:n], um_k[:, 0:n], col("k_k", p), None, mult, None, [um_k, cols], [t1])
                k.tt(t2[:, 0:n], t1[:, 0:n], t1[:, 0:n], mult, [t1], [t2])
                k.mm(bank(7)[:, 0:n], bones, t2[:, 0:n], True, True, [c2, t2], [pb[7]])
                k.ts(t2[:, 0:n], bank(7)[:, 0:n], 1e-19, None, ALU.max, None, [pb[7]], [t2])
                k.act(t2[:, 0:n], t2[:, 0:n], AF.Ln, [t2], [t2])
                k.act(t2[:, 0:n], t2[:, 0:n], AF.Exp, [t2], [t2], scale=-0.5)
                k.tt(kkn[:, 0:n], t1[:, 0:n], t2[:, 0:n], mult, [t1, t2], [kkn])
                k.ts(t1[:, 0:n], a_t[:, 0:n], -1.0, col("k_a", p), add, mult, [a_t, cols], [t1])
                k.stt(t1[:, 0:n], t1[:, 0:n], 1.0, um_k[:, 0:n], add, mult, [t1, um_k], [t1])
                k.scan(csm[:, 0:n], scanm[:, 0:n], sigw[:, 0:n], 0.0, mult, add, [c2, sigw], [csm])
                k.tt(t2[:, 0:n], csm[:, 0:n], sigw[:, 0:n], sub, [csm, sigw], [t2])
                k.act(ecw[:, 0:n], csm[:, 0:n], AF.Exp, [csm], [ecw], scale=-C0)
                k.act(encw[:, 0:n], csm[:, 0:n], AF.Exp, [csm], [encw], scale=C0)
                k.act(exw[:, 0:n], t2[:, 0:n], AF.Exp, [t2], [exw], scale=-C0)
                v3 = lambda ap: ap.rearrange("p (c t) -> p c t", t=64)
                k.tt(t2[:, 0:n], a_t[:, 0:n], kkn[:, 0:n], mult, [a_t, kkn], [t2])
                for hp in range(2):
                    rs = HP[hp]
                    dc_ = slice(64 * hp, 64 * hp + 64)
                    k.stt(AR[rs, 0:nch, 0, dc_], v3(kkn[rs, 0:n]), -1.0, v3(exw[rs, 0:n]), mult, mult, [kkn, exw], [AR])
                    k.tt(AR[rs, 0:nch, 1, dc_], v3(um_r[rs, 0:n]), v3(ecw[rs, 0:n]), mult, [um_r, ecw], [AR])
                    k.tt(Bt[rs, 0:nch, dc_], v3(t2[rs, 0:n]), v3(encw[rs, 0:n]), mult, [t2, encw], [Bt])
                    k.tt(Kt[rs, 0:nch, dc_], v3(t1[rs, 0:n]), v3(encw[rs, 0:n]), mult, [t1, encw], [Kt])
                    k.act(vT[rs, 0:nch, dc_], v3(um_v[rs, 0:n]), AF.Copy, [um_v], [vT])
                k.stt(prod[:, 0:n], um_r[:, 0:n], col("r_k", p), t1[:, 0:n], mult, mult, [um_r, cols, t1], [prod])
                for ch in range(nch):
                    tk = slice(64 * ch, 64 * ch + 64)
                    for hp in range(2):
                        rs = HP[hp]
                        k.mm(bank(6)[rs, 256 + ch:257 + ch], prod[rs, tk], onesb[rs, 0:1], True, True, [prod, onesb],
                             [pb[6]])
                k.cp(st8[:, 5, 0:nch], bank(6)[:, 256:256 + nch], [pb[6]], [st8])
                cpar = [(um_r, um_k), (um_v, sigw), (a_t, kkn), (t1, csm)]
                ubs = []
                for g in range(4):
                    va = k.carve(cpar[g][0], [(1280, BF16), (768, BF16)])
                    vb = k.carve(cpar[g][1], [(768, BF16), (256, BF16), (1024, BF16)])
                    ubs.append([va[0], va[1], vb[0], vb[1], vb[2]])
                sm_a = k.carve(encw, [(256, BF16), (256, BF16), (512, F32)] * 2)
                sm_b = k.carve(t2, [(256, BF16), (256, BF16), (512, F32)] * 2)
                smalls = sm_a + sm_b
                for grp in range(nch // 4):
                    chs = [4 * grp + g for g in range(4)]
                    for g, ch in enumerate(chs):
                        mats = ubs[g][0]
                        arc = AR[:, ch, :, :].rearrange("p a t -> p (a t)")
                        k.mm(bank(g)[:, 0:256], Bt[:, ch, :], arc, True, True, [Bt, AR], [pb[g]])
                        k.mm(bank(g)[:, 256:512], Kt[:, ch, :], arc, True, True, [Kt, AR], [pb[g]])
                        k.tt(mats[:, 0:512], bank(g)[:, 0:512], wmask[:, 0:512], mult, [pb[g], c2], [mats])
                    for g, ch in enumerate(chs):
                        mats, l0 = ubs[g][0], ubs[g][1]
                        k.mm(bank(g)[:, 0:128], AR[:, ch, 0, :], Bt[:, ch, :], True, True, [Bt, AR], [pb[g]])
                        k.tt(mats[:, 512:640], bank(g)[:, 0:128], wmask[:, 512:640], mult, [pb[g], c2], [mats])
                        k.tt(l0[:, 0:128], mats[:, 0:128], ident, add, [mats, consts], [l0])
                    for g in range(4):
                        mats, l0 = ubs[g][0], ubs[g][1]
                        k.mm(bank(g)[:, 0:128], mats[:, 512:640], mats[:, 0:128], True, True, [mats], [pb[g]])
                        k.mm(bank(g)[:, 128:256], mats[:, 0:128], mats[:, 512:640], True, True, [mats], [pb[g]])
                        k.act(l0[:, 128:384], bank(g)[:, 0:256], AF.Copy, [pb[g]], [l0])
                    ci, ni = 1, 2
                    for j in range(1, 5):
                        for g in range(4):
                            cur, nxt = ubs[g][ci], ubs[g][ni]
                            k.mm(bank(g)[:, 0:256], cur[:, 256:384], cur[:, 0:256], True, True, [cur], [pb[g]])
                            k.mm(bank(g)[:, 256:384], cur[:, 128:256], cur[:, 256:384], True, True, [cur], [pb[g]])
                            k.tt(nxt[:, 0:128], cur[:, 0:128], bank(g)[:, 0:128], add, [cur, pb[g]], [nxt])
                            k.act(nxt[:, 128:384], bank(g)[:, 128:384], AF.Copy, [pb[g]], [nxt])
                        ci, ni = ni, ci
                    for g in range(4):
                        cur, TTm_ = ubs[g][ci], ubs[g][3]
                        k.mm(bank(g)[:, 0:128], cur[:, 256:384], cur[:, 0:128], True, True, [cur], [pb[g]])
                        k.tt(TTm_[:, :], cur[:, 0:128], bank(g)[:, 0:128], add, [cur, pb[g]], [TTm_])
                    for g, ch in enumerate(chs):
                        tok_ = ubs[g][4]
                        ptk = bank_bf(g)
                        srcs = [(AR[:, ch, 0, :], AR), (Bt[:, ch, :], Bt), (Kt[:, ch, :], Kt), (vT[:, ch, :], vT)]
                        for q, (sap, sb_) in enumerate(srcs):
                            k.tr(ptk[:, 128 * q:128 * (q + 1)], sap, identb[:, :], [sb_, identb], [pb[g]])
                        k.act(tok_[:, :], bank_bf(g)[:, 0:512], AF.Copy, [pb[g]], [tok_])
                        for hp in range(2):
                            rs = HP[hp]
                            k.cp(vtok[rs, ch, :], tok_[rs, 384 + 64 * hp:448 + 64 * hp], [tok_], [vtok], eng="pool")
                    for g in range(4):
                        mats, tok_, X1b_ = ubs[g][0], ubs[g][4], smalls[3 * g]
                        k.mm(bank(g)[:, 256:384], mats[:, 256:384], tok_[:, 384:512], True, True, [mats, tok_], [pb[g]])
                        k.act(X1b_[:, :], bank(g)[:, 256:384], AF.Copy, [pb[g]], [X1b_])
                    for g in range(4):
                        tok_, TTm_, X1b_ = ubs[g][4], ubs[g][3], smalls[3 * g]
                        ApT_, Vp_ = smalls[3 * g + 1], smalls[3 * g + 2]
                        k.mm(bank(g)[:, 0:128], tok_[:, 0:128], TTm_[:, :], True, True, [tok_, TTm_], [pb[g]])
                        k.mm(bank(g)[:, 128:256], TTm_[:, :], X1b_[:, :], True, True, [TTm_, X1b_], [pb[g]])
                        k.act(ApT_[:, :], bank(g)[:, 0:128], AF.Copy, [pb[g]], [ApT_])
                        k.cp(Vp_[:, :], bank(g)[:, 128:256], [pb[g]], [Vp_])
                    if DBGU < 4:
                        continue
                    for g, ch in enumerate(chs):
                        mats, tok_ = ubs[g][0], ubs[g][4]
                        ApT_, Vp_ = smalls[3 * g + 1], smalls[3 * g + 2]
                        Ub_ = Ub2[g % 2]
                        k.mm(bank(6)[:, 0:128], ApT_[:, :], Hb[:, p, :], True, True, [ApT_, Hbb[p]], [pb[6]])
                        k.tt(Ub_[:, :], bank(6)[:, 0:128], Vp_[:, :], add, [pb[6], Vp_], [Ub_])
                        ho = bank(6)[:, 128:256]
                        k.mm(ho, tok_[:, 256:384], tok_[:, 384:512], True, False, [tok_], [pb[6]])
                        k.mm(ho, tok_[:, 128:256], Ub_[:, :], False, True, [tok_, Ub_], [pb[6]])
                        yo = bank(7)[:, 128 * g:128 * (g + 1)]
                        k.mm(yo, AR[:, ch, 1, :], Hb[:, p, :], True, False, [AR, Hbb[p]], [pb[7]])
                        k.mm(yo, mats[:, 128:256], Ub_[:, :], False, False, [mats, Ub_], [pb[7]])
                        k.mm(yo, mats[:, 384:512], tok_[:, 384:512], False, True, [mats, tok_], [pb[7]])
                        gcol = ecw[:, 64 * ch + 63:64 * ch + 64]
                        k.tt(ht[:, :], bank(6)[:, 128:256], Hf[:, p, :], add, [pb[6], Hfb[p]], [ht])
                        k.act(Hb[:, p, :], ht[:, :], AF.Identity, [ht, ecw], [Hbb[p]], scale=gcol)
                        k.ts(Hf[:, p, :], ht[:, :], gcol, None, mult, None, [ht, ecw], [Hfb[p]])
                    if DBGU < 5:
                        continue
                    k.act(usb[0][:, 0:512], bank(7)[:, :], AF.Copy, [pb[7]], [usb[0]])
                    for hp in range(2):
                        rs = HP[hp]
                        k.cp(ysb[rs, 4 * grp:4 * grp + 4, :],
                             usb[0][rs, 0:512].rearrange("p (c d) -> p c d", d=128)[:, :, 64 * hp:64 * hp + 64],
                             [usb[0]], [ysb], eng="pool")
                for g in range(4):
                    k.merge(cpar[g][0], ubs[g][0:2])
                    k.merge(cpar[g][1], ubs[g][2:5])
                k.merge(encw, sm_a)
                k.merge(t2, sm_b)
                k.red(st8[:, 0, 0:nch], ysb[:, 0:nch, :], add, [ysb], [st8])
                k.tt(ysq[:, 0:nch, :], ysb[:, 0:nch, :], ysb[:, 0:nch, :], mult, [ysb], [ysq])
                k.red(st8[:, 1, 0:nch], ysq[:, 0:nch, :], add, [ysq], [st8])
                k.ts(st8[:, 2, 0:nch], st8[:, 0, 0:nch], 1.0 / 64, None, mult, None, [st8], [st8])
                k.tt(st8[:, 4, 0:nch], st8[:, 2, 0:nch], st8[:, 2, 0:nch], mult, [st8], [st8])
                k.stt(st8[:, 3, 0:nch], st8[:, 1, 0:nch], 1.0 / 64, st8[:, 4, 0:nch], mult, sub, [st8], [st8])
                k.act(st8[:, 3, 0:nch], st8[:, 3, 0:nch], AF.Ln, [st8, epsc], [st8], bias=epsc[:, 1:2])
                k.act(st8[:, 3, 0:nch], st8[:, 3, 0:nch], AF.Exp, [st8], [st8], scale=-0.5)
                k.tt(ysb[:, 0:nch, :], ysb[:, 0:nch, :], bc(st8[:, 2, 0:nch], 2, 64), sub, [ysb, st8], [ysb])
                k.tt(ysb[:, 0:nch, :], ysb[:, 0:nch, :], bc(st8[:, 3, 0:nch], 2, 64), mult, [ysb, st8], [ysb])
                k.tt(ysb[:, 0:nch, :], ysb[:, 0:nch, :], bc(lnw[:, p, :], 1, nch), mult, [ysb, lnw], [ysb])
                k.tt(ysb[:, 0:nch, :], ysb[:, 0:nch, :], bc(lnb[:, p, :], 1, nch), add, [ysb, lnb], [ysb])
                k.tt(ysq[:, 0:nch, :], vtok[:, 0:nch, :], bc(st8[:, 5, 0:nch], 2, 64), mult, [vtok, st8], [ysq])
                k.tt(ynb[:, 0:nch, :], ysb[:, 0:nch, :], ysq[:, 0:nch, :], add, [ysb, ysq], [ynb])
                pty = bank_bf(5)
                for ch in range(nch):
                    for hp in range(2):
                        rs = HP[hp]
                        k.tr(pty[rs, 64 * ch:64 * (ch + 1)], ynb[rs, ch, :], identb[rs, rs], [ynb, identb], [pb[5]])
                k.tt(yrT[:, p, 0:n], pty[:, 0:n], gT[:, 0:n], mult, [pb[5], gT], [yrT])
            if dbg:
                for p in range(4):
                    k.cp(um_r[:, 0:n], yrT[:, p, 0:n], [yrT], [um_r])
                    k.dma("sp", dbg_d["yrwT"][:, p, c0:c0 + n], um_r[:, 0:n], reads=[um_r])
            for bi in range(n // 128):
                blk = 4 * t + bi
                cs = slice(bi * 128, bi * 128 + 128)
                for kk in range(4):
                    for hf in range(2):
                        k.mm(bank(hf)[:, :], yrT[:, kk, cs], Wout2[:, kk, 512 * hf:512 * (hf + 1)], kk == 0, kk == 3,
                             [yrT, Wout2], [pb[hf]])
                for hf in range(2):
                    k.tt(x_tok[:, blk, 512 * hf:512 * (hf + 1)], bank(hf)[:, :], x_tok[:, blk, 512 * hf:512 * (hf + 1)],
                         add, [pb[hf], xb[blk]], [xb[blk]])
        k.cp(shifto[:, :], carry[:, :, 0], [carry], [shifto])
        k.dma("sp", o_shift_p, shifto[:, :], reads=[shifto])
        for hp in range(2):
            rs = HP[hp]
            k.cp(hout[rs, :, :], Hf[rs, :, 64 * hp:64 * hp + 64], Hfb, [hout])
        k.dma("sp", o_wkv_p, hout[:, :, :], reads=[hout])
        k.barrier()

    with ExitStack() as es:
        Wout2 = k.sb("Wout2", [128, 4, 1024], BF16, es)
        k.dma("pool", Wout2[:, :, :], w_out_v[:, 4:8, :], writes=[Wout2])
        G2 = k.sb("G2", [128, 512], BF16, es)
        k.dma("pool", G2[:, :], g2_d, writes=[G2])
        lnwr = k.sb("lnwr", [64, 512], F32, es)
        lnbr = k.sb("lnbr", [64, 512], F32, es)
        rkr = k.sb("rkr", [64, 512], F32, es)
        row_bc("ln_w", lnwr, 64)
        row_bc("ln_b", lnbr, 64)
        row_bc("r_k_row", rkr, 64)
        sg = k.sb("sgs", [128, 64], BF16, es)
        tokq = k.sb("tokq", [64, 6, 512], F32, es)
        rq = k.sb("rq", [128, 6, 4, 64], F32, es)
        yq = k.sb("yqw", [128, 4, 64], F32, es)
        sa = k.sb("sa", [128, 64], F32, es)
        yw = k.sb("yw", [64, 512], F32, es)
        ysq = k.sb("ysqs", [64, 512], F32, es)
        st8 = k.sb("st8s", [64, 6, 8], F32, es)
        gtok = k.sb("gtok", [64, 512], F32, es)
        ynb = k.sb("ynbs", [64, 512], BF16, es)
        yrT = k.sb("yrTs", [128, 4, 64], BF16, es)
        scr_q_t = nc.dram_tensor("scr_q", [6, 64, 512], F32)
        scr_w_t = nc.dram_tensor("scr_yw", [64, 512], F32)
        sbq, sbw = Buf("scq"), Buf("scw")
        esa = ExitStack()
        Wir = k.sb("Wir", [128, 8, 1792], BF16, esa)
        for h2 in range(2):
            k.dma("pool", Wir[:, 4 * h2:4 * h2 + 4, :], w_in_v[:, 4 * h2:4 * h2 + 4, 1544:3336], writes=[Wir])
        W2A2 = k.sb("W2A2", [128, 512], BF16, esa)
        k.dma("pool", W2A2[:, :], w2a2_d, writes=[W2A2])
        bon = k.sb("bon", [128, 128], F32, esa)
        o_, s_ = KO2["bones"]
        k.dma("sp", bon[:, :], consts2_d[:, o_:o_ + s_], writes=[bon])
        omm = k.sb("omm", [128, 14], F32, esa)
        k.ts(omm[:, :], col("mu", 0, 14), -1.0, 1.0, mult, add, [cols], [omm])
        usb = [k.sb("usbs", [128, 80], F32, esa) for _ in range(2)]
        tmpm = k.sb("tmpms", [128, 64], F32, esa)
        carry = k.sb("carrys", [128, 14, 16], F32, esa)
        k.dma("sp", carry[:, :, :], st_shift_d, writes=[carry])
        tlla = k.sb("tllas", [128, 64], BF16, esa)
        f = lambda nm: k.sb(nm, [128, 64], F32, esa)
        um_r, um_k, um_v, sigw, a_t, kkn, t1, t2, wdec = [f(n_) for n_ in
                                                          "um_r um_k um_v sigw a_t kkn t1 t2 wdec".split()]
        n = 64
        S = 16
        c0 = 2048
        hbufs = [hTb[16]]

        def proj_chunk_s(c, dst, ucount):
            bi_ = ucount % 2
            u_ = usb[ucount % 2]
            for kk in range(8):
                k.mm(bank(bi_)[:, 0:n], Wir[:, kk, 128 * c:128 * (c + 1)], hT[:, kk, c0:c0 + n], kk == 0, kk == 7,
                     [Wir] + hbufs, [pb[bi_]])
            k.act(u_[:, S:S + n], bank(bi_)[:, 0:n], AF.Copy, [pb[bi_]], [u_])
            k.cp(u_[:, 0:S], carry[:, c, 0:S], [carry], [u_], eng="pool")
            k.act(tmpm[:, 0:n], u_[:, 0:n], AF.Identity, [u_, cols], [tmpm], scale=col("mu", c))
            k.stt(dst[:, 0:n], u_[:, S:S + n], omm[:, c:c + 1], tmpm[:, 0:n], mult, add, [u_, omm, tmpm], [dst])
            k.cp(carry[:, c, 0:S], u_[:, n:n + S], [u_], [carry], eng="pool")

        uc = 0
        proj_chunk_s(12, um_v, uc); uc += 1
        k.act(tlla[0:64, :], um_v[0:64, :], AF.Tanh, [um_v], [tlla])
        k.act(tlla[64:128, :], um_v[64:128, :], AF.Copy, [um_v], [tlla])
        proj_chunk_s(13, um_v, uc); uc += 1
        k.act(sg[:, :], um_v[:, :], AF.Sigmoid, [um_v], [sg])
        for p in range(4):
            pc = slice(128 * p, 128 * (p + 1))
            k.mm(bank(2)[:, 0:n], W2A2[0:64, pc], tlla[0:64, :], True, True, [W2A2, tlla], [pb[2]])
            k.mm(bank(3)[:, 0:n], W2A2[64:128, pc], tlla[64:128, :], True, True, [W2A2, tlla], [pb[3]])
            k.act(sigw[:, :], bank(2)[:, 0:n], AF.Sigmoid, [pb[2], cols], [sigw], bias=col("w0", p))
            k.act(a_t[:, :], bank(3)[:, 0:n], AF.Sigmoid, [pb[3], cols], [a_t], bias=col("a0", p))
            proj_chunk_s(p, um_r, uc); uc += 1
            proj_chunk_s(4 + p, um_k, uc); uc += 1
            proj_chunk_s(8 + p, um_v, uc); uc += 1
            k.ts(t1[:, :], um_k[:, :], col("k_k", p), None, mult, None, [um_k, cols], [t1])
            k.tt(t2[:, :], t1[:, :], t1[:, :], mult, [t1], [t2])
            k.mm(bank(2)[:, 0:n], bon[:, :], t2[:, :], True, True, [bon, t2], [pb[2]])
            k.act(t2[:, :], bank(2)[:, 0:n], AF.Sqrt, [pb[2]], [t2])
            k.ts(t2[:, :], t2[:, :], 1e-12, None, ALU.max, None, [t2], [t2])
            k.recip(t2[:, :], t2[:, :], [t2], [t2])
            k.tt(kkn[:, :], t1[:, :], t2[:, :], mult, [t1, t2], [kkn])
            k.ts(t1[:, :], a_t[:, :], -1.0, col("k_a", p), add, mult, [a_t, cols], [t1])
            k.stt(t1[:, :], t1[:, :], 1.0, um_k[:, :], add, mult, [t1, um_k], [t1])
            k.tt(t2[:, :], a_t[:, :], kkn[:, :], mult, [a_t, kkn], [t2])
            k.act(wdec[:, :], sigw[:, :], AF.Exp, [sigw], [wdec], scale=-C0)
            srcs = [um_r, wdec, t1, um_v, kkn, t2]
            for q, sb_ in enumerate(srcs):
                bk = 4 + q // 3
                k.tr(bank(bk)[0:64, 128 * (q % 3):128 * (q % 3 + 1)], sb_[:, :], ident, [sb_, consts], [pb[bk]])
            for hq in range(2):
                k.act(tokq[:, 3 * hq:3 * hq + 3, pc], bank(4 + hq)[0:64, 0:384].rearrange("p (q d) -> p q d", q=3),
                      AF.Copy, [pb[4 + hq]], [tokq])
        k.mm(bank(2)[0:64, :], sg[:, :], G2[:, :], True, True, [sg, G2], [pb[2]])
        k.act(gtok[:, :], bank(2)[0:64, :], AF.Copy, [pb[2]], [gtok])
        k.dma("sp", o_shift_s, carry[:, :, :], reads=[carry])
        k.dma("sp", scr_q_t.ap().rearrange("q t d -> t q d"), tokq[:, :, :], reads=[tokq], writes=[sbq])
        k.dma("sp", rq[:, :, :, :], bass.AP(scr_q_t, 0, [[64, 128], [32768, 6], [8192, 4], [1, 64]]), reads=[sbq],
              writes=[rq])
        k.barrier()
        esa.close()
        Sw = k.sb("Sw", [128, 4096], F32, es)
        tmpW = k.sb("tmpW", [128, 4096], F32, es)
        tmpV = k.sb("tmpV", [128, 4096], F32, es)
        k.dma("sp", Sw[:, :], st_wkv_d, writes=[Sw])
        m3 = lambda ap: ap.rearrange("p (i j) -> p i j", j=64)
        for l in range(4):
            k.tt(m3(tmpV[:, :]), bc(rq[:, 3, l, :], 2, 64), bc(rq[:, 2, l, :], 1, 64), mult, [rq], [tmpV], eng="pool")
            k.tt(m3(tmpW[:, :]), m3(Sw[:, :]), bc(rq[:, 4, l, :], 1, 64), mult, [Sw, rq], [tmpW])
            k.red(sa[:, :], m3(tmpW[:, :]), add, [tmpW], [sa], negate=True)
            k.tt(m3(Sw[:, :]), m3(Sw[:, :]), bc(rq[:, 1, l, :], 1, 64), mult, [Sw, rq], [Sw])
            k.tt(m3(tmpW[:, :]), bc(sa[:, :], 2, 64), bc(rq[:, 5, l, :], 1, 64), mult, [sa, rq], [tmpW])
            k.tt(Sw[:, :], Sw[:, :], tmpW[:, :], add, [Sw, tmpW], [Sw])
            k.tt(Sw[:, :], Sw[:, :], tmpV[:, :], add, [Sw, tmpV], [Sw])
            k.tt(m3(tmpW[:, :]), m3(Sw[:, :]), bc(rq[:, 0, l, :], 1, 64), mult, [Sw, rq], [tmpW])
            k.red(yq[:, l, :], m3(tmpW[:, :]), add, [tmpW], [yq])
        k.dma("sp", o_wkv_s, Sw[:, :], reads=[Sw])
        k.dma("sp", bass.AP(scr_w_t, 0, [[64, 128], [8192, 4], [1, 64]]), yq[:, :, :], reads=[yq], writes=[sbw])
        k.dma("sp", yw[:, :], scr_w_t.ap(), reads=[sbw], writes=[yw])
        y8 = lambda ap: ap.rearrange("p (h d) -> p h d", h=8)
        k.red(st8[:, 0, :], y8(yw[:, :]), add, [yw], [st8])
        k.tt(ysq[:, :], yw[:, :], yw[:, :], mult, [yw], [ysq])
        k.red(st8[:, 1, :], y8(ysq[:, :]), add, [ysq], [st8])
        k.ts(st8[:, 2, :], st8[:, 0, :], 1.0 / 64, None, mult, None, [st8], [st8])
        k.tt(st8[:, 4, :], st8[:, 2, :], st8[:, 2, :], mult, [st8], [st8])
        k.stt(st8[:, 3, :], st8[:, 1, :], 1.0 / 64, st8[:, 4, :], mult, sub, [st8], [st8])
        k.act(st8[:, 3, :], st8[:, 3, :], AF.Sqrt, [st8, epsc], [st8], bias=epsc[0:64, 1:2])
        k.recip(st8[:, 3, :], st8[:, 3, :], [st8], [st8])
        k.tt(y8(yw[:, :]), y8(yw[:, :]), bc(st8[:, 2, :], 2, 64), sub, [yw, st8], [yw])
        k.tt(y8(yw[:, :]), y8(yw[:, :]), bc(st8[:, 3, :], 2, 64), mult, [yw, st8], [yw])
        k.tt(yw[:, :], yw[:, :], lnwr[:, :], mult, [yw, lnwr], [yw])
        k.tt(yw[:, :], yw[:, :], lnbr[:, :], add, [yw, lnbr], [yw])
        k.tt(ysq[:, :], tokq[:, 0, :], tokq[:, 2, :], mult, [tokq], [ysq])
        k.tt(ysq[:, :], ysq[:, :], rkr[:, :], mult, [ysq, rkr], [ysq])
        k.red(st8[:, 5, :], y8(ysq[:, :]), add, [ysq], [st8])
        k.tt(y8(ysq[:, :]), y8(tokq[:, 3, :]), bc(st8[:, 5, :], 2, 64), mult, [tokq, st8], [ysq])
        k.tt(yw[:, :], yw[:, :], ysq[:, :], add, [yw, ysq], [yw])
        k.tt(yw[:, :], yw[:, :], gtok[:, :], mult, [yw, gtok], [yw])
        if dbg:
            k.dma("sp", dbg_d["yrws"], yw[:, :], reads=[yw])
        k.cp(ynb[:, :], yw[:, :], [yw], [ynb])
        pty = bank_bf(6)[:, 0:256].rearrange("p (c t) -> p c t", c=4)
        for c in range(4):
            k.tr(pty[:, c, :], ynb[:, 128 * c:128 * (c + 1)], identb[0:64, 0:64], [ynb, identb], [pb[6]])
        k.act(yrT[:, :, :], pty[:, :, :], AF.Copy, [pb[6]], [yrT])
        for hf in range(2):
            bk = hf
            for kk in range(4):
                k.mm(bank(bk)[0:64, :], yrT[:, kk, :], Wout2[:, kk, 512 * hf:512 * (hf + 1)], kk == 0, kk == 3,
                     [yrT, Wout2], [pb[bk]])
            k.tt(x_tok[0:64, 16, 512 * hf:512 * (hf + 1)], bank(bk)[0:64, :], x_tok[0:64, 16, 512 * hf:512 * (hf + 1)],
                 add, [pb[bk], xb[16]], [xb[16]])
        k.barrier()

    if dbg:
        for blk in range(17):
            rows = blk_rows(blk)
            k.dma("sp", dbg_d["x1"][blk * 128:blk * 128 + rows, :], x_tok[0:rows, blk, :], reads=[xb[blk]])

    if stop == "C":
        k.finish()
        return nc
    with ExitStack() as es:
        KT = k.sb("KT", [128, 8, 256], BF16, es)
        Vtok = k.sb("Vtok", [128, 2, 1024], BF16, es)
        junk = k.sb("junk", [128, 1024], BF16, es)
        hb2 = [k.sb("hb", [128, 1024], BF16, es) for _ in range(2)]
        onesbf = k.sb("onesbf", [128, 128], BF16, es)
        k.memset(onesbf[:, :], 1.0, [onesbf])
        with ExitStack() as es0:
            Wk, WkB = load_weight("Wk", wk_d.rearrange("(c p) n -> p c n", p=128), 1024,
                                  [(256 * c_, 256 * c_ + 256) for c_ in range(4)], es0)
            Wv, WvB = load_weight("Wv", wv_d.rearrange("(c p) n -> p c n", p=128), 1024, [(0, 512), (512, 1024)], es0)
            memtok = k.sb("memtok", [128, 2, 1024], F32, es0)
            mT = k.sb("mT", [128, 8, 256], BF16, es0)
            mss = k.sb("mss", [128, 2, 4], F32, es0)
            kvst = [k.sb("kvst", [128, 1024], F32, es0) for _ in range(2)]
            row_bc("mem_norm", gbc)
            k.dma("sp", memtok[:, :, :], mem_d.rearrange("(b p) d -> p b d", p=128), writes=[memtok])
            for mb in range(2):
                hb = hb2[mb]
                k.act(junk[:, :], memtok[:, mb, :], AF.Square, [memtok], [junk, mss], accum_out=mss[:, mb, 0:1])
                k.act(mss[:, mb, 1:2], mss[:, mb, 0:1], AF.Sqrt, [mss, epsc], [mss], scale=1.0 / 1024, bias=epsc[:, 0:1])
                k.recip(mss[:, mb, 2:3], mss[:, mb, 1:2], [mss], [mss])
                k.stt(hb[:, :], memtok[:, mb, :], mss[:, mb, 2:3], gbc[:, :], mult, mult, [memtok, mss, gbc], [hb])
                pt = bank_bf(6)[:, :].rearrange("p (c t) -> p c t", c=8)
                for c in range(8):
                    k.tr(pt[:, c, :], hb[:, c * 128:(c + 1) * 128], identb[:, :], [hb, identb], [pb[6]])
                k.act(mT[:, :, mb * 128:(mb + 1) * 128], pt[:, :, :], AF.Copy, [pb[6]], [mT])
            for c in range(8):
                bi_ = c % 2
                for kk in range(8):
                    k.mm(bank(bi_)[:, 0:256], Wk[:, kk, 128 * c:128 * (c + 1)], mT[:, kk, :], kk == 0, kk == 7,
                         WkB(128 * c, 128 * c + 128) + [mT], [pb[bi_]])
                k.act(KT[:, c, :], bank(bi_)[:, 0:256], AF.Copy, [pb[bi_]], [KT])
            ci = 0
            for W_, WB_, od, isv in ((Wk, WkB, o_mk, False), (Wv, WvB, o_mv, True)):
                for mb in range(2):
                    st_ = kvst[ci % 2]
                    ci += 1
                    for hf in range(2):
                        bk = 2 + hf
                        for kk in range(8):
                            k.mm(bank(bk)[:, :], mT[:, kk, mb * 128:(mb + 1) * 128], W_[:, kk, 512 * hf:512 * (hf + 1)],
                                 kk == 0, kk == 7, WB_(512 * hf, 512 * hf + 512) + [mT], [pb[bk]])
                        k.act(st_[:, 512 * hf:512 * (hf + 1)], bank(bk)[:, :], AF.Copy, [pb[bk]], [st_])
                        if isv:
                            k.cp(Vtok[:, mb, 512 * hf:512 * (hf + 1)], st_[:, 512 * hf:512 * (hf + 1)], [st_], [Vtok])
                    k.dma("sp", od[mb * 128:(mb + 1) * 128, :], st_[:, :], reads=[st_])
            k.barrier()
        Wq, WqB = load_weight("Wq", wq_d.rearrange("(c p) n -> p c n", p=128), 1024,
                              [(256 * c_, 256 * c_ + 256) for c_ in range(4)], es)
        Wo, WoB = load_weight("Wo", wo_d.rearrange("(c p) n -> p c n", p=128), 1024, [(0, 512), (512, 1024)], es)
        row_bc("norm_xa", gbc)
        for blk in range(17):
            norm_T(blk, es, (junk, hb2[blk % 2]))
        qT = k.sb("qT", [128, 8, 512], BF16, es)
        oT = k.sb("oT", [128, 8, 512], BF16, es)
        pT = [k.sb("pT", [128, 512], BF16, es) for _ in range(2)]
        rden = k.sb("rden", [128, 512], F32, es)
        NKV = 2
        KTs = [k.sb("KTs", [128, 8, 256], BF16, es) for _ in range(NKV)]
        Kst = [k.sb("Kst", [128, 2, 1024], BF16, es) for _ in range(NKV)]
        Vst = [k.sb("Vst", [128, 2, 1024], BF16, es) for _ in range(NKV)]

        def attend(KT_, V_, qcols, ocols, n, rd):
            for hd in range(4):
                for mb in range(2):
                    bk = 2 + mb
                    for j in range(2):
                        k.mm(bank(bk)[:, 0:n], KT_[:, 2 * hd + j, mb * 128:(mb + 1) * 128], qT[:, 2 * hd + j, qcols],
                             j == 0, j == 1, rd + [qT], [pb[bk]])
                    k.act(pT[mb][:, 0:n], bank(bk)[:, 0:n], AF.Exp, [pb[bk]], [pT[mb]])
                for mb in range(2):
                    k.mm(bank(4)[:, 0:n], onesbf[:, :], pT[mb][:, 0:n], mb == 0, mb == 1, [onesbf, pT[mb]], [pb[4]])
                k.recip(rden[:, 0:n], bank(4)[:, 0:n], [pb[4]], [rden])
                for j in range(2):
                    dc = 2 * hd + j
                    for mb in range(2):
                        k.mm(bank(5)[:, 0:n], V_[:, mb, 128 * dc:128 * (dc + 1)], pT[mb][:, 0:n], mb == 0, mb == 1,
                             rd + [pT[mb]], [pb[5]])
                    k.tt(oT[:, dc, ocols], bank(5)[:, 0:n], rden[:, 0:n], mult, [pb[5], rden], [oT])

        for t in range(5):
            if stop == "D1p" and t == 4:
                break
            if stop == "D0":
                break
            n = 512 if t < 4 else 64
            c0 = t * 512
            nb = 4 if t < 4 else 1
            hbufs = hTb[4 * t:4 * t + nb]
            for c in range(8):
                bi_ = c % 2
                for kk in range(8):
                    k.mm(bank(bi_)[:, 0:n], Wq[:, kk, 128 * c:128 * (c + 1)], hT[:, kk, c0:c0 + n], kk == 0, kk == 7,
                         WqB(128 * c, 128 * c + 128) + hbufs, [pb[bi_]])
                k.act(qT[:, c, 0:n], bank(bi_)[:, 0:n], AF.Copy, [pb[bi_]], [qT], scale=0.0625)
            if t < 4:
                attend(KT, Vtok, slice(0, n), slice(0, n), n, [KT, Vtok])
            else:
                for s_ in range(16):
                    Ks_, Vs_, KTs_ = Kst[s_ % NKV], Vst[s_ % NKV], KTs[s_ % NKV]
                    k.dma("pool", Ks_[:, :, :], ck_d[s_].rearrange("(b p) d -> p b d", p=128), writes=[Ks_])
                    k.dma("pool", Vs_[:, :, :], cv_d[s_].rearrange("(b p) d -> p b d", p=128), writes=[Vs_])
                    for mb in range(2):
                        ptk = bank_bf(6 + mb)[:, :].rearrange("p (c t) -> p c t", c=8)
                        for c in range(8):
                            k.tr(ptk[:, c, :], Ks_[:, mb, 128 * c:128 * (c + 1)], identb[:, :], [Ks_, identb],
                                 [pb[6 + mb]])
                        k.act(KTs_[:, :, mb * 128:(mb + 1) * 128], ptk[:, :, :], AF.Copy, [pb[6 + mb]], [KTs_])
                    sel = slice(s_, 64, 16)
                    for hd in range(4):
                        for mb in range(2):
                            for j in range(2):
                                k.mm(bank(2 + mb)[:, 4 * hd:4 * hd + 4], KTs_[:, 2 * hd + j, mb * 128:(mb + 1) * 128],
                                     qT[:, 2 * hd + j, sel], j == 0, j == 1, [KTs_, qT], [pb[2 + mb]])
                    for mb in range(2):
                        k.act(pT[mb][:, 0:16], bank(2 + mb)[:, 0:16], AF.Exp, [pb[2 + mb]], [pT[mb]])
                    for mb in range(2):
                        k.mm(bank(4)[:, 0:16], onesbf[:, :], pT[mb][:, 0:16], mb == 0, mb == 1, [onesbf, pT[mb]], [pb[4]])
                    k.recip(rden[:, 0:16], bank(4)[:, 0:16], [pb[4]], [rden])
                    for dc in range(8):
                        hd = dc // 2
                        for mb in range(2):
                            k.mm(bank(5)[:, 4 * dc:4 * dc + 4], Vs_[:, mb, 128 * dc:128 * (dc + 1)],
                                 pT[mb][:, 4 * hd:4 * hd + 4], mb == 0, mb == 1, [Vs_, pT[mb]], [pb[5]])
                    k.tt(oT[:, :, sel].rearrange("p (h j) t -> p h j t", j=2),
                         bank(5)[:, 0:32].rearrange("p (h j t) -> p h j t", h=4, j=2),
                         bc(rden[:, 0:16].rearrange("p (h t) -> p h t", h=4), 2, 2), mult, [pb[5], rden], [oT])
            for bi in range(nb):
                blk = 4 * t + bi
                rows = blk_rows(blk)
                cs = slice(bi * 128, bi * 128 + rows)
                for kk in range(8):
                    for hf in range(2):
                        k.mm(bank(hf)[0:rows, :], oT[:, kk, cs], Wo[:, kk, 512 * hf:512 * (hf + 1)], kk == 0, kk == 7,
                             [oT] + WoB(512 * hf, 512 * hf + 512), [pb[hf]])
                for hf in range(2):
                    k.tt(x_tok[0:rows, blk, 512 * hf:512 * (hf + 1)], bank(hf)[0:rows, :],
                         x_tok[0:rows, blk, 512 * hf:512 * (hf + 1)], add, [pb[hf], xb[blk]], [xb[blk]])
        k.barrier()

    if dbg:
        for blk in range(17):
            rows = blk_rows(blk)
            k.dma("sp", dbg_d["x2"][blk * 128:blk * 128 + rows, :], x_tok[0:rows, blk, :], reads=[xb[blk]])

    if stop in ("D", "D0", "D1p"):
        k.finish()
        return nc
    with ExitStack() as es:
        junk = k.sb("junk", [128, 1024], BF16, es)
        hb2 = [k.sb("hb", [128, 1024], BF16, es) for _ in range(2)]
        row_bc("norm_ffn", gbc)
        for blk in range(17):
            norm_T(blk, es, (junk, hb2[blk % 2]))
        k.barrier()
    with ExitStack() as es:
        G = 3
        Wup = [k.sb("Wup", [128, 8, 2 * G * 128], BF16, es) for _ in range(2)]
        Wdn = [k.sb("Wdn", [128, G, 1024], BF16, es) for _ in range(2)]
        wup_v = wup_d.rearrange("(c p) n -> p c n", p=128)
        wdn_v = wdn_d.rearrange("(f p) n -> p f n", p=128)
        pre = [k.sb("pre", [128, 2050], F32, es) for _ in range(2)]
        accf = [k.sb("accf", [128, 2048], F32, es) for _ in range(2)]
        sgt = accf[0]
        actb = k.sb("actb", [128, G, 2112], BF16, es)
        pres = [k.sb("pres", [128, 96], F32, es) for _ in range(2)]
        carF = k.sb("carF", [128, 44, 2], F32, es)
        carS = k.sb("carS", [128, 44, 2, 16], F32, es)
        k.dma("sp", carS[:, :, :, :], st_ffn_d, writes=[carS])
        for pr in pre:
            k.memset(pr[:, 0:2], 0.0, [pr])
        groups = [(f0, min(G, 22 - f0)) for f0 in range(0, 22, G)]
        pcount = 0
        dcount = 0
        hb_all = hTb[0:16]
        for gi, (f0, gn) in enumerate(groups):
            Wu, Wd = Wup[gi % 2], Wdn[gi % 2]
            k.dma("pool", Wu[:, :, 0:gn * 128], wup_v[:, :, f0 * 128:(f0 + gn) * 128], writes=[Wu])
            k.dma("pool", Wu[:, :, G * 128:G * 128 + gn * 128], wup_v[:, :, 2816 + f0 * 128:2816 + (f0 + gn) * 128],
                  writes=[Wu])
            k.dma("pool", Wd[:, 0:gn, :], wdn_v[:, f0:f0 + gn, :], writes=[Wd])
            for i in range(gn):
                for half in range(2):
                    fc = (f0 + i) + 22 * half
                    wc = half * G * 128 + i * 128
                    pr = pre[pcount % 2]
                    pcount += 1
                    for kk in range(8):
                        for tt_ in range(4):
                            k.mm(bank(tt_)[:, :], Wu[:, kk, wc:wc + 128], hT[:, kk, tt_ * 512:(tt_ + 1) * 512], kk == 0,
                                 kk == 7, [Wu] + hb_all, [pb[tt_]])
                    for h2 in range(2):
                        k.act(pr[:, 2 + 1024 * h2:2 + 1024 * (h2 + 1)], psA[h2][:, :, :].rearrange("p a b -> p (a b)"),
                              AF.Copy, [pb[2 * h2], pb[2 * h2 + 1]], [pr])
                    ac = accf[half]
                    k.act(ac[:, :], pr[:, 0:2048], AF.Identity, [pr, cols], [ac], scale=col("ffn_cw", 3 * fc),
                          bias=col("ffn_cb", fc))
                    for j in range(1, 3):
                        k.stt(ac[:, :], pr[:, j:j + 2048], col("ffn_cw", 3 * fc + j), ac[:, :], mult, add,
                              [pr, cols, ac], [ac])
                    k.cp(carF[:, fc, :], pr[:, 2048:2050], [pr], [carF], eng="pool")
                k.act(sgt[:, :], accf[0][:, :], AF.Silu, [accf[0]], [accf[0]])
                k.tt(actb[:, i, 0:2048], sgt[:, :], accf[1][:, :], mult, [accf[0], accf[1]], [actb])
                n, S, c0 = 64, 16, 2048
                for half in range(2):
                    fc = (f0 + i) + 22 * half
                    wc = half * G * 128 + i * 128
                    prs = pres[half]
                    bk = 6 + half
                    for kk in range(8):
                        k.mm(bank(bk)[:, 0:n], Wu[:, kk, wc:wc + 128], hT[:, kk, c0:c0 + n], kk == 0, kk == 7,
                             [Wu, hTb[16]], [pb[bk]])
                    k.act(prs[:, 32:96], bank(bk)[:, 0:n], AF.Copy, [pb[bk]], [prs])
                    k.cp(prs[:, 0:32], carS[:, fc, :, :].rearrange("p j s -> p (j s)"), [carS], [prs], eng="pool")
                    ac = accf[half]
                    k.act(ac[:, 0:n], prs[:, 0:n], AF.Identity, [prs, cols], [ac], scale=col("ffn_cw", 3 * fc),
                          bias=col("ffn_cb", fc))
                    for j in range(1, 3):
                        k.stt(ac[:, 0:n], prs[:, j * S:j * S + n], col("ffn_cw", 3 * fc + j), ac[:, 0:n], mult, add,
                              [prs, cols, ac], [ac])
                    k.cp(carS[:, fc, :, :].rearrange("p j s -> p (j s)"), prs[:, n:n + 32], [prs], [carS], eng="pool")
                k.act(sgt[:, 0:n], accf[0][:, 0:n], AF.Silu, [accf[0]], [accf[0]])
                k.tt(actb[:, i, 2048:2112], sgt[:, 0:n], accf[1][:, 0:n], mult, [accf[0], accf[1]], [actb])
            for blk in range(17):
                rows = blk_rows(blk)
                cs = slice(blk * 128, blk * 128 + rows)
                bb = 4 + 2 * (dcount % 2)
                dcount += 1
                for i in range(gn):
                    for hf in range(2):
                        k.mm(bank(bb + hf)[0:rows, :], actb[:, i, cs], Wd[:, i, 512 * hf:512 * (hf + 1)], i == 0,
                             i == gn - 1, [actb, Wd], [pb[bb + hf]])
                for hf in range(2):
                    k.tt(x_tok[0:rows, blk, 512 * hf:512 * (hf + 1)], bank(bb + hf)[0:rows, :],
                         x_tok[0:rows, blk, 512 * hf:512 * (hf + 1)], add, [pb[bb + hf], xb[blk]], [xb[blk]])
        k.dma("sp", o_ffn_p, carF[:, :, :], reads=[carF])
        k.dma("sp", o_ffn_s, carS[:, :, :, :], reads=[carS])
        k.barrier()

    if stop == "E":
        k.finish()
        return nc
    with ExitStack() as es:
        junk = k.sb("junk", [128, 1024], BF16, es)
        yo = [k.sb("yo", [128, 1024], F32, es) for _ in range(2)]
        row_bc("final_norm", gbc)
        for blk in range(17):
            rows = blk_rows(blk)
            xin = x_tok[0:rows, blk, :]
            ss = ssx[0:rows, blk, :]
            y_ = yo[blk % 2]
            k.act(junk[0:rows, :], xin, AF.Square, [xb[blk]], [junk, ssb[blk]], accum_out=ss[:, 0:1])
            k.act(ss[:, 1:2], ss[:, 0:1], AF.Sqrt, [ssb[blk], epsc], [ssb[blk]], scale=1.0 / 1024, bias=epsc[0:rows, 0:1])
            k.recip(ss[:, 2:3], ss[:, 1:2], [ssb[blk]], [ssb[blk]])
            k.stt(y_[0:rows, :], xin, ss[:, 2:3], gbc[0:rows, :], mult, mult, [xb[blk], ssb[blk], gbc], [y_])
            dst = y_p_d[blk * 128:(blk + 1) * 128, :] if blk < 16 else y_s_d
            k.dma("sp", dst, y_[0:rows, :], reads=[y_])
        k.barrier()
    k.finish()
    return nc


def _consts():
    c = np.zeros((128, NCONST), np.float32)
    c2 = np.zeros((128, NCONST2), np.float32)
    i = np.arange(128)
    o, s = KO["ident"]
    c[:, o:o + s] = np.eye(128)
    o, s = KO["tri"]
    c[:, o:o + s] = (i[:, None] <= i[None, :])
    o, s = KO["lm"]
    c[:, o:o + s] = (i[:, None] > i[None, :])
    o, s = KO["ones"]
    c[:, o:o + s] = 1.0
    s64 = np.arange(64)
    su = (s64[:, None] < s64[None, :]).astype(np.float32)
    iu = (s64[:, None] <= s64[None, :]).astype(np.float32)
    sl = (s64[:, None] > s64[None, :]).astype(np.float32)
    def bd(m_):
        z_ = np.zeros((128, 128), np.float32)
        z_[:64, :64] = m_
        z_[64:, 64:] = m_
        return z_
    o, s = KO2["wmask"]
    c2[:, o:o + s] = np.concatenate([bd(su), bd(iu), bd(su), bd(iu), bd(sl)], axis=1)
    o, s = KO2["i64"]
    c2[:, o:o + s] = np.concatenate([np.eye(64), np.eye(64)], axis=0)
    o, s = KO2["bones"]
    bo = np.zeros((128, 128), np.float32)
    bo[:64, :64] = 1
    bo[64:, 64:] = 1
    c2[:, o:o + s] = bo
    o, s = KO2["scanm"]
    sm = np.ones(512, np.float32)
    sm[::64] = 0
    c2[:, o:o + s] = sm[None, :]
    return c, c2


def _fm(v, nch):
    return np.ascontiguousarray(np.asarray(v, np.float32).reshape(nch, 128).T)


def _prep_shared(inp):
    sh = {}
    f = lambda a: np.ascontiguousarray(np.asarray(a, np.float32))
    sh["w_in"] = f(inp["w_in"][0])
    sh["w_out"] = f(inp["w_out"][0])
    sh["wq"] = f(inp["xa_w_q"][0])
    sh["wk"] = f(inp["xa_w_k"][0])
    sh["wv"] = f(inp["xa_w_v"][0])
    sh["wo"] = f(inp["xa_w_o"][0])
    sh["wup"] = f(inp["ffn_w_up"][0])
    sh["wdn"] = f(inp["ffn_w_down"][0])
    sh["w2a2"] = f(np.concatenate([inp["rwkv_w2"][0], inp["rwkv_a2"][0]], axis=0))
    sh["g2"] = f(inp["rwkv_g2"][0])
    rows = np.zeros((1, NROW), np.float32)
    src = {"norm_mix": inp["norm_mix_w"][0], "norm_xa": inp["norm_xa_w"][0], "mem_norm": inp["mem_norm_w"][0],
           "norm_ffn": inp["norm_ffn_w"][0], "final_norm": inp["final_norm_w"], "ssd_norm": inp["ssd_norm_w"][0],
           "dt_bias": inp["ssd_dt_bias"][0], "a_log": inp["ssd_a_log"][0], "ssd_d": inp["ssd_d"][0],
           "ln_w": inp["rwkv_ln_w"][0], "ln_b": inp["rwkv_ln_b"][0], "r_k_row": inp["rwkv_r_k"][0]}
    for nme, (o, s) in RO.items():
        rows[0, o:o + s] = np.asarray(src[nme], np.float32).reshape(-1)
    sh["rows"] = rows
    cols = np.zeros((128, NCOL), np.float32)

    def put(nme, arr):
        o, s = CO[nme]
        cols[:, o:o + s] = arr.reshape(128, s)

    cw = np.asarray(inp["ssd_conv_w"][0], np.float32)
    put("ssd_cw", np.ascontiguousarray(cw.T.reshape(8, 128, 4).transpose(1, 0, 2)))
    put("ssd_cb", _fm(inp["ssd_conv_b"][0], 8))
    put("mu", _fm(inp["rwkv_mu"][0], 14))
    put("w0", _fm(inp["rwkv_w0"][0], 4))
    put("a0", _fm(inp["rwkv_a0"][0], 4))
    put("k_k", _fm(inp["rwkv_k_k"][0], 4))
    put("k_a", _fm(inp["rwkv_k_a"][0], 4))
    put("r_k", _fm(np.asarray(inp["rwkv_r_k"][0]).reshape(-1), 4))
    fw = np.asarray(inp["ffn_conv_w"][0], np.float32)
    put("ffn_cw", np.ascontiguousarray(fw.T.reshape(44, 128, 3).transpose(1, 0, 2)))
    put("ffn_cb", _fm(inp["ffn_conv_b"][0], 44))
    sh["cols"] = cols
    sh["consts"], sh["consts2"] = _consts()
    return sh


def _prep_core(inp, c):
    f = lambda a: np.ascontiguousarray(np.asarray(a, np.float32))
    sl = slice(16 * c, 16 * c + 16)
    m = {}
    m["xp"] = f(inp["x_prompt"][c])
    m["xs"] = f(np.asarray(inp["x_sample"][sl]).transpose(1, 0, 2).reshape(64, 1024))
    m["mem"] = f(inp["mem_prompt"][c])
    sc = np.asarray(inp["state_ssm_conv"][0, sl])
    m["st_conv"] = f(sc.transpose(2, 1, 0).reshape(8, 128, 3, 16).transpose(1, 0, 2, 3))
    m["st_ssm"] = f(np.asarray(inp["state_ssm"][0, sl]).reshape(128, 8192))
    ss = np.asarray(inp["state_shift"][0, sl])
    m["st_shift"] = f(ss.T.reshape(14, 128, 16).transpose(1, 0, 2))
    m["st_wkv"] = f(np.asarray(inp["state_wkv"][0, sl]).reshape(128, 4096))
    sf = np.asarray(inp["state_ffn_conv"][0, sl])
    m["st_ffn"] = f(sf.transpose(2, 1, 0).reshape(44, 128, 2, 16).transpose(1, 0, 2, 3))
    m["ck"] = f(np.asarray(inp["cache_mem_k"][0, sl]).reshape(16, 256, 1024))
    m["cv"] = f(np.asarray(inp["cache_mem_v"][0, sl]).reshape(16, 256, 1024))
    return m


_NC_CACHE = {}


def _get_nc(stop="all", dbg=False):
    key = (stop, dbg)
    if key not in _NC_CACHE:
        _NC_CACHE[key] = build(stop, dbg)
    return _NC_CACHE[key]


def run_raw(inp, stop="all", dbg=False):
    nc = _get_nc(stop, dbg)
    sh = _prep_shared(inp)
    in_maps = []
    for c in range(NCORES):
        m = dict(sh)
        m.update(_prep_core(inp, c))
        in_maps.append(m)
    res = run_bass_kernel_spmd(nc, in_maps, core_ids=list(range(NCORES)))
    return res.results


def kernel(**inp):
    rs = run_raw(inp)
    f32 = np.float32
    y_p = np.stack([r["y_p"] for r in rs]).astype(f32)
    y_s = np.concatenate([r["y_s"].reshape(4, 16, 1024).transpose(1, 0, 2) for r in rs]).astype(f32)
    conv_p = np.stack([r["o_conv_p"].transpose(2, 1, 0).reshape(3, 1024) for r in rs])[None]
    conv_s = np.concatenate([r["o_conv_s"].transpose(3, 2, 1, 0).reshape(16, 3, 1024) for r in rs])[None]
    ssm_p = np.stack([r["o_ssm_p"].reshape(8, 64, 128) for r in rs])[None]
    ssm_s = np.concatenate([r["o_ssm_s"].reshape(16, 8, 64, 128) for r in rs])[None]
    shift_p = np.stack([r["o_shift_p"].T.reshape(1792) for r in rs])[None]
    shift_s = np.concatenate([r["o_shift_s"].transpose(2, 1, 0).reshape(16, 1792) for r in rs])[None]
    wkv_p = np.stack([r["o_wkv_p"].reshape(2, 64, 4, 64).transpose(2, 0, 3, 1).reshape(8, 64, 64) for r in rs])[None]
    wkv_s = np.concatenate([r["o_wkv_s"].reshape(16, 8, 64, 64) for r in rs])[None]
    ffn_p = np.stack([r["o_ffn_p"].transpose(2, 1, 0).reshape(2, 5632) for r in rs])[None]
    ffn_s = np.concatenate([r["o_ffn_s"].transpose(3, 2, 1, 0).reshape(16, 2, 5632) for r in rs])[None]
    mk = np.stack([r["o_mk"].reshape(256, 4, 256) for r in rs])[None]
    mv = np.stack([r["o_mv"].reshape(256, 4, 256) for r in rs])[None]
    outs = (y_p, y_s, conv_p, conv_s, ssm_p, ssm_s, shift_p, shift_s, wkv_p, wkv_s, ffn_p, ffn_s, mk, mv)
    return tuple(np.ascontiguousarray(o, dtype=f32) for o in outs)
```

```python
import os
import numpy as np
from contextlib import ExitStack
import concourse.bass as bass
import concourse.mybir as mybir
from concourse.bass_utils import run_bass_kernel_spmd

F32 = mybir.dt.float32
BF16 = mybir.dt.bfloat16
AF = mybir.ActivationFunctionType
ALU = mybir.AluOpType
AX = mybir.AxisListType

NCORES = 8
EPS = 1e-6
GN_EPS = 64e-5
C0 = float(np.exp(-0.5))
NDMA = 40

ROWS = [("norm_mix", 1024), ("norm_xa", 1024), ("mem_norm", 1024), ("norm_ffn", 1024), ("final_norm", 1024),
        ("ssd_norm", 512), ("dt_bias", 8), ("a_log", 8), ("ssd_d", 8), ("ln_w", 512), ("ln_b", 512), ("r_k_row", 512)]
COLS = [("ssd_cw", 32), ("ssd_cb", 8), ("mu", 14), ("w0", 4), ("a0", 4), ("k_k", 4), ("k_a", 4), ("r_k", 4),
        ("ffn_cw", 132), ("ffn_cb", 44)]
CONSTS = [("ident", 128), ("tri", 128), ("lm", 128), ("ones", 128)]
CONSTS2 = [("wmask", 640), ("i64", 64), ("bones", 128), ("scanm", 512)]


def _offs(spec):
    o = {}
    p = 0
    for n, s in spec:
        o[n] = (p, s)
        p += s
    return o, p


RO, NROW = _offs(ROWS)
CO, NCOL = _offs(COLS)
KO, NCONST = _offs(CONSTS)
KO2, NCONST2 = _offs(CONSTS2)


class Buf:
    __slots__ = ("name", "w", "r", "psum")

    def __init__(self, name="", psum=False):
        self.name = name
        self.w = None
        self.r = {}
        self.psum = psum


class TT:
    def __init__(self, t, name):
        self.t = t
        self.b = Buf(name)

    def __getitem__(self, idx):
        return self.t[idx]


def _b(x):
    return x if isinstance(x, Buf) else x.b


class View:
    def __init__(self, ap, buf):
        self.ap = ap
        self.b = buf

    def __getitem__(self, idx):
        return self.ap[idx]


class K:
    def __init__(self, nc):
        self.nc = nc
        self.engs = ["pe", "act", "dve", "pool", "sp"]
        self.sem = {e: nc.alloc_semaphore(name=f"s_{e}") for e in self.engs}
        self.cnt = {e: 0 for e in self.engs}
        self.waited = {e: {} for e in self.engs}
        self.prog = {e: [] for e in self.engs}
        self.dsem = [nc.alloc_semaphore(name=f"d{i}") for i in range(NDMA)]
        self.dval = [0] * NDMA
        self.drr = 0
        self.uid = 0
        self.seq = {e: 0 for e in self.LAZY}
        self.marks = {e: [] for e in self.LAZY}
        self.entry = {e: {} for e in self.LAZY}

    def _semh(self, key):
        return self.sem[key] if isinstance(key, str) else self.dsem[key]

    def _deps(self, e, reads, writes):
        need = {}

        def add(ev):
            if ev is None:
                return
            kk, v = ev
            if need.get(kk, 0) < v:
                need[kk] = v

        for b in reads:
            b = _b(b)
            add(b.w)
            if b.psum:
                for kk, v in b.r.items():
                    if kk != e:
                        add((kk, v))
        for b in writes:
            b = _b(b)
            add(b.w)
            for kk, v in b.r.items():
                add((kk, v))
        waits = []
        wd = self.waited[e]
        for kk, v in need.items():
            if kk == "pe" and e == "pe":
                continue
            if kk in self.LAZY:
                v = self.resolve(kk, v)
            if wd.get(kk, 0) < v:
                wd[kk] = v
                waits.append((self._semh(kk), v))
        return waits

    def _mark(self, ev, reads, writes):
        kk, v = ev
        for b in reads:
            b = _b(b)
            if b.r.get(kk, 0) < v:
                b.r[kk] = v
        for b in writes:
            b = _b(b)
            b.w = ev
            b.r = {}

    LAZY = ("pe",)

    def resolve(self, e, seq):
        import bisect
        marks = self.marks[e]
        i = bisect.bisect_left(marks, seq)
        if i < len(marks):
            return i + 1
        idx = self.entry[e][seq]
        w, fn, sem, inc = self.prog[e][idx]
        assert inc == 0 and fn is not None
        self.prog[e][idx] = (w, fn, sem, 1)
        marks.append(seq)
        self.cnt[e] = len(marks)
        return len(marks)

    def op(self, e, fn, reads=(), writes=()):
        waits = self._deps(e, reads, writes)
        if e in self.LAZY:
            self.seq[e] += 1
            ev = (e, self.seq[e])
            self.entry[e][self.seq[e]] = len(self.prog[e])
            self.prog[e].append((waits, fn, self.sem[e], 0))
        else:
            self.cnt[e] += 1
            ev = (e, self.cnt[e])
            self.prog[e].append((waits, fn, self.sem[e], 1))
        self._mark(ev, reads, writes)
        return ev

    def dma(self, q, out, in_, reads=(), writes=(), **kw):
        waits = self._deps(q, reads, writes)
        i = self.drr
        self.drr = (self.drr + 1) % NDMA
        if self.dval[i] > 0:
            wd = self.waited[q]
            if wd.get(i, 0) < self.dval[i]:
                wd[i] = self.dval[i]
                waits.append((self.dsem[i], self.dval[i]))
        self.dval[i] += 16
        ev = (i, self.dval[i])
        self.prog[q].append((waits, lambda eng: eng.dma_start(out=out, in_=in_, **kw), self.dsem[i], 16))
        self._mark(ev, reads, writes)
        return ev

    def barrier(self):
        for e in self.LAZY:
            if self.seq[e] > 0:
                self.resolve(e, self.seq[e])
        for e in self.engs:
            waits = []
            wd = self.waited[e]
            for e2 in self.engs:
                if self.cnt[e2] > 0 and wd.get(e2, 0) < self.cnt[e2] and e2 != e:
                    wd[e2] = self.cnt[e2]
                    waits.append((self.sem[e2], self.cnt[e2]))
            for i in range(NDMA):
                if self.dval[i] > 0 and wd.get(i, 0) < self.dval[i]:
                    wd[i] = self.dval[i]
                    waits.append((self.dsem[i], self.dval[i]))
            if waits:
                self.prog[e].append((waits, None, None, 0))

    def finish(self):
        self.barrier()
        print("instr counts", {e: (len(self.prog[e]), sum(len(w) for w, _, _, _ in self.prog[e])) for e in self.engs})
        nc = self.nc
        prog = self.prog

        def replay(name, eng):
            for waits, fn, sem, inc in prog[name]:
                for s, v in waits:
                    eng.wait_ge(s, v)
                if fn is not None:
                    ins = fn(eng)
                    if inc:
                        ins.then_inc(sem, inc)

        with nc.Block() as block:
            @block.tensor
            def _(eng):
                replay("pe", eng)

            @block.scalar
            def _(eng):
                replay("act", eng)

            @block.vector
            def _(eng):
                replay("dve", eng)

            @block.gpsimd
            def _(eng):
                replay("pool", eng)

            @block.sync
            def _(eng):
                replay("sp", eng)

    def sb(self, name, shape, dtype, es=None):
        self.uid += 1
        nm = f"{name}_{self.uid}"
        if es is None:
            t = self.nc.alloc_sbuf_tensor(nm, list(shape), dtype)
        else:
            t = es.enter_context(self.nc.sbuf_tensor(nm, list(shape), dtype))
        return TT(t, nm)

    def mm(self, out, lhsT, rhs, start, stop, reads, writes):
        return self.op("pe", lambda e: e.matmul(out, lhsT, rhs, start=start, stop=stop), reads, writes)

    def tr(self, out, in_, ident, reads, writes):
        return self.op("pe", lambda e: e.transpose(out, in_, ident), reads, writes)

    def act(self, out, in_, func, reads, writes, scale=1.0, bias=0.0, accum_out=None):
        if accum_out is None:
            return self.op("act", lambda e: e.activation(out, in_, func, bias=bias, scale=scale), reads, writes)
        return self.op("act", lambda e: e.activation(out, in_, func, bias=bias, scale=scale, accum_out=accum_out),
                       reads, writes)

    def tt(self, out, in0, in1, op, reads, writes, eng="dve"):
        return self.op(eng, lambda e: e.tensor_tensor(out, in0, in1, op), reads, writes)

    def ts(self, out, in0, s1, s2, op0, op1, reads, writes, eng="dve"):
        if op1 is None:
            return self.op(eng, lambda e: e.tensor_scalar(out, in0, s1, None, op0), reads, writes)
        return self.op(eng, lambda e: e.tensor_scalar(out, in0, s1, s2, op0, op1), reads, writes)

    def stt(self, out, in0, scalar, in1, op0, op1, reads, writes):
        return self.op("dve", lambda e: e.scalar_tensor_tensor(out, in0, scalar, in1, op0, op1), reads, writes)

    def cp(self, out, in_, reads, writes, eng="dve"):
        return self.op(eng, lambda e: e.tensor_copy(out, in_), reads, writes)

    def recip(self, out, in_, reads, writes):
        return self.op("dve", lambda e: e.reciprocal(out, in_), reads, writes)

    def red(self, out, in_, op, reads, writes, negate=False):
        return self.op("dve", lambda e: e.tensor_reduce(out, in_, AX.X, op, negate=negate), reads, writes)

    def carve(self, parent, pieces):
        out = []
        off = 0
        for nbytes, dt_ in pieces:
            ap = parent.t[:, off // 4:(off + nbytes) // 4]
            if dt_ == BF16:
                ap = ap.bitcast(BF16)
            b = Buf(parent.b.name + "_v")
            b.w = parent.b.w
            b.r = dict(parent.b.r)
            out.append(View(ap, b))
            off += nbytes
        assert off <= 2048
        return out

    def merge(self, parent, views):
        pr = parent.b.r
        for v in views:
            evs = list(v.b.r.items())
            if v.b.w is not None:
                evs.append(v.b.w)
            for kk, val in evs:
                if pr.get(kk, 0) < val:
                    pr[kk] = val

    def scan(self, out, d0, d1, init, op0, op1, reads, writes):
        return self.op("dve", lambda e: e.tensor_tensor_scan(out, d0, d1, init, op0, op1), reads, writes)

    def memset(self, ap, val, writes, eng="dve"):
        return self.op(eng, lambda e: e.memset(ap, val), (), writes)


def bc(ap, axis, n):
    a = ap.unsqueeze(axis)
    shp = list(a.shape)
    shp[axis] = n
    return a.broadcast_to(shp)


def build(stop="all", dbg=False):
    nc = bass.Bass("TRN2", target_bir_lowering=False)
    k = K(nc)
    DBGU = int(os.environ.get("DBGU", "9"))
    skipBC = stop.startswith("x")
    stop = stop.lstrip("x")
    NTB = 0 if skipBC else 4
    mult, add, sub = ALU.mult, ALU.add, ALU.subtract

    def din(name, shape):
        return nc.dram_tensor(name, list(shape), F32, kind="ExternalInput").ap()

    def dout(name, shape):
        return nc.dram_tensor(name, list(shape), F32, kind="ExternalOutput").ap()

    xp_d = din("xp", [2048, 1024])
    xs_d = din("xs", [64, 1024])
    mem_d = din("mem", [256, 1024])
    st_conv_d = din("st_conv", [128, 8, 3, 16])
    st_ssm_d = din("st_ssm", [128, 8192])
    st_shift_d = din("st_shift", [128, 14, 16])
    st_wkv_d = din("st_wkv", [128, 4096])
    st_ffn_d = din("st_ffn", [128, 44, 2, 16])
    ck_d = din("ck", [16, 256, 1024])
    cv_d = din("cv", [16, 256, 1024])
    w_in_d = din("w_in", [1024, 3336])
    w_out_d = din("w_out", [1024, 1024])
    wq_d = din("wq", [1024, 1024])
    wk_d = din("wk", [1024, 1024])
    wv_d = din("wv", [1024, 1024])
    wo_d = din("wo", [1024, 1024])
    wup_d = din("wup", [1024, 5632])
    wdn_d = din("wdn", [2816, 1024])
    w2a2_d = din("w2a2", [128, 512])
    g2_d = din("g2", [128, 512])
    rows_d = din("rows", [1, NROW])
    cols_d = din("cols", [128, NCOL])
    consts_d = din("consts", [128, NCONST])
    consts2_d = din("consts2", [128, NCONST2])

    y_p_d = dout("y_p", [2048, 1024])
    y_s_d = dout("y_s", [64, 1024])
    o_conv_p = dout("o_conv_p", [128, 8, 3])
    o_conv_s = dout("o_conv_s", [128, 8, 3, 16])
    o_ssm_p = dout("o_ssm_p", [512, 128])
    o_ssm_s = dout("o_ssm_s", [128, 8192])
    o_shift_p = dout("o_shift_p", [128, 14])
    o_shift_s = dout("o_shift_s", [128, 14, 16])
    o_wkv_p = dout("o_wkv_p", [128, 4, 64])
    o_wkv_s = dout("o_wkv_s", [128, 4096])
    o_ffn_p = dout("o_ffn_p", [128, 44, 2])
    o_ffn_s = dout("o_ffn_s", [128, 44, 2, 16])
    o_mk = dout("o_mk", [256, 1024])
    o_mv = dout("o_mv", [256, 1024])
    dbg_d = {}
    if dbg:
        dbg_d["yssd"] = dout("dbg_yssd", [2112, 512])
        dbg_d["yrwT"] = dout("dbg_yrwT", [128, 4, 2112])
        dbg_d["x1"] = dout("dbg_x1", [2112, 1024])
        dbg_d["x2"] = dout("dbg_x2", [2112, 1024])
        dbg_d["yrws"] = dout("dbg_yrws", [64, 512])

    x_tok = k.sb("x_tok", [128, 17, 1024], F32)
    xb = [Buf(f"x{b}") for b in range(17)]
    hT = k.sb("hT", [128, 8, 2112], BF16)
    hTb = [Buf(f"hT{b}") for b in range(17)]
    consts = k.sb("consts", [128, NCONST], F32)
    cols = k.sb("cols", [128, NCOL], F32)
    identb = k.sb("identb", [128, 128], BF16)
    gbc = k.sb("gbc", [128, 1024], F32)
    ssx = k.sb("ssx", [128, 17, 4], F32)
    ssb = [Buf(f"ss{b}") for b in range(17)]

    def cst(name, rows=slice(0, 128)):
        o, s = KO[name]
        return consts[rows, o:o + s]

    def col(name, j0=0, j1=None):
        o, s = CO[name]
        if j1 is None:
            j1 = j0 + 1
        return cols[:, o + j0:o + j1]

    def row_bc(name, tt_, nrows=128):
        o, s = RO[name]
        src = rows_d[0:1, o:o + s].partition_broadcast(nrows)
        k.dma("sp", tt_.t[0:nrows, 0:s].unsqueeze(1), src, writes=[tt_])

    psA = [nc.alloc_psum_tensor(f"ps{i}", [128, 2, 512], F32) for i in range(4)]
    pb = [Buf(f"bank{i}", psum=True) for i in range(8)]

    def bank(i):
        return psA[i // 2][:, i % 2, :]

    def bank_bf(i):
        return psA[i // 2][:, i % 2, :].bitcast(BF16)

    k.dma("sp", consts[:, :], consts_d, writes=[consts])
    k.dma("sp", cols[:, :], cols_d, writes=[cols])
    k.cp(identb[:, :], cst("ident"), [consts], [identb])
    ident = cst("ident")

    def blk_rows(blk):
        return 128 if blk < 16 else 64

    def load_weight(name, src_view, ncols, pieces, es_, nk=8):
        t_ = k.sb(name, [128, nk, ncols], BF16, es_)
        bl = []
        for (c0_, c1_) in pieces:
            b_ = Buf(f"{name}_{c0_}")
            k.dma("pool", t_.t[:, :, c0_:c1_], src_view[:, :, c0_:c1_], writes=[b_])
            bl.append((c0_, c1_, b_))

        def rb(c0_, c1_):
            r_ = [b_ for (a0, a1, b_) in bl if a0 < c1_ and c0_ < a1]
            assert r_
            return r_
        return t_, rb

    def norm_T(blk, es_tmp, tmp):
        rows = blk_rows(blk)
        junk, hb = tmp
        xin = x_tok[0:rows, blk, :]
        ss = ssx[0:rows, blk, :]
        k.act(junk[0:rows, :], xin, AF.Square, [xb[blk]], [junk, ssb[blk]], accum_out=ss[:, 0:1])
        k.act(ss[:, 1:2], ss[:, 0:1], AF.Sqrt, [ssb[blk], epsc], [ssb[blk]], scale=1.0 / 1024, bias=epsc[0:rows, 0:1])
        k.recip(ss[:, 2:3], ss[:, 1:2], [ssb[blk]], [ssb[blk]])
        k.stt(hb[0:rows, :], xin, ss[:, 2:3], gbc[0:rows, :], mult, mult, [xb[blk], ssb[blk], gbc], [hb])
        pt = bank_bf(6)[:, :].rearrange("p (c t) -> p c t", c=8)
        for c in range(8):
            k.tr(pt[:, c, 0:rows], hb[0:rows, c * 128:(c + 1) * 128], identb[0:rows, 0:rows], [hb, identb], [pb[6]])
        k.act(hT[:, :, blk * 128:blk * 128 + rows], pt[:, :, 0:rows], AF.Copy, [pb[6]], [hTb[blk]])

    epsc = k.sb("epsc", [128, 2], F32)
    k.memset(epsc[:, 0:1], EPS, [epsc])
    k.memset(epsc[:, 1:2], GN_EPS, [epsc])

    with ExitStack() as es:
        junk = k.sb("junk", [128, 1024], BF16, es)
        hb2 = [k.sb("hb", [128, 1024], BF16, es) for _ in range(2)]
        row_bc("norm_mix", gbc)
        for blk in range(17):
            rows = blk_rows(blk)
            src = xp_d[blk * 128:(blk + 1) * 128, :] if blk < 16 else xs_d
            k.dma("sp", x_tok[0:rows, blk, :], src, writes=[xb[blk]])
        for blk in range(17):
            norm_T(blk, es, (junk, hb2[blk % 2]))
        k.barrier()

    with ExitStack() as es:
        w_in_v = w_in_d.rearrange("(c p) n -> p c n", p=128)
        Wis, WisB = load_weight("Wis", w_in_v[:, :, 0:1544], 1544,
                                [(512 + 128 * c_, 640 + 128 * c_) for c_ in range(8)] + [(0, 512), (1536, 1544)], es)
        Wout = k.sb("Wout", [128, 4, 1024], BF16, es)
        w_out_v = w_out_d.rearrange("(c p) n -> p c n", p=128)
        k.dma("pool", Wout[:, :, :], w_out_v[:, 0:4, :], writes=[Wout])
        sm = k.sb("smallrow", [128, 32], F32, es)
        nwbc = k.sb("nwbc", [128, 512], F32, es)
        for nm, o in (("dt_bias", 0), ("a_log", 8), ("ssd_d", 16)):
            ro, rs = RO[nm]
            k.dma("sp", sm.t[:, o:o + 8].unsqueeze(1), rows_d[0:1, ro:ro + 8].partition_broadcast(128), writes=[sm])
        row_bc("ssd_norm", nwbc)
        k.act(sm[:, 24:32], sm[:, 8:16], AF.Exp, [sm], [sm])
        k.ts(sm[:, 24:32], sm[:, 24:32], -1.0, None, mult, None, [sm], [sm])
        xpre = [k.sb(f"xpre{c}", [128, 515], F32, es) for c in range(8)]
        xact = [k.sb(f"xact{c}", [128, 512], BF16, es) for c in range(8)]
        acc2 = [k.sb("acc", [128, 512], F32, es) for _ in range(2)]
        zs = k.sb("zs", [128, 512], F32, es)
        dtt = k.sb("dtt", [128, 32], F32, es)
        e3 = k.sb("e3", [128, 24], F32, es)
        xdt = k.sb("xdt", [128, 8, 64], BF16, es)
        xdte = k.sb("xdte", [128, 8, 64], BF16, es)
        xsD = k.sb("xsD", [128, 512], F32, es)
        Btok = k.sb("Btok", [128, 256], BF16, es)
        Rm = k.sb("Rm", [128, 8, 128], F32, es)
        seg = Rm
        cbm = k.sb("cbm", [128, 2, 128], F32, es)
        MT = k.sb("MT", [128, 8, 128], BF16, es)
        hst = k.sb("hst", [128, 512], F32, es)
        hbf = k.sb("hbf", [128, 512], BF16, es)
        y1 = k.sb("y1", [128, 512], F32, es)
        yj = k.sb("yj", [128, 512], BF16, es)
        ss2 = k.sb("ss2", [128, 8], F32, es)
        ybf = k.sb("ybf", [128, 512], BF16, es)
        ysT = k.sb("ysT", [128, 4, 512], BF16, es)
        convo = k.sb("convo", [128, 8, 3], F32, es)
        for c in range(8):
            k.memset(xpre[c][:, 0:3], 0.0, [xpre[c]])
        tri, lm, ones = cst("tri"), cst("lm"), cst("ones")

        for t in range(NTB):
            n = 512
            c0 = t * 512
            hbufs = hTb[4 * t:4 * t + 4]
            for c in range(8):
                bi_ = c % 2
                for kk in range(8):
                    k.mm(bank(bi_)[:, 0:n], Wis[:, kk, 512 + 128 * c:512 + 128 * (c + 1)], hT[:, kk, c0:c0 + n],
                         kk == 0, kk == 7, WisB(512 + 128 * c, 640 + 128 * c) + hbufs, [pb[bi_]])
                k.act(xpre[c][:, 3:3 + n], bank(bi_)[:, 0:n], AF.Copy, [pb[bi_]], [xpre[c]])
                acc = acc2[c % 2]
                k.act(acc[:, 0:n], xpre[c][:, 0:n], AF.Identity, [xpre[c], cols], [acc],
                      scale=col("ssd_cw", 4 * c), bias=col("ssd_cb", c))
                for j in range(1, 4):
                    k.stt(acc[:, 0:n], xpre[c][:, j:j + n], col("ssd_cw", 4 * c + j), acc[:, 0:n], mult, add,
                          [xpre[c], cols, acc], [acc])
                k.act(xact[c][:, 0:n], acc[:, 0:n], AF.Silu, [acc], [xact[c]])
                if t == 3:
                    k.cp(convo[:, c, :], xpre[c][:, n:n + 3], [xpre[c]], [convo], eng="pool")
                else:
                    k.cp(xpre[c][:, 0:3], xpre[c][:, n:n + 3], [xpre[c]], [xpre[c]], eng="pool")
            for bi in range(4):
                blk = 4 * t + bi
                cs = slice(bi * 128, bi * 128 + 128)
                hc = slice(blk * 128, blk * 128 + 128)
                for kk in range(8):
                    k.mm(bank(2)[:, :], hT[:, kk, hc], Wis[:, kk, 0:512], kk == 0, kk == 7, WisB(0, 512) + [hTb[blk]], [pb[2]])
                    k.mm(bank(3)[:, 0:8], hT[:, kk, hc], Wis[:, kk, 1536:1544], kk == 0, kk == 7,
                         WisB(1536, 1544) + [hTb[blk]], [pb[3]])
                k.act(zs[:, :], bank(2)[:, :], AF.Silu, [pb[2]], [zs])
                k.tt(dtt[:, 0:8], bank(3)[:, 0:8], sm[:, 0:8], add, [pb[3], sm], [dtt])
                k.ts(dtt[:, 24:32], dtt[:, 0:8], 30.0, None, ALU.min, None, [dtt], [dtt])
                k.act(dtt[:, 24:32], dtt[:, 24:32], AF.Exp, [dtt], [dtt])
                k.act(dtt[:, 8:16], dtt[:, 24:32], AF.Ln, [dtt], [dtt], bias=1.0)
                k.ts(dtt[:, 24:32], dtt[:, 0:8], -30.0, 0.0, add, ALU.max, [dtt], [dtt])
                k.tt(dtt[:, 8:16], dtt[:, 8:16], dtt[:, 24:32], add, [dtt], [dtt])
                k.tt(dtt[:, 16:24], dtt[:, 8:16], sm[:, 24:32], mult, [dtt, sm], [dtt])
                dt_ = dtt[:, 8:16]
                dta = dtt[:, 16:24]
                ptb = bank_bf(6)[:, 0:768].rearrange("p (c t) -> p c t", c=6)
                for c in range(6):
                    k.tr(ptb[:, c, :], xact[c][:, cs], identb[:, :], [xact[c], identb], [pb[6]])
                xs_ps = ptb[:, 0:4, :].rearrange("p c (h d) -> p (c h) d", h=2)
                k.tt(xdt[:, :, :], xs_ps, bc(dt_, 2, 64), mult, [pb[6], dtt], [xdt])
                k.tt(xsD[:, :].rearrange("p (h d) -> p h d", h=8), xs_ps, bc(sm[:, 16:24], 2, 64), mult,
                     [pb[6], sm], [xsD])
                k.act(Btok[:, :], ptb[:, 4:6, :].rearrange("p c t -> p (c t)"), AF.Copy, [pb[6]], [Btok])
                k.tt(Rm[:, :, :], bc(tri, 1, 8), bc(dta, 2, 128), mult, [consts, dtt], [Rm])
                for hh in range(2):
                    k.mm(bank(4 + hh)[:, :], lm, Rm[:, 4 * hh:4 * hh + 4, :].rearrange("p h q -> p (h q)"), True, True,
                         [consts, Rm], [pb[4 + hh]])
                k.mm(bank(3)[:, 8:16], tri, dta, True, True, [consts, dtt], [pb[3]])
                k.mm(bank(3)[:, 16:24], lm, dta, True, True, [consts, dtt], [pb[3]])
                k.mm(bank(3)[:, 24:32], ones, dta, True, True, [consts, dtt], [pb[3]])
                k.act(e3[:, :], bank(3)[:, 8:32], AF.Exp, [pb[3]], [e3])
                k.act(seg[:, :, :].rearrange("p h q -> p (h q)"), psA[2][:, :, :].rearrange("p a b -> p (a b)"),
                      AF.Exp, [pb[4], pb[5]], [seg])
                for g in range(2):
                    k.mm(bank(3)[:, 64 + 128 * g:64 + 128 * (g + 1)], xact[4 + g][:, cs], xact[6 + g][:, cs], True,
                         True, [xact[4 + g], xact[6 + g]], [pb[3]])
                k.tt(cbm[:, :, :], bank(3)[:, 64:320].rearrange("p (g q) -> p g q", g=2), bc(tri, 1, 2), mult,
                     [pb[3], consts], [cbm])
                k.tt(MT[:, :, :].rearrange("p (g e) q -> p g e q", g=2),
                     seg[:, :, :].rearrange("p (g e) q -> p g e q", g=2), bc(cbm[:, :, :], 2, 4), mult,
                     [seg, cbm], [MT])
                k.tt(xdte[:, :, :], xdt[:, :, :], bc(e3[:, 8:16], 2, 64), mult, [xdt, e3], [xdte])
                for h in range(8):
                    k.mm(bank(7)[:, 64 * h:64 * (h + 1)], MT[:, h, :], xdt[:, h, :], True, True, [MT, xdt], [pb[7]])
                if blk > 0:
                    for g in range(2):
                        k.mm(bank(5)[:, 256 * g:256 * (g + 1)], xact[6 + g][:, cs], hbf[:, 256 * g:256 * (g + 1)],
                             True, True, [xact[6 + g], hbf], [pb[5]])
                    k.tt(y1[:, :].rearrange("p (h d) -> p h d", h=8),
                         bank(5)[:, :].rearrange("p (h d) -> p h d", h=8), bc(e3[:, 0:8], 2, 64), mult,
                         [pb[5], e3], [y1])
                    k.tt(y1[:, :], bank(7)[:, :], y1[:, :], add, [pb[7], y1], [y1])
                    k.tt(y1[:, :], y1[:, :], xsD[:, :], add, [y1, xsD], [y1])
                else:
                    k.tt(y1[:, :], bank(7)[:, :], xsD[:, :], add, [pb[7], xsD], [y1])
                for g in range(2):
                    k.mm(bank(4)[:, 256 * g:256 * (g + 1)], Btok[:, 128 * g:128 * (g + 1)],
                         xdte[:, 4 * g:4 * g + 4, :].rearrange("p h d -> p (h d)"), True, True, [Btok, xdte], [pb[4]])
                if blk > 0:
                    k.tt(hst[:, :].rearrange("p (h d) -> p h d", h=8), hst[:, :].rearrange("p (h d) -> p h d", h=8),
                         bc(e3[:, 16:24], 2, 64), mult, [hst, e3], [hst])
                    k.tt(hst[:, :], bank(4)[:, :], hst[:, :], add, [pb[4], hst], [hst])
                else:
                    k.cp(hst[:, :], bank(4)[:, :], [pb[4]], [hst])
                k.act(hbf[:, :], hst[:, :], AF.Copy, [hst], [hbf])
                k.tt(y1[:, :], y1[:, :], zs[:, :], mult, [y1, zs], [y1])
                for g in range(2):
                    k.act(yj[:, 256 * g:256 * (g + 1)], y1[:, 256 * g:256 * (g + 1)], AF.Square, [y1], [yj, ss2],
                          accum_out=ss2[:, g:g + 1])
                k.act(ss2[:, 2:4], ss2[:, 0:2], AF.Sqrt, [ss2, epsc], [ss2], scale=1.0 / 256, bias=epsc[:, 0:1])
                k.recip(ss2[:, 4:6], ss2[:, 2:4], [ss2], [ss2])
                k.tt(y1[:, :].rearrange("p (g d) -> p g d", g=2), y1[:, :].rearrange("p (g d) -> p g d", g=2),
                     bc(ss2[:, 4:6], 2, 256), mult, [y1, ss2], [y1])
                k.tt(y1[:, :], y1[:, :], nwbc[:, :], mult, [y1, nwbc], [y1])
                if dbg and blk in (0, 5, 15):
                    k.dma("sp", dbg_d["yssd"][blk * 128:(blk + 1) * 128, :], y1[:, :], reads=[y1])
                k.cp(ybf[:, :], y1[:, :], [y1], [ybf])
                pty = bank_bf(6)[:, 0:512].rearrange("p (c t) -> p c t", c=4)
                for c in range(4):
                    k.tr(pty[:, c, :], ybf[:, 128 * c:128 * (c + 1)], identb[:, :], [ybf, identb], [pb[6]])
                k.act(ysT[:, :, cs], pty[:, :, :], AF.Copy, [pb[6]], [ysT])
            for bi in range(4):
                blk = 4 * t + bi
                cs = slice(bi * 128, bi * 128 + 128)
                for kk in range(4):
                    for hf in range(2):
                        k.mm(bank(hf)[:, :], ysT[:, kk, cs], Wout[:, kk, 512 * hf:512 * (hf + 1)], kk == 0, kk == 3,
                             [ysT, Wout], [pb[hf]])
                for hf in range(2):
                    k.tt(x_tok[:, blk, 512 * hf:512 * (hf + 1)], bank(hf)[:, :], x_tok[:, blk, 512 * hf:512 * (hf + 1)],
                         add, [pb[hf], xb[blk]], [xb[blk]])
        k.dma("sp", o_conv_p, convo[:, :, :], reads=[convo])
        for j in range(4):
            k.tr(bank(2)[:, 128 * j:128 * (j + 1)], hst[:, 128 * j:128 * (j + 1)], ident, [hst, consts], [pb[2]])
        k.cp(y1[:, :], bank(2)[:, :], [pb[2]], [y1])
        k.dma("sp", o_ssm_p.rearrange("(j p) n -> p j n", p=128), y1[:, :].rearrange("p (j n) -> p j n", j=4),
              reads=[y1])
        k.barrier()

    with ExitStack() as es:
        Wout = k.sb("Wout", [128, 4, 1024], BF16, es)
        k.dma("pool", Wout[:, :, :], w_out_v[:, 0:4, :], writes=[Wout])
        sm = k.sb("smallrow", [128, 32], F32, es)
        nwbc = k.sb("nwbc", [128, 512], F32, es)
        for nm, o in (("dt_bias", 0), ("a_log", 8), ("ssd_d", 16)):
            ro, rs_ = RO[nm]
            k.dma("sp", sm.t[:, o:o + 8].unsqueeze(1), rows_d[0:1, ro:ro + 8].partition_broadcast(128), writes=[sm])
        row_bc("ssd_norm", nwbc)
        k.act(sm[:, 24:32], sm[:, 8:16], AF.Exp, [sm], [sm])
        k.ts(sm[:, 24:32], sm[:, 24:32], -1.0, None, mult, None, [sm], [sm])
        zs = k.sb("zss", [64, 512], F32, es)
        dtt = k.sb("dtts", [64, 40], F32, es)
        xsD = k.sb("xsDs", [64, 512], F32, es)
        xq = k.sb("xq", [128, 4, 64], F32, es)
        Bq = k.sb("Bq", [128, 4, 128], F32, es)
        Cq = k.sb("Cq", [128, 4, 128], F32, es)
        dAq = k.sb("dAq", [128, 4, 1], F32, es)
        yq = k.sb("yq", [128, 4, 64], F32, es)
        y1 = k.sb("y1s", [64, 512], F32, es)
        yj = k.sb("yjs", [64, 512], BF16, es)
        ss2 = k.sb("ss2s", [64, 8], F32, es)
        ybf = k.sb("ybfs", [64, 512], BF16, es)
        ysT = k.sb("ysTs", [128, 4, 64], BF16, es)
        esa = ExitStack()
        Wis = k.sb("Wis", [128, 8, 1544], BF16, esa)
        for h2 in range(2):
            k.dma("pool", Wis[:, 4 * h2:4 * h2 + 4, :], w_in_v[:, 4 * h2:4 * h2 + 4, 0:1544], writes=[Wis])
        xpre = k.sb("xpres", [128, 8, 112], F32, esa)
        xactf = k.sb("xactf", [128, 8, 64], F32, esa)
        acc = k.sb("accs", [128, 64], F32, esa)
        tokx = k.sb("tokx", [64, 1024], F32, esa)
        xdtf = k.sb("xdtf", [64, 512], F32, esa)
        Bh = k.sb("Bh", [64, 8, 128], F32, esa)
        Ch = k.sb("Ch", [64, 8, 128], F32, esa)
        scr_x_t = nc.dram_tensor("scr_x", [64, 512], F32)
        scr_B_t = nc.dram_tensor("scr_B", [64, 1024], F32)
        scr_C_t = nc.dram_tensor("scr_C", [64, 1024], F32)
        scr_a_t = nc.dram_tensor("scr_a", [64, 8], F32)
        scr_y_t = nc.dram_tensor("scr_y", [64, 512], F32)
        sbx, sbB, sbC, sba, sby = [Buf(n_) for n_ in "scx scB scC sca scy".split()]
        k.dma("sp", xpre[:, :, 0:48], st_conv_d.rearrange("p c j s -> p c (j s)"), writes=[xpre])
        n = 64
        c0 = 2048
        hbufs = [hTb[16]]
        for c in range(8):
            bi_ = c % 2
            for kk in range(8):
                k.mm(bank(bi_)[:, 0:n], Wis[:, kk, 512 + 128 * c:512 + 128 * (c + 1)], hT[:, kk, c0:c0 + n],
                     kk == 0, kk == 7, [Wis] + hbufs, [pb[bi_]])
            k.act(xpre[:, c, 48:112], bank(bi_)[:, 0:n], AF.Copy, [pb[bi_]], [xpre])
            k.act(acc[:, :], xpre[:, c, 0:64], AF.Identity, [xpre, cols], [acc], scale=col("ssd_cw", 4 * c),
                  bias=col("ssd_cb", c))
            for j in range(1, 4):
                k.stt(acc[:, :], xpre[:, c, 16 * j:16 * j + 64], col("ssd_cw", 4 * c + j), acc[:, :], mult, add,
                      [xpre, cols, acc], [acc])
            k.act(xactf[:, c, :], acc[:, :], AF.Silu, [acc], [xactf])
        k.dma("sp", o_conv_s.rearrange("p c j s -> p c (j s)"), xpre[:, :, 64:112], reads=[xpre])
        for kk in range(8):
            k.mm(bank(2)[0:64, :], hT[:, kk, c0:c0 + n], Wis[:, kk, 0:512], kk == 0, kk == 7, [Wis] + hbufs, [pb[2]])
        k.act(zs[:, :], bank(2)[0:64, :], AF.Silu, [pb[2]], [zs])
        for kk in range(8):
            k.mm(bank(3)[0:64, 0:8], hT[:, kk, c0:c0 + n], Wis[:, kk, 1536:1544], kk == 0, kk == 7, [Wis] + hbufs,
                 [pb[3]])
        k.tt(dtt[:, 0:8], bank(3)[0:64, 0:8], sm[0:64, 0:8], add, [pb[3], sm], [dtt])
        k.ts(dtt[:, 24:32], dtt[:, 0:8], 30.0, None, ALU.min, None, [dtt], [dtt])
        k.act(dtt[:, 24:32], dtt[:, 24:32], AF.Exp, [dtt], [dtt])
        k.act(dtt[:, 8:16], dtt[:, 24:32], AF.Ln, [dtt], [dtt], bias=1.0)
        k.ts(dtt[:, 24:32], dtt[:, 0:8], -30.0, 0.0, add, ALU.max, [dtt], [dtt])
        k.tt(dtt[:, 8:16], dtt[:, 8:16], dtt[:, 24:32], add, [dtt], [dtt])
        k.tt(dtt[:, 16:24], dtt[:, 8:16], sm[0:64, 24:32], mult, [dtt, sm], [dtt])
        k.act(dtt[:, 32:40], dtt[:, 16:24], AF.Exp, [dtt], [dtt])
        for c in range(8):
            bk = 4 + c // 4
            k.tr(bank(bk)[0:64, 128 * (c % 4):128 * (c % 4 + 1)], xactf[:, c, :], ident, [xactf, consts], [pb[bk]])
        k.act(tokx[:, 0:512], bank(4)[0:64, :], AF.Copy, [pb[4]], [tokx])
        k.act(tokx[:, 512:1024], bank(5)[0:64, :], AF.Copy, [pb[5]], [tokx])
        v8 = lambda ap: ap.rearrange("p (h d) -> p h d", h=8)
        k.tt(v8(xdtf[:, :]), v8(tokx[:, 0:512]), bc(dtt[:, 8:16], 2, 64), mult, [tokx, dtt], [xdtf])
        k.tt(v8(xsD[:, :]), v8(tokx[:, 0:512]), bc(sm[0:64, 16:24], 2, 64), mult, [tokx, sm], [xsD])
        k.cp(Bh[:, :, :].rearrange("p (g e) n -> p g e n", g=2),
             bc(tokx[:, 512:768].rearrange("p (g n) -> p g n", g=2), 2, 4), [tokx], [Bh])
        k.cp(Ch[:, :, :].rearrange("p (g e) n -> p g e n", g=2),
             bc(tokx[:, 768:1024].rearrange("p (g n) -> p g n", g=2), 2, 4), [tokx], [Ch])
        k.dma("sp", scr_x_t.ap(), xdtf[:, :], reads=[xdtf], writes=[sbx])
        k.dma("sp", scr_B_t.ap(), Bh[:, :, :].rearrange("p h n -> p (h n)"), reads=[Bh], writes=[sbB])
        k.dma("sp", scr_C_t.ap(), Ch[:, :, :].rearrange("p h n -> p (h n)"), reads=[Ch], writes=[sbC])
        k.dma("sp", scr_a_t.ap(), dtt[:, 32:40], reads=[dtt], writes=[sba])
        k.dma("sp", xq[:, :, :], bass.AP(scr_x_t, 0, [[64, 128], [8192, 4], [1, 64]]), reads=[sbx], writes=[xq])
        k.dma("sp", Bq[:, :, :], bass.AP(scr_B_t, 0, [[128, 128], [16384, 4], [1, 128]]), reads=[sbB], writes=[Bq])
        k.dma("sp", Cq[:, :, :], bass.AP(scr_C_t, 0, [[128, 128], [16384, 4], [1, 128]]), reads=[sbC], writes=[Cq])
        k.dma("sp", dAq[:, :, :], bass.AP(scr_a_t, 0, [[1, 128], [128, 4], [1, 1]]), reads=[sba], writes=[dAq],
              allow_slow_non_contiguous=True)
        k.barrier()
        esa.close()
        hS = k.sb("hS", [128, 8192], F32, es)
        tmpS = k.sb("tmpS", [128, 8192], F32, es)
        k.dma("sp", hS[:, :], st_ssm_d, writes=[hS])
        h3 = lambda ap: ap.rearrange("p (a b) -> p a b", b=128)
        for l in range(4):
            k.tt(h3(tmpS[:, :]), bc(xq[:, l, :], 2, 128), bc(Bq[:, l, :], 1, 64), mult, [xq, Bq], [tmpS], eng="pool")
            k.stt(hS[:, :], hS[:, :], dAq[:, l, :], tmpS[:, :], mult, add, [hS, dAq, tmpS], [hS])
            k.tt(h3(tmpS[:, :]), h3(hS[:, :]), bc(Cq[:, l, :], 1, 64), mult, [hS, Cq], [tmpS])
            k.red(yq[:, l, :], h3(tmpS[:, :]), add, [tmpS], [yq])
        k.dma("sp", o_ssm_s, hS[:, :], reads=[hS])
        k.dma("sp", bass.AP(scr_y_t, 0, [[64, 128], [8192, 4], [1, 64]]), yq[:, :, :], reads=[yq], writes=[sby])
        k.dma("sp", y1[:, :], scr_y_t.ap(), reads=[sby], writes=[y1])
        k.tt(y1[:, :], y1[:, :], xsD[:, :], add, [y1, xsD], [y1])
        k.tt(y1[:, :], y1[:, :], zs[:, :], mult, [y1, zs], [y1])
        for g in range(2):
            k.act(yj[:, 256 * g:256 * (g + 1)], y1[:, 256 * g:256 * (g + 1)], AF.Square, [y1], [yj, ss2],
                  accum_out=ss2[:, g:g + 1])
        k.act(ss2[:, 2:4], ss2[:, 0:2], AF.Sqrt, [ss2, epsc], [ss2], scale=1.0 / 256, bias=epsc[0:64, 0:1])
        k.recip(ss2[:, 4:6], ss2[:, 2:4], [ss2], [ss2])
        k.tt(y1[:, :].rearrange("p (g d) -> p g d", g=2), y1[:, :].rearrange("p (g d) -> p g d", g=2),
             bc(ss2[:, 4:6], 2, 256), mult, [y1, ss2], [y1])
        k.tt(y1[:, :], y1[:, :], nwbc[0:64, :], mult, [y1, nwbc], [y1])
        if dbg:
            k.dma("sp", dbg_d["yssd"][2048:2112, :], y1[:, :], reads=[y1])
        k.cp(ybf[:, :], y1[:, :], [y1], [ybf])
        pty = bank_bf(6)[:, 0:256].rearrange("p (c t) -> p c t", c=4)
        for c in range(4):
            k.tr(pty[:, c, :], ybf[:, 128 * c:128 * (c + 1)], identb[0:64, 0:64], [ybf, identb], [pb[6]])
        k.act(ysT[:, :, :], pty[:, :, :], AF.Copy, [pb[6]], [ysT])
        for hf in range(2):
            bk = hf
            for kk in range(4):
                k.mm(bank(bk)[0:64, :], ysT[:, kk, :], Wout[:, kk, 512 * hf:512 * (hf + 1)], kk == 0, kk == 3,
                     [ysT, Wout], [pb[bk]])
            k.tt(x_tok[0:64, 16, 512 * hf:512 * (hf + 1)], bank(bk)[0:64, :], x_tok[0:64, 16, 512 * hf:512 * (hf + 1)],
                 add, [pb[bk], xb[16]], [xb[16]])
        k.barrier()

    with ExitStack() as es:
        Wir, WirB = load_weight("Wir", w_in_v[:, :, 1544:3336], 1792,
                                [(128 * c_, 128 * c_ + 128) for c_ in [12, 13] + [c_ for p_ in range(4) for c_ in (p_, 4 + p_, 8 + p_)]], es)
        Wout2 = k.sb("Wout2", [128, 4, 1024], BF16, es)
        k.dma("pool", Wout2[:, :, :], w_out_v[:, 4:8, :], writes=[Wout2])
        W2A2 = k.sb("W2A2", [128, 512], BF16, es)
        G2 = k.sb("G2", [128, 512], BF16, es)
        k.dma("pool", W2A2[:, :], w2a2_d, writes=[W2A2])
        k.dma("pool", G2[:, :], g2_d, writes=[G2])
        c2 = k.sb("consts2", [128, NCONST2], F32, es)
        k.dma("sp", c2[:, :], consts2_d, writes=[c2])

        def cst2(name):
            o, s = KO2[name]
            return c2[:, o:o + s]

        lnw = k.sb("lnw", [128, 4, 64], F32, es)
        lnb = k.sb("lnb", [128, 4, 64], F32, es)
        for nm, tt_ in (("ln_w", lnw), ("ln_b", lnb)):
            o, s = RO[nm]
            v = rows_d[0:1, o:o + 512].rearrange("a (pr hp d) -> a pr hp d", pr=4, hp=2)
            for hp in range(2):
                k.dma("sp", tt_.t[64 * hp:64 * hp + 64, :, :].unsqueeze(1), v[:, :, hp, :].partition_broadcast(64),
                      writes=[tt_])
        omm = k.sb("omm", [128, 14], F32, es)
        k.ts(omm[:, :], col("mu", 0, 14), -1.0, 1.0, mult, add, [cols], [omm])
        onesb = k.sb("onesb", [128, 2], BF16, es)
        k.memset(onesb[:, :], 1.0, [onesb])
        usb = [k.sb("usb", [128, 516], F32, es) for _ in range(2)]
        carry = k.sb("carry", [128, 14, 16], F32, es)
        k.memset(carry[:, :, :], 0.0, [carry])
        tlla = k.sb("tlla", [128, 512], BF16, es)
        sg = k.sb("sg", [128, 512], BF16, es)
        f = lambda nm: k.sb(nm, [128, 512], F32, es)
        um_r, um_k, um_v, sigw, a_t, kkn, t1, t2, csm, ecw, encw, exw = [f(n_) for n_ in
            "um_r um_k um_v sigw a_t kkn t1 t2 csm ecw encw exw".split()]
        gT = k.sb("gT", [128, 512], BF16, es)
        um12 = um_v
        tmpm = kkn
        AR = k.sb("AR", [128, 8, 2, 128], BF16, es)
        Bt = k.sb("Bt", [128, 8, 128], BF16, es)
        Kt = k.sb("Kt", [128, 8, 128], BF16, es)
        vT = k.sb("vT", [128, 8, 128], BF16, es)
        for z_ in (AR, Bt, Kt, vT):
            k.memset(z_.t[:].rearrange("p a b -> p (a b)") if len(z_.t.shape) == 3 else
                     z_.t[:].rearrange("p a b c -> p (a b c)"), 0.0, [z_])
        Ub2 = [k.sb("Ub", [128, 128], BF16, es) for _ in range(2)]
        Hf = k.sb("Hf", [128, 4, 128], F32, es)
        Hb = k.sb("Hb", [128, 4, 128], BF16, es)
        Hfb = [Buf(f"Hf{p}") for p in range(4)]
        Hbb = [Buf(f"Hb{p}") for p in range(4)]
        ht = k.sb("ht", [128, 128], F32, es)
        hout = k.sb("hout", [128, 4, 64], F32, es)
        vtok = k.sb("vtok", [128, 8, 64], BF16, es)
        class _V:
            def __init__(self, base):
                self.b = base.b
                self.v = base.t[:, :].rearrange("p (c d) -> p c d", d=64)

            def __getitem__(self, idx):
                return self.v[idx]
        ysb = _V(exw)
        ysq = _V(t2)
        st8 = k.sb("st8", [128, 6, 8], F32, es)
        ynb = k.sb("ynb", [128, 8, 64], BF16, es)
        prod = View(ynb.t[:].rearrange("p c d -> p (c d)"), ynb.b)
        yrT = k.sb("yrT", [128, 4, 512], BF16, es)
        shifto = k.sb("shifto", [128, 14], F32, es)
        k.memset(Hf[:, :, :], 0.0, Hfb)
        k.memset(Hb[:, :, :], 0.0, Hbb)
        wmask, i64c, bones, scanm = cst2("wmask"), cst2("i64"), cst2("bones"), cst2("scanm")
        HP = [slice(0, 64), slice(64, 128)]

        def proj_chunk(c, c0, n, S, hbufs, dst, ucount):
            bi_ = 4 + ucount % 2
            u_ = usb[ucount % 2]
            for kk in range(8):
                k.mm(bank(bi_)[:, 0:n], Wir[:, kk, 128 * c:128 * (c + 1)], hT[:, kk, c0:c0 + n], kk == 0, kk == 7,
                     WirB(128 * c, 128 * c + 128) + hbufs, [pb[bi_]])
            k.act(u_[:, S:S + n], bank(bi_)[:, 0:n], AF.Copy, [pb[bi_]], [u_])
            k.cp(u_[:, 0:S], carry[:, c, 0:S], [carry], [u_], eng="pool")
            k.act(tmpm[:, 0:n], u_[:, 0:n], AF.Identity, [u_, cols], [tmpm], scale=col("mu", c))
            k.stt(dst[:, 0:n], u_[:, S:S + n], omm[:, c:c + 1], tmpm[:, 0:n], mult, add, [u_, omm, tmpm], [dst])
            k.cp(carry[:, c, 0:S], u_[:, n:n + S], [u_], [carry], eng="pool")

        ucount = 0
        for t in range(NTB):
            n = 512
            S = 1
            c0 = t * 512
            nch = n // 64
            hbufs = hTb[4 * t:4 * t + 4]
            proj_chunk(12, c0, n, S, hbufs, um12, ucount); ucount += 1
            k.act(tlla[0:64, 0:n], um12[0:64, 0:n], AF.Tanh, [um12], [tlla])
            k.act(tlla[64:128, 0:n], um12[64:128, 0:n], AF.Copy, [um12], [tlla])
            proj_chunk(13, c0, n, S, hbufs, um12, ucount); ucount += 1
            k.act(sg[:, 0:n], um12[:, 0:n], AF.Sigmoid, [um12], [sg])
            for p in range(4):
                pc = slice(128 * p, 128 * (p + 1))
                k.mm(bank(6)[:, 0:n], W2A2[0:64, pc], tlla[0:64, 0:n], True, True, [W2A2, tlla], [pb[6]])
                k.mm(bank(7)[:, 0:n], W2A2[64:128, pc], tlla[64:128, 0:n], True, True, [W2A2, tlla], [pb[7]])
                k.act(sigw[:, 0:n], bank(6)[:, 0:n], AF.Sigmoid, [pb[6], cols], [sigw], bias=col("w0", p))
                k.act(a_t[:, 0:n], bank(7)[:, 0:n], AF.Sigmoid, [pb[7], cols], [a_t], bias=col("a0", p))
                k.mm(bank(6)[:, 0:n], G2[:, pc], sg[:, 0:n], True, True, [G2, sg], [pb[6]])
                k.act(gT[:, 0:n], bank(6)[:, 0:n], AF.Copy, [pb[6]], [gT])
                k.scan(csm[:, 0:n], scanm[:, 0:n], sigw[:, 0:n], 0.0, mult, add, [c2, sigw], [csm])
                k.tt(t2[:, 0:n], csm[:, 0:n], sigw[:, 0:n], sub, [csm, sigw], [t2])
                k.act(ecw[:, 0:n], csm[:, 0:n], AF.Exp, [csm], [ecw], scale=-C0)
                k.act(encw[:, 0:n], csm[:, 0:n], AF.Exp, [csm], [encw], scale=C0)
                k.act(exw[:, 0:n], t2[:, 0:n], AF.Exp, [t2], [exw], scale=-C0)
                proj_chunk(p, c0, n, S, hbufs, um_r, ucount); ucount += 1
                proj_chunk(4 + p, c0, n, S, hbufs, um_k, ucount); ucount += 1
                proj_chunk(8 + p, c0, n, S, hbufs, um_v, ucount); ucount += 1
                k.ts(t1[:, 0:n], um_k[:, 0:n], col("k_k", p), None, mult, None, [um_k, cols], [t1])
                k.tt(t2[:, 0:n], t1[:, 0:n], t1[:, 0:n], mult, [t1], [t2])
                k.mm(bank(7)[:, 0:n], bones, t2[:, 0:n], True, True, [c2, t2], [pb[7]])
                k.ts(t2[:, 0:n], bank(7)[:, 0:n], 1e-19, None, ALU.max, None, [pb[7]], [t2])
                k.act(t2[:, 0:n], t2[:, 0:n], AF.Ln, [t2], [t2])
                k.act(t2[:, 0:n], t2[:, 0:n], AF.Exp, [t2], [t2], scale=-0.5)
                k.tt(kkn[:, 0:n], t1[:, 0:n], t2[:, 0:n], mult, [t1, t2], [kkn])
                k.ts(t1[:, 0:n], a_t[:, 0:n], -1.0, col("k_a", p), add, mult, [a_t, cols], [t1])
                k.stt(t1[:, 0:n], t1[:, 0:n], 1.0, um_k[:, 0:n], add, mult, [t1, um_k], [t1])
                v3 = lambda ap: ap.rearrange("p (c t) -> p c t", t=64)
                k.tt(t2[:, 0:n], a_t[:, 0:n], kkn[:, 0:n], mult, [a_t, kkn], [t2])
                for hp in range(2):
                    rs = HP[hp]
                    dc_ = slice(64 * hp, 64 * hp + 64)
                    k.stt(AR[rs, 0:nch, 0, dc_], v3(kkn[rs, 0:n]), -1.0, v3(exw[rs, 0:n]), mult, mult, [kkn, exw], [AR])
                    k.tt(AR[rs, 0:nch, 1, dc_], v3(um_r[rs, 0:n]), v3(ecw[rs, 0:n]), mult, [um_r, ecw], [AR])
                    k.tt(Bt[rs, 0:nch, dc_], v3(t2[rs, 0:n]), v3(encw[rs, 0:n]), mult, [t2, encw], [Bt])
                    k.tt(Kt[rs, 0:nch, dc_], v3(t1[rs, 0:n]), v3(encw[rs, 0:n]), mult, [t1, encw], [Kt])
                    k.act(vT[rs, 0:nch, dc_], v3(um_v[rs, 0:n]), AF.Copy, [um_v], [vT])
                k.stt(prod[:, 0:n], um_r[:, 0:n], col("r_k", p), t1[:, 0:n], mult, mult, [um_r, cols, t1], [prod])
                for ch in range(nch):
                    tk = slice(64 * ch, 64 * ch + 64)
                    for hp in range(2):
                        rs = HP[hp]
                        k.mm(bank(6)[rs, 256 + ch:257 + ch], prod[rs, tk], onesb[rs, 0:1], True, True, [prod, onesb],
                             [pb[6]])
                k.cp(st8[:, 5, 0:nch], bank(6)[:, 256:256 + nch], [pb[6]], [st8])
                cpar = [(um_r, um_k), (um_v, sigw), (a_t, kkn), (t1, csm)]
                ubs = []
                for g in range(4):
                    va = k.carve(cpar[g][0], [(1280, BF16), (768, BF16)])
                    vb = k.carve(cpar[g][1], [(768, BF16), (256, BF16), (1024, BF16)])
                    ubs.append([va[0], va[1], vb[0], vb[1], vb[2]])
                sm_a = k.carve(encw, [(256, BF16), (256, BF16), (512, F32)] * 2)
                sm_b = k.carve(t2, [(256, BF16), (256, BF16), (512, F32)] * 2)
                smalls = sm_a + sm_b
                for grp in range(nch // 4):
                    chs = [4 * grp + g for g in range(4)]
                    for g, ch in enumerate(chs):
                        mats = ubs[g][0]
                        arc = AR[:, ch, :, :].rearrange("p a t -> p (a t)")
                        k.mm(bank(g)[:, 0:256], Bt[:, ch, :], arc, True, True, [Bt, AR], [pb[g]])
                        k.mm(bank(g)[:, 256:512], Kt[:, ch, :], arc, True, True, [Kt, AR], [pb[g]])
                        k.tt(mats[:, 0:512], bank(g)[:, 0:512], wmask[:, 0:512], mult, [pb[g], c2], [mats])
                    for g, ch in enumerate(chs):
                        mats, l0 = ubs[g][0], ubs[g][1]
                        k.mm(bank(g)[:, 0:128], AR[:, ch, 0, :], Bt[:, ch, :], True, True, [Bt, AR], [pb[g]])
                        k.tt(mats[:, 512:640], bank(g)[:, 0:128], wmask[:, 512:640], mult, [pb[g], c2], [mats])
                        k.tt(l0[:, 0:128], mats[:, 0:128], ident, add, [mats, consts], [l0])
                    for g in range(4):
                        mats, l0 = ubs[g][0], ubs[g][1]
                        k.mm(bank(g)[:, 0:128], mats[:, 512:640], mats[:, 0:128], True, True, [mats], [pb[g]])
                        k.mm(bank(g)[:, 128:256], mats[:, 0:128], mats[:, 512:640], True, True, [mats], [pb[g]])
                        k.act(l0[:, 128:384], bank(g)[:, 0:256], AF.Copy, [pb[g]], [l0])
                    ci, ni = 1, 2
                    for j in range(1, 5):
                        for g in range(4):
                            cur, nxt = ubs[g][ci], ubs[g][ni]
                            k.mm(bank(g)[:, 0:256], cur[:, 256:384], cur[:, 0:256], True, True, [cur], [pb[g]])
                            k.mm(bank(g)[:, 256:384], cur[:, 128:256], cur[:, 256:384], True, True, [cur], [pb[g]])
                            k.tt(nxt[:, 0:128], cur[:, 0:128], bank(g)[:, 0:128], add, [cur, pb[g]], [nxt])
                            k.act(nxt[:, 128:384], bank(g)[:, 128:384], AF.Copy, [pb[g]], [nxt])
                        ci, ni = ni, ci
                    for g in range(4):
                        cur, TTm_ = ubs[g][ci], ubs[g][3]
                        k.mm(bank(g)[:, 0:128], cur[:, 256:384], cur[:, 0:128], True, True, [cur], [pb[g]])
                        k.tt(TTm_[:, :], cur[:, 0:128], bank(g)[:, 0:128], add, [cur, pb[g]], [TTm_])
                    for g, ch in enumerate(chs):
                        tok_ = ubs[g][4]
                        ptk = bank_bf(g)
                        srcs = [(AR[:, ch, 0, :], AR), (Bt[:, ch, :], Bt), (Kt[:, ch, :], Kt), (vT[:, ch, :], vT)]
                        for q, (sap, sb_) in enumerate(srcs):
                            k.tr(ptk[:, 128 * q:128 * (q + 1)], sap, identb[:, :], [sb_, identb], [pb[g]])
                        k.act(tok_[:, :], bank_bf(g)[:, 0:512], AF.Copy, [pb[g]], [tok_])
                        for hp in range(2):
                            rs = HP[hp]
                            k.cp(vtok[rs, ch, :], tok_[rs, 384 + 64 * hp:448 + 64 * hp], [tok_], [vtok], eng="pool")
                    for g in range(4):
                        mats, tok_, X1b_ = ubs[g][0], ubs[g][4], smalls[3 * g]
                        k.mm(bank(g)[:, 256:384], mats[:, 256:384], tok_[:, 384:512], True, True, [mats, tok_], [pb[g]])
                        k.act(X1b_[:, :], bank(g)[:, 256:384], AF.Copy, [pb[g]], [X1b_])
                    for g in range(4):
                        tok_, TTm_, X1b_ = ubs[g][4], ubs[g][3], smalls[3 * g]
                        ApT_, Vp_ = smalls[3 * g + 1], smalls[3 * g + 2]
                        k.mm(bank(g)[:, 0:128], tok_[:, 0:128], TTm_[:, :], True, True, [tok_, TTm_], [pb[g]])
                        k.mm(bank(g)[:, 128:256], TTm_[:, :], X1b_[:, :], True, True, [TTm_, X1b_], [pb[g]])
                        k.act(ApT_[:, :], bank(g)[:, 0:128], AF.Copy, [pb[g]], [ApT_])
                        k.cp(Vp_[:, :], bank(g)[:, 128:256], [pb[g]], [Vp_])
                    if DBGU < 4:
                        continue
                    for g, ch in enumerate(chs):
                        mats, tok_ = ubs[g][0], ubs[g][4]
                        ApT_, Vp_ = smalls[3 * g + 1], smalls[3 * g + 2]
                        Ub_ = Ub2[g % 2]
                        k.mm(bank(6)[:, 0:128], ApT_[:, :], Hb[:, p, :], True, True, [ApT_, Hbb[p]], [pb[6]])
                        k.tt(Ub_[:, :], bank(6)[:, 0:128], Vp_[:, :], add, [pb[6], Vp_], [Ub_])
                        ho = bank(6)[:, 128:256]
                        k.mm(ho, tok_[:, 256:384], tok_[:, 384:512], True, False, [tok_], [pb[6]])
                        k.mm(ho, tok_[:, 128:256], Ub_[:, :], False, True, [tok_, Ub_], [pb[6]])
                        yo = bank(7)[:, 128 * g:128 * (g + 1)]
                        k.mm(yo, AR[:, ch, 1, :], Hb[:, p, :], True, False, [AR, Hbb[p]], [pb[7]])
                        k.mm(yo, mats[:, 128:256], Ub_[:, :], False, False, [mats, Ub_], [pb[7]])
                        k.mm(yo, mats[:, 384:512], tok_[:, 384:512], False, True, [mats, tok_], [pb[7]])
                        gcol = ecw[:, 64 * ch + 63:64 * ch + 64]
                        k.tt(ht[:, :], bank(6)[:, 128:256], Hf[:, p, :], add, [pb[6], Hfb[p]], [ht])
                        k.act(Hb[:, p, :], ht[:, :], AF.Identity, [ht, ecw], [Hbb[p]], scale=gcol)
                        k.ts(Hf[:, p, :], ht[:, :], gcol, None, mult, None, [ht, ecw], [Hfb[p]])
                    if DBGU < 5:
                        continue
                    k.act(usb[0][:, 0:512], bank(7)[:, :], AF.Copy, [pb[7]], [usb[0]])
                    for hp in range(2):
                        rs = HP[hp]
                        k.cp(ysb[rs, 4 * grp:4 * grp + 4, :],
                             usb[0][rs, 0:512].rearrange("p (c d) -> p c d", d=128)[:, :, 64 * hp:64 * hp + 64],
                             [usb[0]], [ysb], eng="pool")
                for g in range(4):
                    k.merge(cpar[g][0], ubs[g][0:2])
                    k.merge(cpar[g][1], ubs[g][2:5])
                k.merge(encw, sm_a)
                k.merge(t2, sm_b)
                k.red(st8[:, 0, 0:nch], ysb[:, 0:nch, :], add, [ysb], [st8])
                k.tt(ysq[:, 0:nch, :], ysb[:, 0:nch, :], ysb[:, 0:nch, :], mult, [ysb], [ysq])
                k.red(st8[:, 1, 0:nch], ysq[:, 0:nch, :], add, [ysq], [st8])
                k.ts(st8[:, 2, 0:nch], st8[:, 0, 0:nch], 1.0 / 64, None, mult, None, [st8], [st8])
                k.tt(st8[:, 4, 0:nch], st8[:, 2, 0:nch], st8[:, 2, 0:nch], mult, [st8], [st8])
                k.stt(st8[:, 3, 0:nch], st8[:, 1, 0:nch], 1.0 / 64, st8[:, 4, 0:nch], mult, sub, [st8], [st8])
                k.act(st8[:, 3, 0:nch], st8[:, 3, 0:nch], AF.Sqrt, [st8, epsc], [st8], bias=epsc[:, 1:2])
                k.recip(st8[:, 3, 0:nch], st8[:, 3, 0:nch], [st8], [st8])
                k.tt(ysb[:, 0:nch, :], ysb[:, 0:nch, :], bc(st8[:, 2, 0:nch], 2, 64), sub, [ysb, st8], [ysb])
                k.tt(ysb[:, 0:nch, :], ysb[:, 0:nch, :], bc(st8[:, 3, 0:nch], 2, 64), mult, [ysb, st8], [ysb])
                k.tt(ysb[:, 0:nch, :], ysb[:, 0:nch, :], bc(lnw[:, p, :], 1, nch), mult, [ysb, lnw], [ysb])
                k.tt(ysb[:, 0:nch, :], ysb[:, 0:nch, :], bc(lnb[:, p, :], 1, nch), add, [ysb, lnb], [ysb])
                k.tt(ysq[:, 0:nch, :], vtok[:, 0:nch, :], bc(st8[:, 5, 0:nch], 2, 64), mult, [vtok, st8], [ysq])
                k.tt(ynb[:, 0:nch, :], ysb[:, 0:nch, :], ysq[:, 0:nch, :], add, [ysb, ysq], [ynb])
                pty = bank_bf(5)
                for ch in range(nch):
                    for hp in range(2):
                        rs = HP[hp]
                        k.tr(pty[rs, 64 * ch:64 * (ch + 1)], ynb[rs, ch, :], identb[rs, rs], [ynb, identb], [pb[5]])
                k.tt(yrT[:, p, 0:n], pty[:, 0:n], gT[:, 0:n], mult, [pb[5], gT], [yrT])
            if dbg:
                for p in range(4):
                    k.cp(um_r[:, 0:n], yrT[:, p, 0:n], [yrT], [um_r])
                    k.dma("sp", dbg_d["yrwT"][:, p, c0:c0 + n], um_r[:, 0:n], reads=[um_r])
            for bi in range(n // 128):
                blk = 4 * t + bi
                cs = slice(bi * 128, bi * 128 + 128)
                for kk in range(4):
                    for hf in range(2):
                        k.mm(bank(hf)[:, :], yrT[:, kk, cs], Wout2[:, kk, 512 * hf:512 * (hf + 1)], kk == 0, kk == 3,
                             [yrT, Wout2], [pb[hf]])
                for hf in range(2):
                    k.tt(x_tok[:, blk, 512 * hf:512 * (hf + 1)], bank(hf)[:, :], x_tok[:, blk, 512 * hf:512 * (hf + 1)],
                         add, [pb[hf], xb[blk]], [xb[blk]])
        k.cp(shifto[:, :], carry[:, :, 0], [carry], [shifto])
        k.dma("sp", o_shift_p, shifto[:, :], reads=[shifto])
        for hp in range(2):
            rs = HP[hp]
            k.cp(hout[rs, :, :], Hf[rs, :, 64 * hp:64 * hp + 64], Hfb, [hout])
        k.dma("sp", o_wkv_p, hout[:, :, :], reads=[hout])
        k.barrier()

    with ExitStack() as es:
        Wout2 = k.sb("Wout2", [128, 4, 1024], BF16, es)
        k.dma("pool", Wout2[:, :, :], w_out_v[:, 4:8, :], writes=[Wout2])
        G2 = k.sb("G2", [128, 512], BF16, es)
        k.dma("pool", G2[:, :], g2_d, writes=[G2])
        lnwr = k.sb("lnwr", [64, 512], F32, es)
        lnbr = k.sb("lnbr", [64, 512], F32, es)
        rkr = k.sb("rkr", [64, 512], F32, es)
        row_bc("ln_w", lnwr, 64)
        row_bc("ln_b", lnbr, 64)
        row_bc("r_k_row", rkr, 64)
        sg = k.sb("sgs", [128, 64], BF16, es)
        tokq = k.sb("tokq", [64, 6, 512], F32, es)
        rq = k.sb("rq", [128, 6, 4, 64], F32, es)
        yq = k.sb("yqw", [128, 4, 64], F32, es)
        sa = k.sb("sa", [128, 64], F32, es)
        yw = k.sb("yw", [64, 512], F32, es)
        ysq = k.sb("ysqs", [64, 512], F32, es)
        st8 = k.sb("st8s", [64, 6, 8], F32, es)
        gtok = k.sb("gtok", [64, 512], F32, es)
        ynb = k.sb("ynbs", [64, 512], BF16, es)
        yrT = k.sb("yrTs", [128, 4, 64], BF16, es)
        scr_q_t = nc.dram_tensor("scr_q", [6, 64, 512], F32)
        scr_w_t = nc.dram_tensor("scr_yw", [64, 512], F32)
        sbq, sbw = Buf("scq"), Buf("scw")
        esa = ExitStack()
        Wir = k.sb("Wir", [128, 8, 1792], BF16, esa)
        for h2 in range(2):
            k.dma("pool", Wir[:, 4 * h2:4 * h2 + 4, :], w_in_v[:, 4 * h2:4 * h2 + 4, 1544:3336], writes=[Wir])
        W2A2 = k.sb("W2A2", [128, 512], BF16, esa)
        k.dma("pool", W2A2[:, :], w2a2_d, writes=[W2A2])
        bon = k.sb("bon", [128, 128], F32, esa)
        o_, s_ = KO2["bones"]
        k.dma("sp", bon[:, :], consts2_d[:, o_:o_ + s_], writes=[bon])
        omm = k.sb("omm", [128, 14], F32, esa)
        k.ts(omm[:, :], col("mu", 0, 14), -1.0, 1.0, mult, add, [cols], [omm])
        usb = [k.sb("usbs", [128, 80], F32, esa) for _ in range(2)]
        tmpm = k.sb("tmpms", [128, 64], F32, esa)
        carry = k.sb("carrys", [128, 14, 16], F32, esa)
        k.dma("sp", carry[:, :, :], st_shift_d, writes=[carry])
        tlla = k.sb("tllas", [128, 64], BF16, esa)
        f = lambda nm: k.sb(nm, [128, 64], F32, esa)
        um_r, um_k, um_v, sigw, a_t, kkn, t1, t2, wdec = [f(n_) for n_ in
                                                          "um_r um_k um_v sigw a_t kkn t1 t2 wdec".split()]
        n = 64
        S = 16
        c0 = 2048
        hbufs = [hTb[16]]

        def proj_chunk_s(c, dst, ucount):
            bi_ = ucount % 2
            u_ = usb[ucount % 2]
            for kk in range(8):
                k.mm(bank(bi_)[:, 0:n], Wir[:, kk, 128 * c:128 * (c + 1)], hT[:, kk, c0:c0 + n], kk == 0, kk == 7,
                     [Wir] + hbufs, [pb[bi_]])
            k.act(u_[:, S:S + n], bank(bi_)[:, 0:n], AF.Copy, [pb[bi_]], [u_])
            k.cp(u_[:, 0:S], carry[:, c, 0:S], [carry], [u_], eng="pool")
            k.act(tmpm[:, 0:n], u_[:, 0:n], AF.Identity, [u_, cols], [tmpm], scale=col("mu", c))
            k.stt(dst[:, 0:n], u_[:, S:S + n], omm[:, c:c + 1], tmpm[:, 0:n], mult, add, [u_, omm, tmpm], [dst])
            k.cp(carry[:, c, 0:S], u_[:, n:n + S], [u_], [carry], eng="pool")

        uc = 0
        proj_chunk_s(12, um_v, uc); uc += 1
        k.act(tlla[0:64, :], um_v[0:64, :], AF.Tanh, [um_v], [tlla])
        k.act(tlla[64:128, :], um_v[64:128, :], AF.Copy, [um_v], [tlla])
        proj_chunk_s(13, um_v, uc); uc += 1
        k.act(sg[:, :], um_v[:, :], AF.Sigmoid, [um_v], [sg])
        for p in range(4):
            pc = slice(128 * p, 128 * (p + 1))
            k.mm(bank(2)[:, 0:n], W2A2[0:64, pc], tlla[0:64, :], True, True, [W2A2, tlla], [pb[2]])
            k.mm(bank(3)[:, 0:n], W2A2[64:128, pc], tlla[64:128, :], True, True, [W2A2, tlla], [pb[3]])
            k.act(sigw[:, :], bank(2)[:, 0:n], AF.Sigmoid, [pb[2], cols], [sigw], bias=col("w0", p))
            k.act(a_t[:, :], bank(3)[:, 0:n], AF.Sigmoid, [pb[3], cols], [a_t], bias=col("a0", p))
            proj_chunk_s(p, um_r, uc); uc += 1
            proj_chunk_s(4 + p, um_k, uc); uc += 1
            proj_chunk_s(8 + p, um_v, uc); uc += 1
            k.ts(t1[:, :], um_k[:, :], col("k_k", p), None, mult, None, [um_k, cols], [t1])
            k.tt(t2[:, :], t1[:, :], t1[:, :], mult, [t1], [t2])
            k.mm(bank(2)[:, 0:n], bon[:, :], t2[:, :], True, True, [bon, t2], [pb[2]])
            k.act(t2[:, :], bank(2)[:, 0:n], AF.Sqrt, [pb[2]], [t2])
            k.ts(t2[:, :], t2[:, :], 1e-12, None, ALU.max, None, [t2], [t2])
            k.recip(t2[:, :], t2[:, :], [t2], [t2])
            k.tt(kkn[:, :], t1[:, :], t2[:, :], mult, [t1, t2], [kkn])
            k.ts(t1[:, :], a_t[:, :], -1.0, col("k_a", p), add, mult, [a_t, cols], [t1])
            k.stt(t1[:, :], t1[:, :], 1.0, um_k[:, :], add, mult, [t1, um_k], [t1])
            k.tt(t2[:, :], a_t[:, :], kkn[:, :], mult, [a_t, kkn], [t2])
            k.act(wdec[:, :], sigw[:, :], AF.Exp, [sigw], [wdec], scale=-C0)
            srcs = [um_r, wdec, t1, um_v, kkn, t2]
            for q, sb_ in enumerate(srcs):
                bk = 4 + q // 3
                k.tr(bank(bk)[0:64, 128 * (q % 3):128 * (q % 3 + 1)], sb_[:, :], ident, [sb_, consts], [pb[bk]])
            for hq in range(2):
                k.act(tokq[:, 3 * hq:3 * hq + 3, pc], bank(4 + hq)[0:64, 0:384].rearrange("p (q d) -> p q d", q=3),
                      AF.Copy, [pb[4 + hq]], [tokq])
        k.mm(bank(2)[0:64, :], sg[:, :], G2[:, :], True, True, [sg, G2], [pb[2]])
        k.act(gtok[:, :], bank(2)[0:64, :], AF.Copy, [pb[2]], [gtok])
        k.dma("sp", o_shift_s, carry[:, :, :], reads=[carry])
        k.dma("sp", scr_q_t.ap().rearrange("q t d -> t q d"), tokq[:, :, :], reads=[tokq], writes=[sbq])
        k.dma("sp", rq[:, :, :, :], bass.AP(scr_q_t, 0, [[64, 128], [32768, 6], [8192, 4], [1, 64]]), reads=[sbq],
              writes=[rq])
        k.barrier()
        esa.close()
        Sw = k.sb("Sw", [128, 4096], F32, es)
        tmpW = k.sb("tmpW", [128, 4096], F32, es)
        tmpV = k.sb("tmpV", [128, 4096], F32, es)
        k.dma("sp", Sw[:, :], st_wkv_d, writes=[Sw])
        m3 = lambda ap: ap.rearrange("p (i j) -> p i j", j=64)
        for l in range(4):
            k.tt(m3(tmpV[:, :]), bc(rq[:, 3, l, :], 2, 64), bc(rq[:, 2, l, :], 1, 64), mult, [rq], [tmpV], eng="pool")
            k.tt(m3(tmpW[:, :]), m3(Sw[:, :]), bc(rq[:, 4, l, :], 1, 64), mult, [Sw, rq], [tmpW])
            k.red(sa[:, :], m3(tmpW[:, :]), add, [tmpW], [sa], negate=True)
            k.tt(m3(Sw[:, :]), m3(Sw[:, :]), bc(rq[:, 1, l, :], 1, 64), mult, [Sw, rq], [Sw])
            k.tt(m3(tmpW[:, :]), bc(sa[:, :], 2, 64), bc(rq[:, 5, l, :], 1, 64), mult, [sa, rq], [tmpW])
            k.tt(Sw[:, :], Sw[:, :], tmpW[:, :], add, [Sw, tmpW], [Sw])
            k.tt(Sw[:, :], Sw[:, :], tmpV[:, :], add, [Sw, tmpV], [Sw])
            k.tt(m3(tmpW[:, :]), m3(Sw[:, :]), bc(rq[:, 0, l, :], 1, 64), mult, [Sw, rq], [tmpW])
            k.red(yq[:, l, :], m3(tmpW[:, :]), add, [tmpW], [yq])
        k.dma("sp", o_wkv_s, Sw[:, :], reads=[Sw])
        k.dma("sp", bass.AP(scr_w_t, 0, [[64, 128], [8192, 4], [1, 64]]), yq[:, :, :], reads=[yq], writes=[sbw])
        k.dma("sp", yw[:, :], scr_w_t.ap(), reads=[sbw], writes=[yw])
        y8 = lambda ap: ap.rearrange("p (h d) -> p h d", h=8)
        k.red(st8[:, 0, :], y8(yw[:, :]), add, [yw], [st8])
        k.tt(ysq[:, :], yw[:, :], yw[:, :], mult, [yw], [ysq])
        k.red(st8[:, 1, :], y8(ysq[:, :]), add, [ysq], [st8])
        k.ts(st8[:, 2, :], st8[:, 0, :], 1.0 / 64, None, mult, None, [st8], [st8])
        k.tt(st8[:, 4, :], st8[:, 2, :], st8[:, 2, :], mult, [st8], [st8])
        k.stt(st8[:, 3, :], st8[:, 1, :], 1.0 / 64, st8[:, 4, :], mult, sub, [st8], [st8])
        k.act(st8[:, 3, :], st8[:, 3, :], AF.Sqrt, [st8, epsc], [st8], bias=epsc[0:64, 1:2])
        k.recip(st8[:, 3, :], st8[:, 3, :], [st8], [st8])
        k.tt(y8(yw[:, :]), y8(yw[:, :]), bc(st8[:, 2, :], 2, 64), sub, [yw, st8], [yw])
        k.tt(y8(yw[:, :]), y8(yw[:, :]), bc(st8[:, 3, :], 2, 64), mult, [yw, st8], [yw])
        k.tt(yw[:, :], yw[:, :], lnwr[:, :], mult, [yw, lnwr], [yw])
        k.tt(yw[:, :], yw[:, :], lnbr[:, :], add, [yw, lnbr], [yw])
        k.tt(ysq[:, :], tokq[:, 0, :], tokq[:, 2, :], mult, [tokq], [ysq])
        k.tt(ysq[:, :], ysq[:, :], rkr[:, :], mult, [ysq, rkr], [ysq])
        k.red(st8[:, 5, :], y8(ysq[:, :]), add, [ysq], [st8])
        k.tt(y8(ysq[:, :]), y8(tokq[:, 3, :]), bc(st8[:, 5, :], 2, 64), mult, [tokq, st8], [ysq])
        k.tt(yw[:, :], yw[:, :], ysq[:, :], add, [yw, ysq], [yw])
        k.tt(yw[:, :], yw[:, :], gtok[:, :], mult, [yw, gtok], [yw])
        if dbg:
            k.dma("sp", dbg_d["yrws"], yw[:, :], reads=[yw])
        k.cp(ynb[:, :], yw[:, :], [yw], [ynb])
        pty = bank_bf(6)[:, 0:256].rearrange("p (c t) -> p c t", c=4)
        for c in range(4):
            k.tr(pty[:, c, :], ynb[:, 128 * c:128 * (c + 1)], identb[0:64, 0:64], [ynb, identb], [pb[6]])
        k.act(yrT[:, :, :], pty[:, :, :], AF.Copy, [pb[6]], [yrT])
        for hf in range(2):
            bk = hf
            for kk in range(4):
                k.mm(bank(bk)[0:64, :], yrT[:, kk, :], Wout2[:, kk, 512 * hf:512 * (hf + 1)], kk == 0, kk == 3,
                     [yrT, Wout2], [pb[bk]])
            k.tt(x_tok[0:64, 16, 512 * hf:512 * (hf + 1)], bank(bk)[0:64, :], x_tok[0:64, 16, 512 * hf:512 * (hf + 1)],
                 add, [pb[bk], xb[16]], [xb[16]])
        k.barrier()

    if dbg:
        for blk in range(17):
            rows = blk_rows(blk)
            k.dma("sp", dbg_d["x1"][blk * 128:blk * 128 + rows, :], x_tok[0:rows, blk, :], reads=[xb[blk]])

    if stop == "C":
        k.finish()
        return nc
    with ExitStack() as es:
        KT = k.sb("KT", [128, 8, 256], BF16, es)
        Vtok = k.sb("Vtok", [128, 2, 1024], BF16, es)
        junk = k.sb("junk", [128, 1024], BF16, es)
        hb2 = [k.sb("hb", [128, 1024], BF16, es) for _ in range(2)]
        onesbf = k.sb("onesbf", [128, 128], BF16, es)
        k.memset(onesbf[:, :], 1.0, [onesbf])
        with ExitStack() as es0:
            Wk, WkB = load_weight("Wk", wk_d.rearrange("(c p) n -> p c n", p=128), 1024,
                                  [(256 * c_, 256 * c_ + 256) for c_ in range(4)], es0)
            Wv, WvB = load_weight("Wv", wv_d.rearrange("(c p) n -> p c n", p=128), 1024, [(0, 512), (512, 1024)], es0)
            memtok = k.sb("memtok", [128, 2, 1024], F32, es0)
            mT = k.sb("mT", [128, 8, 256], BF16, es0)
            mss = k.sb("mss", [128, 2, 4], F32, es0)
            kvst = [k.sb("kvst", [128, 1024], F32, es0) for _ in range(2)]
            row_bc("mem_norm", gbc)
            k.dma("sp", memtok[:, :, :], mem_d.rearrange("(b p) d -> p b d", p=128), writes=[memtok])
            for mb in range(2):
                hb = hb2[mb]
                k.act(junk[:, :], memtok[:, mb, :], AF.Square, [memtok], [junk, mss], accum_out=mss[:, mb, 0:1])
                k.act(mss[:, mb, 1:2], mss[:, mb, 0:1], AF.Sqrt, [mss, epsc], [mss], scale=1.0 / 1024, bias=epsc[:, 0:1])
                k.recip(mss[:, mb, 2:3], mss[:, mb, 1:2], [mss], [mss])
                k.stt(hb[:, :], memtok[:, mb, :], mss[:, mb, 2:3], gbc[:, :], mult, mult, [memtok, mss, gbc], [hb])
                pt = bank_bf(6)[:, :].rearrange("p (c t) -> p c t", c=8)
                for c in range(8):
                    k.tr(pt[:, c, :], hb[:, c * 128:(c + 1) * 128], identb[:, :], [hb, identb], [pb[6]])
                k.act(mT[:, :, mb * 128:(mb + 1) * 128], pt[:, :, :], AF.Copy, [pb[6]], [mT])
            for c in range(8):
                bi_ = c % 2
                for kk in range(8):
                    k.mm(bank(bi_)[:, 0:256], Wk[:, kk, 128 * c:128 * (c + 1)], mT[:, kk, :], kk == 0, kk == 7,
                         WkB(128 * c, 128 * c + 128) + [mT], [pb[bi_]])
                k.act(KT[:, c, :], bank(bi_)[:, 0:256], AF.Copy, [pb[bi_]], [KT])
            ci = 0
            for W_, WB_, od, isv in ((Wk, WkB, o_mk, False), (Wv, WvB, o_mv, True)):
                for mb in range(2):
                    st_ = kvst[ci % 2]
                    ci += 1
                    for hf in range(2):
                        bk = 2 + hf
                        for kk in range(8):
                            k.mm(bank(bk)[:, :], mT[:, kk, mb * 128:(mb + 1) * 128], W_[:, kk, 512 * hf:512 * (hf + 1)],
                                 kk == 0, kk == 7, WB_(512 * hf, 512 * hf + 512) + [mT], [pb[bk]])
                        k.act(st_[:, 512 * hf:512 * (hf + 1)], bank(bk)[:, :], AF.Copy, [pb[bk]], [st_])
                        if isv:
                            k.cp(Vtok[:, mb, 512 * hf:512 * (hf + 1)], st_[:, 512 * hf:512 * (hf + 1)], [st_], [Vtok])
                    k.dma("sp", od[mb * 128:(mb + 1) * 128, :], st_[:, :], reads=[st_])
            k.barrier()
        Wq, WqB = load_weight("Wq", wq_d.rearrange("(c p) n -> p c n", p=128), 1024,
                              [(256 * c_, 256 * c_ + 256) for c_ in range(4)], es)
        Wo, WoB = load_weight("Wo", wo_d.rearrange("(c p) n -> p c n", p=128), 1024, [(0, 512), (512, 1024)], es)
        row_bc("norm_xa", gbc)
        for blk in range(17):
            norm_T(blk, es, (junk, hb2[blk % 2]))
        qT = k.sb("qT", [128, 8, 512], BF16, es)
        oT = k.sb("oT", [128, 8, 512], BF16, es)
        pT = [k.sb("pT", [128, 512], BF16, es) for _ in range(2)]
        rden = k.sb("rden", [128, 512], F32, es)
        NKV = 2
        KTs = [k.sb("KTs", [128, 8, 256], BF16, es) for _ in range(NKV)]
        Kst = [k.sb("Kst", [128, 2, 1024], BF16, es) for _ in range(NKV)]
        Vst = [k.sb("Vst", [128, 2, 1024], BF16, es) for _ in range(NKV)]

        def attend(KT_, V_, qcols, ocols, n, rd):
            for hd in range(4):
                for mb in range(2):
                    bk = 2 + mb
                    for j in range(2):
                        k.mm(bank(bk)[:, 0:n], KT_[:, 2 * hd + j, mb * 128:(mb + 1) * 128], qT[:, 2 * hd + j, qcols],
                             j == 0, j == 1, rd + [qT], [pb[bk]])
                    k.act(pT[mb][:, 0:n], bank(bk)[:, 0:n], AF.Exp, [pb[bk]], [pT[mb]])
                for mb in range(2):
                    k.mm(bank(4)[:, 0:n], onesbf[:, :], pT[mb][:, 0:n], mb == 0, mb == 1, [onesbf, pT[mb]], [pb[4]])
                k.recip(rden[:, 0:n], bank(4)[:, 0:n], [pb[4]], [rden])
                for j in range(2):
                    dc = 2 * hd + j
                    for mb in range(2):
                        k.mm(bank(5)[:, 0:n], V_[:, mb, 128 * dc:128 * (dc + 1)], pT[mb][:, 0:n], mb == 0, mb == 1,
                             rd + [pT[mb]], [pb[5]])
                    k.tt(oT[:, dc, ocols], bank(5)[:, 0:n], rden[:, 0:n], mult, [pb[5], rden], [oT])

        for t in range(5):
            if stop == "D1p" and t == 4:
                break
            if stop == "D0":
                break
            n = 512 if t < 4 else 64
            c0 = t * 512
            nb = 4 if t < 4 else 1
            hbufs = hTb[4 * t:4 * t + nb]
            for c in range(8):
                bi_ = c % 2
                for kk in range(8):
                    k.mm(bank(bi_)[:, 0:n], Wq[:, kk, 128 * c:128 * (c + 1)], hT[:, kk, c0:c0 + n], kk == 0, kk == 7,
                         WqB(128 * c, 128 * c + 128) + hbufs, [pb[bi_]])
                k.act(qT[:, c, 0:n], bank(bi_)[:, 0:n], AF.Copy, [pb[bi_]], [qT], scale=0.0625)
            if t < 4:
                attend(KT, Vtok, slice(0, n), slice(0, n), n, [KT, Vtok])
            else:
                for s_ in range(16):
                    Ks_, Vs_, KTs_ = Kst[s_ % NKV], Vst[s_ % NKV], KTs[s_ % NKV]
                    k.dma("pool", Ks_[:, :, :], ck_d[s_].rearrange("(b p) d -> p b d", p=128), writes=[Ks_])
                    k.dma("pool", Vs_[:, :, :], cv_d[s_].rearrange("(b p) d -> p b d", p=128), writes=[Vs_])
                    for mb in range(2):
                        ptk = bank_bf(6 + mb)[:, :].rearrange("p (c t) -> p c t", c=8)
                        for c in range(8):
                            k.tr(ptk[:, c, :], Ks_[:, mb, 128 * c:128 * (c + 1)], identb[:, :], [Ks_, identb],
                                 [pb[6 + mb]])
                        k.act(KTs_[:, :, mb * 128:(mb + 1) * 128], ptk[:, :, :], AF.Copy, [pb[6 + mb]], [KTs_])
                    sel = slice(s_, 64, 16)
                    for hd in range(4):
                        for mb in range(2):
                            for j in range(2):
                                k.mm(bank(2 + mb)[:, 4 * hd:4 * hd + 4], KTs_[:, 2 * hd + j, mb * 128:(mb + 1) * 128],
                                     qT[:, 2 * hd + j, sel], j == 0, j == 1, [KTs_, qT], [pb[2 + mb]])
                    for mb in range(2):
                        k.act(pT[mb][:, 0:16], bank(2 + mb)[:, 0:16], AF.Exp, [pb[2 + mb]], [pT[mb]])
                    for mb in range(2):
                        k.mm(bank(4)[:, 0:16], onesbf[:, :], pT[mb][:, 0:16], mb == 0, mb == 1, [onesbf, pT[mb]], [pb[4]])
                    k.recip(rden[:, 0:16], bank(4)[:, 0:16], [pb[4]], [rden])
                    for dc in range(8):
                        hd = dc // 2
                        for mb in range(2):
                            k.mm(bank(5)[:, 4 * dc:4 * dc + 4], Vs_[:, mb, 128 * dc:128 * (dc + 1)],
                                 pT[mb][:, 4 * hd:4 * hd + 4], mb == 0, mb == 1, [Vs_, pT[mb]], [pb[5]])
                    k.tt(oT[:, :, sel].rearrange("p (h j) t -> p h j t", j=2),
                         bank(5)[:, 0:32].rearrange("p (h j t) -> p h j t", h=4, j=2),
                         bc(rden[:, 0:16].rearrange("p (h t) -> p h t", h=4), 2, 2), mult, [pb[5], rden], [oT])
            for bi in range(nb):
                blk = 4 * t + bi
                rows = blk_rows(blk)
                cs = slice(bi * 128, bi * 128 + rows)
                for kk in range(8):
                    for hf in range(2):
                        k.mm(bank(hf)[0:rows, :], oT[:, kk, cs], Wo[:, kk, 512 * hf:512 * (hf + 1)], kk == 0, kk == 7,
                             [oT] + WoB(512 * hf, 512 * hf + 512), [pb[hf]])
                for hf in range(2):
                    k.tt(x_tok[0:rows, blk, 512 * hf:512 * (hf + 1)], bank(hf)[0:rows, :],
                         x_tok[0:rows, blk, 512 * hf:512 * (hf + 1)], add, [pb[hf], xb[blk]], [xb[blk]])
        k.barrier()

    if dbg:
        for blk in range(17):
            rows = blk_rows(blk)
            k.dma("sp", dbg_d["x2"][blk * 128:blk * 128 + rows, :], x_tok[0:rows, blk, :], reads=[xb[blk]])

    if stop in ("D", "D0", "D1p"):
        k.finish()
        return nc
    with ExitStack() as es:
        junk = k.sb("junk", [128, 1024], BF16, es)
        hb2 = [k.sb("hb", [128, 1024], BF16, es) for _ in range(2)]
        row_bc("norm_ffn", gbc)
        for blk in range(17):
            norm_T(blk, es, (junk, hb2[blk % 2]))
        k.barrier()
    with ExitStack() as es:
        G = 3
        Wup = [k.sb("Wup", [128, 8, 2 * G * 128], BF16, es) for _ in range(2)]
        Wdn = [k.sb("Wdn", [128, G, 1024], BF16, es) for _ in range(2)]
        wup_v = wup_d.rearrange("(c p) n -> p c n", p=128)
        wdn_v = wdn_d.rearrange("(f p) n -> p f n", p=128)
        pre = [k.sb("pre", [128, 2050], F32, es) for _ in range(2)]
        accf = [k.sb("accf", [128, 2048], F32, es) for _ in range(2)]
        sgt = accf[0]
        actb = k.sb("actb", [128, G, 2112], BF16, es)
        pres = [k.sb("pres", [128, 96], F32, es) for _ in range(2)]
        carF = k.sb("carF", [128, 44, 2], F32, es)
        carS = k.sb("carS", [128, 44, 2, 16], F32, es)
        k.dma("sp", carS[:, :, :, :], st_ffn_d, writes=[carS])
        for pr in pre:
            k.memset(pr[:, 0:2], 0.0, [pr])
        groups = [(f0, min(G, 22 - f0)) for f0 in range(0, 22, G)]
        pcount = 0
        dcount = 0
        hb_all = hTb[0:16]
        for gi, (f0, gn) in enumerate(groups):
            Wu, Wd = Wup[gi % 2], Wdn[gi % 2]
            k.dma("pool", Wu[:, :, 0:gn * 128], wup_v[:, :, f0 * 128:(f0 + gn) * 128], writes=[Wu])
            k.dma("pool", Wu[:, :, G * 128:G * 128 + gn * 128], wup_v[:, :, 2816 + f0 * 128:2816 + (f0 + gn) * 128],
                  writes=[Wu])
            k.dma("pool", Wd[:, 0:gn, :], wdn_v[:, f0:f0 + gn, :], writes=[Wd])
            for i in range(gn):
                for half in range(2):
                    fc = (f0 + i) + 22 * half
                    wc = half * G * 128 + i * 128
                    pr = pre[pcount % 2]
                    pcount += 1
                    for kk in range(8):
                        for tt_ in range(4):
                            k.mm(bank(tt_)[:, :], Wu[:, kk, wc:wc + 128], hT[:, kk, tt_ * 512:(tt_ + 1) * 512], kk == 0,
                                 kk == 7, [Wu] + hb_all, [pb[tt_]])
                    for h2 in range(2):
                        k.act(pr[:, 2 + 1024 * h2:2 + 1024 * (h2 + 1)], psA[h2][:, :, :].rearrange("p a b -> p (a b)"),
                              AF.Copy, [pb[2 * h2], pb[2 * h2 + 1]], [pr])
                    ac = accf[half]
                    k.act(ac[:, :], pr[:, 0:2048], AF.Identity, [pr, cols], [ac], scale=col("ffn_cw", 3 * fc),
                          bias=col("ffn_cb", fc))
                    for j in range(1, 3):
                        k.stt(ac[:, :], pr[:, j:j + 2048], col("ffn_cw", 3 * fc + j), ac[:, :], mult, add,
                              [pr, cols, ac], [ac])
                    k.cp(carF[:, fc, :], pr[:, 2048:2050], [pr], [carF], eng="pool")
                k.act(sgt[:, :], accf[0][:, :], AF.Silu, [accf[0]], [accf[0]])
                k.tt(actb[:, i, 0:2048], sgt[:, :], accf[1][:, :], mult, [accf[0], accf[1]], [actb])
                n, S, c0 = 64, 16, 2048
                for half in range(2):
                    fc = (f0 + i) + 22 * half
                    wc = half * G * 128 + i * 128
                    prs = pres[half]
                    bk = 6 + half
                    for kk in range(8):
                        k.mm(bank(bk)[:, 0:n], Wu[:, kk, wc:wc + 128], hT[:, kk, c0:c0 + n], kk == 0, kk == 7,
                             [Wu, hTb[16]], [pb[bk]])
                    k.act(prs[:, 32:96], bank(bk)[:, 0:n], AF.Copy, [pb[bk]], [prs])
                    k.cp(prs[:, 0:32], carS[:, fc, :, :].rearrange("p j s -> p (j s)"), [carS], [prs], eng="pool")
                    ac = accf[half]
                    k.act(ac[:, 0:n], prs[:, 0:n], AF.Identity, [prs, cols], [ac], scale=col("ffn_cw", 3 * fc),
                          bias=col("ffn_cb", fc))
                    for j in range(1, 3):
                        k.stt(ac[:, 0:n], prs[:, j * S:j * S + n], col("ffn_cw", 3 * fc + j), ac[:, 0:n], mult, add,
                              [prs, cols, ac], [ac])
                    k.cp(carS[:, fc, :, :].rearrange("p j s -> p (j s)"), prs[:, n:n + 32], [prs], [carS], eng="pool")
                k.act(sgt[:, 0:n], accf[0][:, 0:n], AF.Silu, [accf[0]], [accf[0]])
                k.tt(actb[:, i, 2048:2112], sgt[:, 0:n], accf[1][:, 0:n], mult, [accf[0], accf[1]], [actb])
            for blk in range(17):
                rows = blk_rows(blk)
                cs = slice(blk * 128, blk * 128 + rows)
                bb = 4 + 2 * (dcount % 2)
                dcount += 1
                for i in range(gn):
                    for hf in range(2):
                        k.mm(bank(bb + hf)[0:rows, :], actb[:, i, cs], Wd[:, i, 512 * hf:512 * (hf + 1)], i == 0,
                             i == gn - 1, [actb, Wd], [pb[bb + hf]])
                for hf in range(2):
                    k.tt(x_tok[0:rows, blk, 512 * hf:512 * (hf + 1)], bank(bb + hf)[0:rows, :],
                         x_tok[0:rows, blk, 512 * hf:512 * (hf + 1)], add, [pb[bb + hf], xb[blk]], [xb[blk]])
        k.dma("sp", o_ffn_p, carF[:, :, :], reads=[carF])
        k.dma("sp", o_ffn_s, carS[:, :, :, :], reads=[carS])
        k.barrier()

    if stop == "E":
        k.finish()
        return nc
    with ExitStack() as es:
        junk = k.sb("junk", [128, 1024], BF16, es)
        yo = [k.sb("yo", [128, 1024], F32, es) for _ in range(2)]
        row_bc("final_norm", gbc)
        for blk in range(17):
            rows = blk_rows(blk)
            xin = x_tok[0:rows, blk, :]
            ss = ssx[0:rows, blk, :]
            y_ = yo[blk % 2]
            k.act(junk[0:rows, :], xin, AF.Square, [xb[blk]], [junk, ssb[blk]], accum_out=ss[:, 0:1])
            k.act(ss[:, 1:2], ss[:, 0:1], AF.Sqrt, [ssb[blk], epsc], [ssb[blk]], scale=1.0 / 1024, bias=epsc[0:rows, 0:1])
            k.recip(ss[:, 2:3], ss[:, 1:2], [ssb[blk]], [ssb[blk]])
            k.stt(y_[0:rows, :], xin, ss[:, 2:3], gbc[0:rows, :], mult, mult, [xb[blk], ssb[blk], gbc], [y_])
            dst = y_p_d[blk * 128:(blk + 1) * 128, :] if blk < 16 else y_s_d
            k.dma("sp", dst, y_[0:rows, :], reads=[y_])
        k.barrier()
    k.finish()
    return nc


def _consts():
    c = np.zeros((128, NCONST), np.float32)
    c2 = np.zeros((128, NCONST2), np.float32)
    i = np.arange(128)
    o, s = KO["ident"]
    c[:, o:o + s] = np.eye(128)
    o, s = KO["tri"]
    c[:, o:o + s] = (i[:, None] <= i[None, :])
    o, s = KO["lm"]
    c[:, o:o + s] = (i[:, None] > i[None, :])
    o, s = KO["ones"]
    c[:, o:o + s] = 1.0
    s64 = np.arange(64)
    su = (s64[:, None] < s64[None, :]).astype(np.float32)
    iu = (s64[:, None] <= s64[None, :]).astype(np.float32)
    sl = (s64[:, None] > s64[None, :]).astype(np.float32)
    def bd(m_):
        z_ = np.zeros((128, 128), np.float32)
        z_[:64, :64] = m_
        z_[64:, 64:] = m_
        return z_
    o, s = KO2["wmask"]
    c2[:, o:o + s] = np.concatenate([bd(su), bd(iu), bd(su), bd(iu), bd(sl)], axis=1)
    o, s = KO2["i64"]
    c2[:, o:o + s] = np.concatenate([np.eye(64), np.eye(64)], axis=0)
    o, s = KO2["bones"]
    bo = np.zeros((128, 128), np.float32)
    bo[:64, :64] = 1
    bo[64:, 64:] = 1
    c2[:, o:o + s] = bo
    o, s = KO2["scanm"]
    sm = np.ones(512, np.float32)
    sm[::64] = 0
    c2[:, o:o + s] = sm[None, :]
    return c, c2


def _fm(v, nch):
    return np.ascontiguousarray(np.asarray(v, np.float32).reshape(nch, 128).T)


def _prep_shared(inp):
    sh = {}
    f = lambda a: np.ascontiguousarray(np.asarray(a, np.float32))
    sh["w_in"] = f(inp["w_in"][0])
    sh["w_out"] = f(inp["w_out"][0])
    sh["wq"] = f(inp["xa_w_q"][0])
    sh["wk"] = f(inp["xa_w_k"][0])
    sh["wv"] = f(inp["xa_w_v"][0])
    sh["wo"] = f(inp["xa_w_o"][0])
    sh["wup"] = f(inp["ffn_w_up"][0])
    sh["wdn"] = f(inp["ffn_w_down"][0])
    sh["w2a2"] = f(np.concatenate([inp["rwkv_w2"][0], inp["rwkv_a2"][0]], axis=0))
    sh["g2"] = f(inp["rwkv_g2"][0])
    rows = np.zeros((1, NROW), np.float32)
    src = {"norm_mix": inp["norm_mix_w"][0], "norm_xa": inp["norm_xa_w"][0], "mem_norm": inp["mem_norm_w"][0],
           "norm_ffn": inp["norm_ffn_w"][0], "final_norm": inp["final_norm_w"], "ssd_norm": inp["ssd_norm_w"][0],
           "dt_bias": inp["ssd_dt_bias"][0], "a_log": inp["ssd_a_log"][0], "ssd_d": inp["ssd_d"][0],
           "ln_w": inp["rwkv_ln_w"][0], "ln_b": inp["rwkv_ln_b"][0], "r_k_row": inp["rwkv_r_k"][0]}
    for nme, (o, s) in RO.items():
        rows[0, o:o + s] = np.asarray(src[nme], np.float32).reshape(-1)
    sh["rows"] = rows
    cols = np.zeros((128, NCOL), np.float32)

    def put(nme, arr):
        o, s = CO[nme]
        cols[:, o:o + s] = arr.reshape(128, s)

    cw = np.asarray(inp["ssd_conv_w"][0], np.float32)
    put("ssd_cw", np.ascontiguousarray(cw.T.reshape(8, 128, 4).transpose(1, 0, 2)))
    put("ssd_cb", _fm(inp["ssd_conv_b"][0], 8))
    put("mu", _fm(inp["rwkv_mu"][0], 14))
    put("w0", _fm(inp["rwkv_w0"][0], 4))
    put("a0", _fm(inp["rwkv_a0"][0], 4))
    put("k_k", _fm(inp["rwkv_k_k"][0], 4))
    put("k_a", _fm(inp["rwkv_k_a"][0], 4))
    put("r_k", _fm(np.asarray(inp["rwkv_r_k"][0]).reshape(-1), 4))
    fw = np.asarray(inp["ffn_conv_w"][0], np.float32)
    put("ffn_cw", np.ascontiguousarray(fw.T.reshape(44, 128, 3).transpose(1, 0, 2)))
    put("ffn_cb", _fm(inp["ffn_conv_b"][0], 44))
    sh["cols"] = cols
    sh["consts"], sh["consts2"] = _consts()
    return sh


def _prep_core(inp, c):
    f = lambda a: np.ascontiguousarray(np.asarray(a, np.float32))
    sl = slice(16 * c, 16 * c + 16)
    m = {}
    m["xp"] = f(inp["x_prompt"][c])
    m["xs"] = f(np.asarray(inp["x_sample"][sl]).transpose(1, 0, 2).reshape(64, 1024))
    m["mem"] = f(inp["mem_prompt"][c])
    sc = np.asarray(inp["state_ssm_conv"][0, sl])
    m["st_conv"] = f(sc.transpose(2, 1, 0).reshape(8, 128, 3, 16).transpose(1, 0, 2, 3))
    m["st_ssm"] = f(np.asarray(inp["state_ssm"][0, sl]).reshape(128, 8192))
    ss = np.asarray(inp["state_shift"][0, sl])
    m["st_shift"] = f(ss.T.reshape(14, 128, 16).transpose(1, 0, 2))
    m["st_wkv"] = f(np.asarray(inp["state_wkv"][0, sl]).reshape(128, 4096))
    sf = np.asarray(inp["state_ffn_conv"][0, sl])
    m["st_ffn"] = f(sf.transpose(2, 1, 0).reshape(44, 128, 2, 16).transpose(1, 0, 2, 3))
    m["ck"] = f(np.asarray(inp["cache_mem_k"][0, sl]).reshape(16, 256, 1024))
    m["cv"] = f(np.asarray(inp["cache_mem_v"][0, sl]).reshape(16, 256, 1024))
    return m


_NC_CACHE = {}


def _get_nc(stop="all", dbg=False):
    key = (stop, dbg)
    if key not in _NC_CACHE:
        _NC_CACHE[key] = build(stop, dbg)
    return _NC_CACHE[key]


def run_raw(inp, stop="all", dbg=False):
    nc = _get_nc(stop, dbg)
    sh = _prep_shared(inp)
    in_maps = []
    for c in range(NCORES):
        m = dict(sh)
        m.update(_prep_core(inp, c))
        in_maps.append(m)
    res = run_bass_kernel_spmd(nc, in_maps, core_ids=list(range(NCORES)))
    return res.results


def kernel(**inp):
    rs = run_raw(inp)
    f32 = np.float32
    y_p = np.stack([r["y_p"] for r in rs]).astype(f32)
    y_s = np.concatenate([r["y_s"].reshape(4, 16, 1024).transpose(1, 0, 2) for r in rs]).astype(f32)
    conv_p = np.stack([r["o_conv_p"].transpose(2, 1, 0).reshape(3, 1024) for r in rs])[None]
    conv_s = np.concatenate([r["o_conv_s"].transpose(3, 2, 1, 0).reshape(16, 3, 1024) for r in rs])[None]
    ssm_p = np.stack([r["o_ssm_p"].reshape(8, 64, 128) for r in rs])[None]
    ssm_s = np.concatenate([r["o_ssm_s"].reshape(16, 8, 64, 128) for r in rs])[None]
    shift_p = np.stack([r["o_shift_p"].T.reshape(1792) for r in rs])[None]
    shift_s = np.concatenate([r["o_shift_s"].transpose(2, 1, 0).reshape(16, 1792) for r in rs])[None]
    wkv_p = np.stack([r["o_wkv_p"].reshape(2, 64, 4, 64).transpose(2, 0, 3, 1).reshape(8, 64, 64) for r in rs])[None]
    wkv_s = np.concatenate([r["o_wkv_s"].reshape(16, 8, 64, 64) for r in rs])[None]
    ffn_p = np.stack([r["o_ffn_p"].transpose(2, 1, 0).reshape(2, 5632) for r in rs])[None]
    ffn_s = np.concatenate([r["o_ffn_s"].transpose(3, 2, 1, 0).reshape(16, 2, 5632) for r in rs])[None]
    mk = np.stack([r["o_mk"].reshape(256, 4, 256) for r in rs])[None]
    mv = np.stack([r["o_mv"].reshape(256, 4, 256) for r in rs])[None]
    outs = (y_p, y_s, conv_p, conv_s, ssm_p, ssm_s, shift_p, shift_s, wkv_p, wkv_s, ffn_p, ffn_s, mk, mv)
    return tuple(np.ascontiguousarray(o, dtype=f32) for o in outs)
```
